# Optimizing a Trainium2 kernel written in Bass

```python
import math
import jax, jax.numpy as jnp
from jax import lax
import numpy as np

D_MODEL = 1024
BATCH = 4
SEQ = 8192
DEPTH = 2

MLA_HEADS = 8
MLA_NOPE = 64
MLA_ROPE = 32
MLA_QK = MLA_NOPE + MLA_ROPE
MLA_V = 64
MLA_Q_RANK = 384
MLA_KV_RANK = 256
ROPE_BASE = 10000.0
Q_BLOCK = 128

GLA_HEADS = 4
GLA_DK = 64
GLA_DV = 128
GLA_GATE_RANK = 16
GLA_TAU = 16.0
GLA_CHUNK = 64

S5_CH = 512
S5_GROUP = 16
S5_GROUPS = S5_CH // S5_GROUP
S5_STATE = 64
S5_DT_MIN = 0.001
S5_DT_MAX = 0.1

N_BRANCH = 3
BRANCH_W = 512
D_FF = 4 * D_MODEL
EPS = 1e-6

IN_SIZES = (MLA_Q_RANK, MLA_KV_RANK, MLA_ROPE,
            GLA_HEADS * GLA_DK, GLA_HEADS * GLA_DK, GLA_HEADS * GLA_DV, GLA_GATE_RANK, GLA_HEADS * GLA_DV,
            S5_CH,
            N_BRANCH * D_MODEL)
D_IN = sum(IN_SIZES)

kernel_name = 'hybrid_mla_gla_s5_gated_block'


def rms_norm(x, g):
    xf = x.astype(jnp.float32)
    y = xf * lax.rsqrt(jnp.mean(xf * xf, axis=-1, keepdims=True) + EPS)
    return (y * g.astype(jnp.float32)).astype(x.dtype)


def rope(x, cos, sin):
    x1, x2 = jnp.split(x, 2, axis=-1)
    return jnp.concatenate([x1 * cos - x2 * sin, x1 * sin + x2 * cos], axis=-1)


def split_in(z):
    idx = []
    acc = 0
    for s in IN_SIZES[:-1]:
        acc += s
        idx.append(acc)
    return jnp.split(z, idx, axis=-1)


def mla_mixer(h_cq, h_ckv, k_pe, q_norm_g, w_uq, kv_norm_g, w_ukv, q_head_g, k_head_g, cos, sin):
    B, L, _ = h_cq.shape
    dt = h_cq.dtype
    q = (rms_norm(h_cq, q_norm_g) @ w_uq).reshape(B, L, MLA_HEADS, MLA_QK)
    kv = (rms_norm(h_ckv, kv_norm_g) @ w_ukv).reshape(B, L, MLA_HEADS, MLA_NOPE + MLA_V)
    k_nope, v = kv[..., :MLA_NOPE], kv[..., MLA_NOPE:]
    k = jnp.concatenate([k_nope, jnp.broadcast_to(k_pe[:, :, None, :], (B, L, MLA_HEADS, MLA_ROPE))], axis=-1)
    q = rms_norm(q, q_head_g)
    k = rms_norm(k, k_head_g)
    c4, s4 = cos[:, None, :].astype(dt), sin[:, None, :].astype(dt)
    q = jnp.concatenate([q[..., :MLA_NOPE], rope(q[..., MLA_NOPE:], c4, s4)], axis=-1)
    k = jnp.concatenate([k[..., :MLA_NOPE], rope(k[..., MLA_NOPE:], c4, s4)], axis=-1)
    q = q.transpose(0, 2, 1, 3)
    k = k.transpose(0, 2, 1, 3)
    v = v.transpose(0, 2, 1, 3)
    n_blk = L // Q_BLOCK
    qb = q.reshape(B, MLA_HEADS, n_blk, Q_BLOCK, MLA_QK).transpose(2, 0, 1, 3, 4)
    key_pos = jnp.arange(L)
    scale = MLA_QK ** -0.5

    def attend(args):
        qi, blk = args
        s = jnp.einsum('bhqd,bhkd->bhqk', qi, k).astype(jnp.float32) * scale
        q_pos = blk * Q_BLOCK + jnp.arange(Q_BLOCK)
        mask = key_pos[None, :] <= q_pos[:, None]
        s = jnp.where(mask, s, jnp.finfo(jnp.float32).min)
        p = jax.nn.softmax(s, axis=-1)
        return jnp.einsum('bhqk,bhkd->bhqd', p.astype(v.dtype), v)

    o = lax.map(attend, (qb, jnp.arange(n_blk)))
    return o.transpose(1, 0, 3, 2, 4).reshape(B, L, MLA_HEADS * MLA_V)


def gla_mixer(q, k, v, g_lr, r, w_gate, b_gate, out_g):
    B, L, _ = q.shape
    dt = q.dtype
    n_ch = L // GLA_CHUNK
    f32 = jnp.float32

    def chunks(t, d):
        return t.astype(f32).reshape(B, n_ch, GLA_CHUNK, GLA_HEADS, d).transpose(0, 3, 1, 2, 4)

    log_a = jax.nn.log_sigmoid(g_lr.astype(f32) @ w_gate.astype(f32) + b_gate.astype(f32)) / GLA_TAU
    qc = chunks(q, GLA_DK) * (GLA_DK ** -0.5)
    kc = chunks(k, GLA_DK)
    vc = chunks(v, GLA_DV)
    bc = jnp.cumsum(chunks(log_a, GLA_DK), axis=3)
    b_last = bc[..., -1:, :]
    q_t = qc * jnp.exp(bc)
    k_t = kc * jnp.exp(-bc)
    k_end = kc * jnp.exp(b_last - bc)
    causal = jnp.tril(jnp.ones((GLA_CHUNK, GLA_CHUNK), dtype=bool))
    a_intra = jnp.where(causal, jnp.einsum('bhncd,bhnsd->bhncs', q_t, k_t), 0.0)
    o_intra = jnp.einsum('bhncs,bhnse->bhnce', a_intra, vc)
    d_state = jnp.einsum('bhncd,bhnce->bhnde', k_end, vc)
    decay = jnp.exp(b_last[..., 0, :])

    def step(S, inp):
        dS_n, dec_n = inp
        return dec_n[..., None] * S + dS_n, S

    S0 = jnp.zeros((B, GLA_HEADS, GLA_DK, GLA_DV), f32)
    _, S_prev = lax.scan(step, S0, (jnp.moveaxis(d_state, 2, 0), jnp.moveaxis(decay, 2, 0)))
    S_prev = jnp.moveaxis(S_prev, 0, 2)
    o = o_intra + jnp.einsum('bhncd,bhnde->bhnce', q_t, S_prev)
    o = o.transpose(0, 2, 3, 1, 4).reshape(B, L, GLA_HEADS, GLA_DV)
    o = rms_norm(o, out_g) * jax.nn.silu(r.astype(f32)).reshape(B, L, GLA_HEADS, GLA_DV)
    return o.reshape(B, L, GLA_HEADS * GLA_DV).astype(dt)


def diag_combine(e1, e2):
    a1, b1 = e1
    a2, b2 = e2
    return a1 * a2, a2 * b1 + b2


def s5_mixer(u, lam_re, lam_im, b_re, b_im, c_re, c_im, d, log_dt, w_glu, b_glu):
    B, L, _ = u.shape
    f32 = jnp.float32
    uf = u.astype(f32).reshape(B, L, S5_GROUPS, S5_GROUP)
    lam = lax.complex(jnp.minimum(lam_re.astype(f32), -1e-4), lam_im.astype(f32))
    step = jnp.exp(log_dt.astype(f32))[:, None]
    lam_bar = jnp.exp(lam * step)
    b_bar = ((lam_bar - 1.0) / lam)[..., None] * lax.complex(b_re.astype(f32), b_im.astype(f32))
    bu = lax.complex(jnp.einsum('blgi,gpi->blgp', uf, jnp.real(b_bar)),
                     jnp.einsum('blgi,gpi->blgp', uf, jnp.imag(b_bar)))
    a = jnp.broadcast_to(lam_bar, bu.shape)
    _, states = lax.associative_scan(diag_combine, (a, bu), axis=1)
    y = (jnp.einsum('blgp,gip->blgi', jnp.real(states), c_re.astype(f32))
         - jnp.einsum('blgp,gip->blgi', jnp.imag(states), c_im.astype(f32))
         + d.astype(f32) * uf)
    y = jax.nn.gelu(y.reshape(B, L, S5_CH))
    y = y * jax.nn.sigmoid(y @ w_glu.astype(f32) + b_glu.astype(f32))
    return y.astype(u.dtype)


def setup_inputs(seed: int = 0) -> dict:
    key = jax.random.key(seed)
    ks = iter(jax.random.split(key, 40))
    f32 = jnp.float32

    def nrm(shape, scale):
        return jax.random.normal(next(ks), shape, f32) * scale

    def gain(shape):
        return 1.0 + 0.02 * jax.random.normal(next(ks), shape, f32)

    n_idx = jnp.arange(S5_STATE, dtype=f32)
    return {
        'x': nrm((BATCH, SEQ, D_MODEL), 1.0),
        'norm1_g': gain((DEPTH, D_MODEL)),
        'w_in': nrm((DEPTH, D_MODEL, D_IN), D_MODEL ** -0.5),
        'mla_q_norm_g': gain((DEPTH, MLA_Q_RANK)),
        'mla_w_uq': nrm((DEPTH, MLA_Q_RANK, MLA_HEADS * MLA_QK), MLA_Q_RANK ** -0.5),
        'mla_kv_norm_g': gain((DEPTH, MLA_KV_RANK)),
        'mla_w_ukv': nrm((DEPTH, MLA_KV_RANK, MLA_HEADS * (MLA_NOPE + MLA_V)), MLA_KV_RANK ** -0.5),
        'mla_q_head_g': gain((DEPTH, MLA_QK)),
        'mla_k_head_g': gain((DEPTH, MLA_QK)),
        'gla_w_gate': nrm((DEPTH, GLA_GATE_RANK, GLA_HEADS * GLA_DK), GLA_GATE_RANK ** -0.5),
        'gla_b_gate': nrm((DEPTH, GLA_HEADS * GLA_DK), 0.1),
        'gla_out_g': gain((DEPTH, GLA_DV)),
        's5_lam_re': -0.5 + nrm((DEPTH, S5_GROUPS, S5_STATE), 0.01),
        's5_lam_im': jnp.pi * n_idx + nrm((DEPTH, S5_GROUPS, S5_STATE), 0.01),
        's5_b_re': nrm((DEPTH, S5_GROUPS, S5_STATE, S5_GROUP), (2 * S5_GROUP) ** -0.5),
        's5_b_im': nrm((DEPTH, S5_GROUPS, S5_STATE, S5_GROUP), (2 * S5_GROUP) ** -0.5),
        's5_c_re': nrm((DEPTH, S5_GROUPS, S5_GROUP, S5_STATE), S5_STATE ** -0.5),
        's5_c_im': nrm((DEPTH, S5_GROUPS, S5_GROUP, S5_STATE), S5_STATE ** -0.5),
        's5_d': nrm((DEPTH, S5_GROUPS, S5_GROUP), 1.0),
        's5_log_dt': jax.random.uniform(next(ks), (DEPTH, S5_GROUPS), f32,
                                        minval=math.log(S5_DT_MIN), maxval=math.log(S5_DT_MAX)),
        's5_w_glu': nrm((DEPTH, S5_CH, S5_CH), S5_CH ** -0.5),
        's5_b_glu': nrm((DEPTH, S5_CH), 0.01),
        'w_br_mla': nrm((DEPTH, BRANCH_W, D_MODEL), BRANCH_W ** -0.5),
        'w_br_gla': nrm((DEPTH, BRANCH_W, D_MODEL), BRANCH_W ** -0.5),
        'w_br_s5': nrm((DEPTH, BRANCH_W, D_MODEL), BRANCH_W ** -0.5),
        'gate_b': nrm((DEPTH, N_BRANCH * D_MODEL), 0.01),
        'w_out': nrm((DEPTH, D_MODEL, D_MODEL), D_MODEL ** -0.5),
        'norm2_g': gain((DEPTH, D_MODEL)),
        'w_ff1': nrm((DEPTH, D_MODEL, D_FF), D_MODEL ** -0.5),
        'w_ff2': nrm((DEPTH, D_FF, D_MODEL), D_FF ** -0.5),
    }


def reference(x, norm1_g, w_in, mla_q_norm_g, mla_w_uq, mla_kv_norm_g, mla_w_ukv, mla_q_head_g,
              mla_k_head_g, gla_w_gate, gla_b_gate, gla_out_g, s5_lam_re, s5_lam_im, s5_b_re, s5_b_im,
              s5_c_re, s5_c_im, s5_d, s5_log_dt, s5_w_glu, s5_b_glu, w_br_mla, w_br_gla, w_br_s5, gate_b,
              w_out, norm2_g, w_ff1, w_ff2):
    B, L, D = x.shape
    pos = jnp.arange(L, dtype=jnp.float32)
    inv_freq = ROPE_BASE ** (-jnp.arange(0, MLA_ROPE, 2, dtype=jnp.float32) / MLA_ROPE)
    ang = pos[:, None] * inv_freq[None, :]
    cos, sin = jnp.cos(ang), jnp.sin(ang)
    for l in range(DEPTH):
        h = rms_norm(x, norm1_g[l])
        z = h @ w_in[l]
        (cq, ckv, kpe, gq, gk, gv, glr, gr, su, gates) = split_in(z)
        kpe = rope(kpe, cos.astype(kpe.dtype), sin.astype(kpe.dtype)) * 1.0 if False else kpe
        o_a = mla_mixer(cq, ckv, kpe, mla_q_norm_g[l], mla_w_uq[l], mla_kv_norm_g[l], mla_w_ukv[l],
                        mla_q_head_g[l], mla_k_head_g[l], cos, sin)
        o_b = gla_mixer(gq, gk, gv, glr, gr, gla_w_gate[l], gla_b_gate[l], gla_out_g[l])
        o_c = s5_mixer(su, s5_lam_re[l], s5_lam_im[l], s5_b_re[l], s5_b_im[l], s5_c_re[l], s5_c_im[l],
                       s5_d[l], s5_log_dt[l], s5_w_glu[l], s5_b_glu[l])
        g = jax.nn.sigmoid(gates + gate_b[l]).reshape(B, L, N_BRANCH, D)
        merged = (g[:, :, 0] * (o_a @ w_br_mla[l])
                  + g[:, :, 1] * (o_b @ w_br_gla[l])
                  + g[:, :, 2] * (o_c @ w_br_s5[l]))
        x = x + merged @ w_out[l]
        h2 = rms_norm(x, norm2_g[l])
        x = x + jnp.square(jax.nn.relu(h2 @ w_ff1[l])) @ w_ff2[l]
    return x
```

```python
import math
import os
from contextlib import ExitStack
import numpy as np
import concourse.bass as bass
import concourse.mybir as mybir
from concourse.bass_utils import run_bass_kernel_spmd

F32 = mybir.dt.float32
BF16 = mybir.dt.bfloat16
AF = mybir.ActivationFunctionType
ALU = mybir.AluOpType
AX = mybir.AxisListType

D = 1024
DEPTH = 2
SEQ = 8192
BATCH = 4
EPS = 1e-6
MAGIC = 12582912.0
TWO_PI = 2.0 * math.pi

PK_G1, PK_G2, PK_QNG, PK_KVNG, PK_LRE, PK_LIM, PK_LDT, PK_D5, PK_BGLU, PK_GB = 0, 8, 16, 19, 21, 37, 53, 69, 73, 77
PKW = 101
C_ID, C_TRIU, C_TRIL, C_TAU, C_INVF, C_POS = 0, 128, 256, 384, 512, 528


class Res:
    __slots__ = ("w", "r")

    def __init__(self):
        self.w = None
        self.r = {}


class TV:
    def __init__(self, ap, res=None):
        self.ap = ap
        self.res = res if res is not None else Res()

    def __getitem__(self, k):
        return TV(self.ap[k], self.res)

    def v(self, f):
        return TV(f(self.ap), self.res)

    def sub(self, k):
        return TV(self.ap[k], Res())


class KB:
    def __init__(self, nc):
        self.nc = nc
        self.eng = {"pe": nc.tensor, "act": nc.scalar, "dve": nc.vector, "pool": nc.gpsimd, "sp": nc.sync}
        self.sems = {}
        self.cnt = {}
        for e in ("pe", "act", "dve", "pool"):
            self.sems[e] = nc.alloc_semaphore("c_" + e)
            self.cnt[e] = 0
        self.dq = {}
        for q, n in (("sp", 16), ("act", 4), ("pool", 4)):
            keys = []
            for i in range(n):
                k = "d_%s%d" % (q, i)
                self.sems[k] = nc.alloc_semaphore(k)
                self.cnt[k] = 0
                keys.append(k)
            self.dq[q] = [keys, 0]
        self.seen = {e: {} for e in self.eng}
        self.rr = 0

    def _deps(self, reads, writes):
        need = {}

        def add(k, v):
            if need.get(k, 0) < v:
                need[k] = v

        for R in reads:
            if R.w is not None:
                add(*R.w)
        for R in writes:
            if R.w is not None:
                add(*R.w)
            for k, v in R.r.items():
                add(k, v)
        return need

    def _wait(self, e, need):
        for k, v in need.items():
            if e == "pe" and k == "pe":
                continue
            if self.seen[e].get(k, 0) >= v:
                continue
            self.eng[e].wait_ge(self.sems[k], v)
            self.seen[e][k] = v

    def op(self, e, fn, reads, writes):
        reads = [t.res for t in reads]
        writes = [t.res for t in writes]
        self._wait(e, self._deps(reads, writes))
        ins = fn()
        self.cnt[e] += 1
        ins.then_inc(self.sems[e], 1)
        c = self.cnt[e]
        for R in writes:
            R.w = (e, c)
            R.r = {}
        for R in reads:
            R.r[e] = c
        return ins

    def dma(self, out, in_, q="sp"):
        keys, idx = self.dq[q]
        k = keys[idx]
        self.dq[q][1] = (idx + 1) % len(keys)
        need = self._deps([in_.res], [out.res])
        if self.cnt[k] > 0:
            need[k] = max(need.get(k, 0), self.cnt[k])
        self._wait(q, need)
        ins = self.eng[q].dma_start(out=out.ap, in_=in_.ap)
        self.cnt[k] += 16
        ins.then_inc(self.sems[k], 16)
        c = self.cnt[k]
        out.res.w = (k, c)
        out.res.r = {}
        in_.res.r[k] = c

    def barrier(self):
        for e in self.eng:
            for k, v in self.cnt.items():
                if v > 0 and self.seen[e].get(k, 0) < v:
                    self.eng[e].wait_ge(self.sems[k], v)
                    self.seen[e][k] = v

    def _ve(self, e):
        return self.eng[e]

    def tt(self, e, out, a, b, op):
        return self.op(e, lambda: self._ve(e).tensor_tensor(out=out.ap, in0=a.ap, in1=b.ap, op=op), [a, b], [out])

    def ts(self, e, out, a, s1, op0, s2=None, op1=None):
        rd = [a]
        s1a, s2a = s1, s2
        if isinstance(s1, TV):
            rd.append(s1)
            s1a = s1.ap
        if isinstance(s2, TV):
            rd.append(s2)
            s2a = s2.ap
        if op1 is None:
            return self.op(e, lambda: self._ve(e).tensor_scalar(out=out.ap, in0=a.ap, scalar1=s1a, scalar2=None, op0=op0), rd, [out])
        return self.op(e, lambda: self._ve(e).tensor_scalar(out=out.ap, in0=a.ap, scalar1=s1a, scalar2=s2a, op0=op0, op1=op1), rd, [out])

    def stt(self, out, a, sc, b, op0, op1):
        rd = [a, b]
        sca = sc
        if isinstance(sc, TV):
            rd.append(sc)
            sca = sc.ap
        return self.op("dve", lambda: self.nc.vector.scalar_tensor_tensor(out=out.ap, in0=a.ap, scalar=sca, in1=b.ap, op0=op0, op1=op1), rd, [out])

    def copy(self, e, out, a):
        if e == "act":
            return self.op(e, lambda: self.nc.scalar.copy(out=out.ap, in_=a.ap), [a], [out])
        return self.op(e, lambda: self._ve(e).tensor_copy(out=out.ap, in_=a.ap), [a], [out])

    def memset(self, e, out, val):
        return self.op(e, lambda: self._ve(e).memset(out.ap, val), [], [out])

    def act(self, out, a, func, bias=None, scale=None, accum=None):
        rd = [a]
        wr = [out]
        kw = {}
        if bias is not None:
            if isinstance(bias, TV):
                rd.append(bias)
                kw["bias"] = bias.ap
            else:
                kw["bias"] = bias
        if scale is not None:
            if isinstance(scale, TV):
                rd.append(scale)
                kw["scale"] = scale.ap
            else:
                kw["scale"] = scale
        if accum is not None:
            wr.append(accum)
            kw["accum_out"] = accum.ap
        return self.op("act", lambda: self.nc.scalar.activation(out=out.ap, in_=a.ap, func=func, **kw), rd, wr)

    def mm(self, out, lhsT, rhs, start, stop):
        return self.op("pe", lambda: self.nc.tensor.matmul(out.ap, lhsT=lhsT.ap, rhs=rhs.ap, start=start, stop=stop), [lhsT, rhs], [out])

    def tr(self, out, a, ident):
        return self.op("pe", lambda: self.nc.tensor.transpose(out.ap, a.ap, ident.ap), [a, ident], [out])

    def red(self, out, a, op=ALU.add):
        return self.op("dve", lambda: self.nc.vector.tensor_reduce(out=out.ap, in_=a.ap, axis=AX.X, op=op), [a], [out])

    def scan(self, out, d0, d1, init, op0=ALU.mult, op1=ALU.add):
        rd = [d0, d1]
        ia = init
        if isinstance(init, TV):
            rd.append(init)
            ia = init.ap
        return self.op("dve", lambda: self.nc.vector.tensor_tensor_scan(out=out.ap, data0=d0.ap, data1=d1.ap, initial=ia, op0=op0, op1=op1), rd, [out])

    def recip(self, out, a):
        return self.op("dve", lambda: self.nc.vector.reciprocal(out=out.ap, in_=a.ap), [a], [out])


class Rot:
    def __init__(self, items):
        self.items = items
        self.i = 0

    def next(self):
        t = self.items[self.i]
        self.i = (self.i + 1) % len(self.items)
        return t


def build(L, dbg=False, nlayers=DEPTH, stop=None):
    nc = bass.Bass("TRN2", target_bir_lowering=False)
    kb = KB(nc)
    NT = L // 128
    NB = L // 512

    def din(name, shape, dt=F32):
        return nc.dram_tensor(name, list(shape), dt, kind="ExternalInput").ap()

    def dscr(name, shape, dt):
        if dbg:
            return nc.dram_tensor(name, list(shape), dt, kind="ExternalOutput").ap()
        return nc.dram_tensor(name, list(shape), dt).ap()

    x_in = din("x", [L, D])
    w_in = din("w_in", [DEPTH, 1024, 5808])
    w_uq = din("w_uq", [DEPTH, 384, 768])
    w_ukv = din("w_ukv", [DEPTH, 256, 1024])
    w_glu = din("w_glu", [DEPTH, 512, 512])
    w_br = [din("w_br%d" % i, [DEPTH, 512, 1024]) for i in range(3)]
    w_out = din("w_out", [DEPTH, 1024, 1024])
    w_ff1 = din("w_ff1", [DEPTH, 1024, 4096])
    w_ff2 = din("w_ff2", [DEPTH, 4096, 1024])
    pk = din("pk", [DEPTH, 128, PKW])
    rowp = din("rowp", [DEPTH, 320])
    wg = din("wg", [DEPTH, 17, 256])
    btd = din("bt", [DEPTH, 2, 128, 16, 128])
    ctd = din("ct", [DEPTH, 2, 128, 16, 128])
    CW = C_POS + NT
    consts = din("consts", [128, CW])

    QT = dscr("QT", [8, 96, L], BF16)
    KT = dscr("KT", [8, 96, L], BF16)
    Vd = dscr("Vd", [8, 128, NT, 64], BF16)
    oaT = dscr("oaT", [512, L], BF16)
    obT = dscr("obT", [512, L], BF16)
    ocT = dscr("ocT", [512, L], BF16)
    x1d = dscr("x1", [L, D], F32)
    xmd = dscr("xm", [L, D], F32)
    outd = nc.dram_tensor("out", [L, D], F32, kind="ExternalOutput").ap()

    def DR(ap):
        return TV(ap)

    def sb(name, shape, dt=F32):
        return TV(nc.alloc_sbuf_tensor(name, list(shape), dt).ap())

    PS = [TV(nc.alloc_psum_tensor("ps%d" % i, [128, 512], F32).ap()) for i in range(8)]
    psrot = Rot(PS)

    def psb(p):
        return p.v(lambda a: a.bitcast(BF16))

    cst = sb("cst", [128, CW])
    kb.dma(cst, DR(consts))
    identf = cst[:, C_ID:C_ID + 128]
    identb = sb("identb", [128, 128], BF16)
    kb.copy("dve", identb, identf)
    trimb = sb("trimb", [128, 128], BF16)
    kb.copy("dve", trimb, cst[:, C_TRIU:C_TRIU + 128])
    triU = cst[:, C_TRIU:C_TRIU + 128]
    triUs = sb("triUs", [128, 128])
    triLs = sb("triLs", [128, 128])
    kb.ts("dve", triUs, cst[:, C_TRIU:C_TRIU + 128], -1.0 / 16.0, ALU.mult)
    kb.ts("dve", triLs, cst[:, C_TRIL:C_TRIL + 128], -1.0 / 16.0, ALU.mult)
    onesb = sb("onesb", [128, 128], BF16)
    kb.memset("dve", onesb, 1.0)
    mhalf = sb("mhalf", [128, 16])
    kb.memset("dve", mhalf, -0.5)

    def sincos(alloc_fn, ang, n, sin_out, cos_out, tag):
        t0 = alloc_fn("sc0" + tag, [128, n], F32)
        t1 = alloc_fn("sc1" + tag, [128, n], F32)
        for off, dst in ((0.0, sin_out), (0.25, cos_out)):
            kb.ts("dve", t0, ang, 1.0 / TWO_PI, ALU.mult, off, ALU.add)
            kb.ts("dve", t1, t0, MAGIC, ALU.add)
            kb.ts("dve", t1, t1, MAGIC, ALU.subtract)
            kb.tt("dve", t0, t0, t1, ALU.subtract)
            kb.act(dst, t0, AF.Sin, scale=TWO_PI * (1.0 - 1e-6))

    ropec = sb("ropec", [128, NT * 16])
    ropes = sb("ropes", [128, NT * 16])
    es0 = ExitStack()

    def sb0(name, shape, dt=F32):
        return TV(es0.enter_context(nc.sbuf_tensor(name, list(shape), dt)).ap())

    rang = sb0("rang", [128, NT * 16])
    kb.tt("dve", rang.v(lambda a: a.rearrange("p (t i) -> p t i", i=16)),
          cst[:, C_POS:C_POS + NT].v(lambda a: a.unsqueeze(2).to_broadcast([128, NT, 16])),
          cst[:, C_INVF:C_INVF + 16].v(lambda a: a.unsqueeze(1).to_broadcast([128, NT, 16])), ALU.mult)
    sincos(sb0, rang, NT * 16, ropes, ropec, "r")
    kb.barrier()
    es0.close()
    ropec3 = ropec.v(lambda a: a.rearrange("p (t i) -> p t i", i=16))
    ropes3 = ropes.v(lambda a: a.rearrange("p (t i) -> p t i", i=16))

    stage = Rot([sb("stg%d" % i, [128, 8, 256]) for i in range(2)])
    engcyc = Rot(["dve", "pool", "act"])

    def load_w(dst, dcol0, src2d, c0, c1, KC, scale=None):
        for cc in range(c0, c1, 256):
            ce = min(cc + 256, c1)
            n = ce - cc
            st = stage.next()
            kb.dma(st[:, 0:KC, 0:n], DR(src2d[:, cc:ce].rearrange("(kc p) n -> p kc n", p=128)))
            d0 = dcol0 + (cc - c0)
            if scale is None:
                kb.copy(engcyc.next(), dst[:, :, d0:d0 + n], st[:, 0:KC, 0:n])
            else:
                for kc in range(KC):
                    e = engcyc.next()
                    if e == "act":
                        kb.act(dst[:, kc, d0:d0 + n], st[:, kc, 0:n], AF.Copy, scale=scale[:, kc:kc + 1])
                    else:
                        kb.ts(e, dst[:, kc, d0:d0 + n], st[:, kc, 0:n], scale[:, kc:kc + 1], ALU.mult)

    def rstd_of(out, ss, n, cols):
        kb.ts("pool", out, ss, 1.0 / n, ALU.mult, EPS, ALU.add)
        kb.tt("pool", out, out, mhalf[:, 0:cols], ALU.pow)

    def norm_transpose(x_t, hT_dst, tcols, pool_tiles, rot=None):
        junk, ssr, hb = pool_tiles
        kb.act(junk, x_t, AF.Square, accum=ssr[:, 0:1])
        rstd_of(ssr[:, 1:2], ssr[:, 0:1], 1024.0, 1)
        kb.ts("dve", hb, x_t, ssr[:, 1:2], ALU.mult)
        p = (rot or psrot).next()
        pb = psb(p)
        for kc in range(8):
            kb.tr(pb[:, kc * 128:(kc + 1) * 128], hb[:, kc * 128:(kc + 1) * 128], identb)
        kb.copy("act", hT_dst[:, :, tcols], pb.v(lambda a: a.rearrange("p (k t) -> p k t", k=8)))

    if stop == '0':
        kb.barrier()
        return nc
    for l in range(nlayers):
        xsrc = x_in if l == 0 else xmd
        xdst = xmd if l == 0 else outd

        with ExitStack() as es:
            def A(name, shape, dt=F32, es=es):
                return TV(es.enter_context(nc.sbuf_tensor("A%d_%s" % (l, name), list(shape), dt)).ap())

            pkt = A("pk", [128, PKW])
            kb.dma(pkt, DR(pk[l]))
            rows = A("rows", [128, 320])
            kb.dma(rows, DR(rowp[l].partition_broadcast(128)))
            gq_s = A("gqs", [128, 96])
            kb.ts("dve", gq_s, rows[:, 0:96], 96.0 ** -0.5, ALU.mult)
            gk_r = rows[:, 96:192]
            og_h = A("ogh", [128, 128])
            kb.ts("dve", og_h, rows[:, 192:320], 0.5, ALU.mult)
            wga = A("wga", [32, 256])
            kb.dma(wga[0:17, :], DR(wg[l]))
            WAf = A("WAf", [128, 8, 1168], BF16)
            WAt = A("WAt", [128, 8, 1312], BF16)
            g1p = pkt[:, PK_G1:PK_G1 + 8]
            wl = w_in[l]
            for (c0, c1, d0) in ((0, 384, 0), (384, 640, 384), (672, 928, 640), (928, 1184, 896), (1696, 1712, 1152)):
                load_w(WAf, d0, wl, c0, c1, 8, g1p)
            for (c0, c1, d0) in ((640, 672, 0), (928, 1184, 32), (1184, 1696, 288), (1712, 2224, 800)):
                load_w(WAt, d0, wl, c0, c1, 8, g1p)
            if stop == 'A1':
                kb.barrier()
                return nc
            Wuq = A("Wuq", [128, 3, 768], BF16)
            load_w(Wuq, 0, w_uq[l], 0, 768, 3, pkt[:, PK_QNG:PK_QNG + 3])
            Wukv = A("Wukv", [128, 2, 1024], BF16)
            load_w(Wukv, 0, w_ukv[l], 0, 1024, 2, pkt[:, PK_KVNG:PK_KVNG + 2])
            Sst = [A("S%d" % i, [128, 128]) for i in range(2)]
            for s_ in Sst:
                kb.memset("dve", s_, 0.0)

            if stop == 'A2':
                kb.barrier()
                return nc
            xpool = Rot([A("x%d" % i, [128, 1024]) for i in range(2)])
            junk = A("junk", [128, 1024], BF16)
            ssr = Rot([A("ssr%d" % i, [128, 2]) for i in range(2)])
            hb = A("hb", [128, 1024], BF16)
            hT = A("hT", [128, 8, 512], BF16)
            cqT = A("cqT", [128, 3, 512], BF16)
            ckvT = A("ckvT", [128, 2, 512], BF16)
            sqT = A("sqT", [128, 5, 512], BF16)
            gqT = A("gqT", [128, 2, 512])
            gkT = A("gkT", [128, 2, 512])
            glrT = A("glrT", [32, 512])
            kb.memset("dve", glrT, 1.0)
            kk = A("kk", [128, 288])
            gv = A("gv", [128, 512])
            sr = A("sr", [128, 512])
            st2 = A("st2", [128, 4])
            q32 = A("q32", [128, 768])
            k96 = A("k96", [128, 768])
            kv32 = A("kv32", [128, 1024])
            sq768 = A("sq768", [128, 768])
            ssh = A("ssh", [128, 16])
            nrm = A("nrm", [128, 768])
            rtmp = A("rtmp", [128, 8, 16])
            rtmp2 = A("rtmp2", [128, 8, 16])
            qb = A("qb", [128, 768], BF16)
            qTb = A("qTb", [96, 8, 512], BF16)
            kTb = A("kTb", [96, 8, 512], BF16)
            vb = A("vb", [128, 8, 4, 64], BF16)
            lsp = A("lsp", [128, 256])
            eq = A("eq", [128, 256])
            ek = A("ek", [128, 256])
            eend = A("eend", [128, 256])
            qtT = A("qtT", [128, 2, 128])
            ktT = A("ktT", [128, 2, 128])
            kend = A("kend", [128, 256])
            Am = A("Am", [128, 128])
            o32 = A("o32", [128, 512])
            osq = A("osq", [128, 512])
            ob = A("ob", [128, 512], BF16)
            obTb = A("obTb", [128, 4, 512], BF16)
            def headnorm_rope(src, g_rep, tix, dstT, tcols):
                s3 = src.v(lambda a: a.rearrange("p (h d) -> p h d", h=8))
                kb.tt("pool", sq768, src, src, ALU.mult)
                kb.red(ssh[:, 0:8], sq768.v(lambda a: a.rearrange("p (h d) -> p h d", h=8)))
                rstd_of(ssh[:, 8:16], ssh[:, 0:8], 96.0, 8)
                n3 = nrm.v(lambda a: a.rearrange("p (h d) -> p h d", h=8))
                kb.tt("dve", n3, s3, ssh[:, 8:16].v(lambda a: a.unsqueeze(2).to_broadcast([128, 8, 96])), ALU.mult)
                kb.tt("dve", n3, n3, g_rep.v(lambda a: a.unsqueeze(1).to_broadcast([128, 8, 96])), ALU.mult)
                b3 = qb.v(lambda a: a.rearrange("p (h d) -> p h d", h=8))
                kb.copy("pool", b3[:, :, 0:64], n3[:, :, 0:64])
                cs = ropec3[:, tix, :].v(lambda a: a.unsqueeze(1).to_broadcast([128, 8, 16]))
                sn = ropes3[:, tix, :].v(lambda a: a.unsqueeze(1).to_broadcast([128, 8, 16]))
                x1_, x2_ = n3[:, :, 64:80], n3[:, :, 80:96]
                kb.tt("dve", rtmp, x1_, cs, ALU.mult)
                kb.tt("dve", rtmp2, x2_, sn, ALU.mult)
                kb.tt("dve", b3[:, :, 64:80], rtmp, rtmp2, ALU.subtract)
                kb.tt("dve", rtmp, x1_, sn, ALU.mult)
                kb.tt("dve", rtmp2, x2_, cs, ALU.mult)
                kb.tt("dve", b3[:, :, 80:96], rtmp, rtmp2, ALU.add)
                p = psrot.next()
                pb = psb(p)
                for h in range(8):
                    kb.tr(pb[0:96, h * 128:(h + 1) * 128], qb[:, h * 96:(h + 1) * 96], identb)
                kb.copy("act", dstT[:, :, tcols], pb[0:96, :].v(lambda a: a.rearrange("p (h t) -> p h t", h=8)))

            for blk in range(NB):
                bsl = slice(blk * 512, (blk + 1) * 512)
                for tt_ in range(4):
                    tix = blk * 4 + tt_
                    xt = xpool.next()
                    kb.dma(xt, DR(xsrc[tix * 128:(tix + 1) * 128, :]))
                    norm_transpose(xt, hT, slice(tt_ * 128, (tt_ + 1) * 128), (junk, ssr.next(), hb))
                if stop == 'A3':
                    kb.barrier()
                    return nc
                fm = [(0, 128, ("cq", 0)), (128, 128, ("cq", 1)), (256, 128, ("cq", 2)), (384, 128, ("ckv", 0)), (512, 128, ("ckv", 1)),
                      (640, 128, ("gq", 0)), (768, 128, ("gq", 1)), (896, 128, ("gk", 0)), (1024, 128, ("gk", 1)), (1152, 16, ("glr", 0))]
                import os
                for (c0, M, (kind, ci)) in fm[:int(os.environ.get('FMN', '99'))]:
                    p = psrot.next()
                    for kc in range(8):
                        kb.mm(p[0:M, :], WAf[:, kc, c0:c0 + M], hT[:, kc, :], kc == 0, kc == 7)
                    if kind == "cq":
                        kb.copy("dve", cqT[:, ci, :], p)
                        kb.tt("pool", sqT[:, ci, :], cqT[:, ci, :], cqT[:, ci, :], ALU.mult)
                    elif kind == "ckv":
                        kb.copy("dve", ckvT[:, ci, :], p)
                        kb.tt("pool", sqT[:, 3 + ci, :], ckvT[:, ci, :], ckvT[:, ci, :], ALU.mult)
                    elif kind == "gq":
                        kb.copy("act", gqT[:, ci, :], p)
                    elif kind == "gk":
                        kb.copy("dve", gkT[:, ci, :], p)
                    else:
                        kb.copy("act", glrT[0:16, :], p[0:16, :])

                if stop == 'A4':
                    kb.barrier()
                    return nc
                for tt_ in range(4):
                    tix = blk * 4 + tt_
                    tsl = slice(tt_ * 128, (tt_ + 1) * 128)
                    pkk, pgv, pgr = psrot.next(), psrot.next(), psrot.next()
                    for (c0, n, p) in ((0, 288, pkk), (288, 512, pgv), (800, 512, pgr)):
                        for kc in range(8):
                            kb.mm(p[:, 0:n], hT[:, kc, tsl], WAt[:, kc, c0:c0 + n], kc == 0, kc == 7)
                    kb.copy("act", kk, pkk[:, 0:288])
                    kb.copy("dve", gv, pgv)
                    kb.act(sr, pgr, AF.Tanh, scale=0.5)
                    kb.stt(sr, sr, 1.0, pgr, ALU.add, ALU.mult)
                    if stop == 'A5':
                        kb.barrier()
                        return nc
                    pst = psrot.next()
                    for c in range(3):
                        kb.mm(pst[:, 0:1], sqT[:, c, tsl], onesb[:, 0:1], c == 0, c == 2)
                    for c in range(2):
                        kb.mm(pst[:, 1:2], sqT[:, 3 + c, tsl], onesb[:, 0:1], c == 0, c == 1)
                    kb.copy("dve", st2[:, 0:2], pst[:, 0:2])
                    rstd_of(st2[:, 2:3], st2[:, 0:1], 384.0, 1)
                    rstd_of(st2[:, 3:4], st2[:, 1:2], 256.0, 1)
                    pq0, pq1 = psrot.next(), psrot.next()
                    for (p, n0, n1) in ((pq0, 0, 512), (pq1, 512, 768)):
                        for c in range(3):
                            kb.mm(p[:, 0:n1 - n0], cqT[:, c, tsl], Wuq[:, c, n0:n1], c == 0, c == 2)
                    kb.act(q32[:, 0:512], pq0, AF.Copy, scale=st2[:, 2:3])
                    kb.act(q32[:, 512:768], pq1[:, 0:256], AF.Copy, scale=st2[:, 2:3])
                    headnorm_rope(q32, gq_s, tix, qTb, tsl)
                    if stop == 'A6':
                        kb.barrier()
                        return nc
                    pk0, pk1 = psrot.next(), psrot.next()
                    for (p, n0) in ((pk0, 0), (pk1, 512)):
                        for c in range(2):
                            kb.mm(p, ckvT[:, c, tsl], Wukv[:, c, n0:n0 + 512], c == 0, c == 1)
                    kb.act(kv32[:, 0:512], pk0, AF.Copy, scale=st2[:, 3:4])
                    kb.act(kv32[:, 512:1024], pk1, AF.Copy, scale=st2[:, 3:4])
                    kv3 = kv32.v(lambda a: a.rearrange("p (h d) -> p h d", h=8))
                    k3 = k96.v(lambda a: a.rearrange("p (h d) -> p h d", h=8))
                    kb.copy("pool", k3[:, :, 0:64], kv3[:, :, 0:64])
                    kb.copy("pool", k3[:, :, 64:96], kk[:, 0:32].v(lambda a: a.unsqueeze(1).to_broadcast([128, 8, 32])))
                    kb.copy("pool", vb[:, :, tt_, :], kv3[:, :, 64:128])
                    headnorm_rope(k96, gk_r, tix, kTb, tsl)

                    if stop == 'A7':
                        kb.barrier()
                        return nc
                    pl = psrot.next()
                    kb.mm(pl[:, 0:256], glrT[0:17, tsl], wga[0:17, :], True, True)
                    kb.act(lsp, pl[:, 0:256], AF.Exp, scale=-1.0)
                    kb.act(lsp, lsp, AF.Ln, bias=1.0)
                    pbc = psrot.next()
                    for hc in range(2):
                        kb.mm(pbc[:, hc * 128:(hc + 1) * 128], lsp[:, hc * 128:(hc + 1) * 128], triUs, True, True)
                    kb.mm(pbc[:, 256:512], triLs, lsp, True, True)
                    kb.act(eq, pbc[:, 0:256], AF.Exp)
                    kb.act(ek, pbc[:, 0:256], AF.Exp, scale=-1.0)
                    kb.act(eend, pbc[:, 256:512], AF.Exp)
                    for hc in range(2):
                        kb.stt(qtT[:, hc, :], gqT[:, hc, tsl], 0.125, eq[:, hc * 128:(hc + 1) * 128], ALU.mult, ALU.mult)
                        kb.tt("pool", ktT[:, hc, :], gkT[:, hc, tsl], ek[:, hc * 128:(hc + 1) * 128], ALU.mult)
                    kb.tt("pool", kend, kk[:, 32:288], eend, ALU.mult)
                    if stop == 'A8':
                        kb.barrier()
                        return nc
                    po = psrot.next()
                    pds = psrot.next()
                    for h in range(4):
                        hc, hb_ = h // 2, (h % 2) * 64
                        pa = psrot.next()
                        kb.mm(pa[:, 0:128], ktT[hb_:hb_ + 64, hc, :], qtT[hb_:hb_ + 64, hc, :], True, True)
                        kb.tt("dve", Am, pa[:, 0:128], triU, ALU.mult)
                        kb.mm(po[:, h * 128:(h + 1) * 128], Am, gv[:, h * 128:(h + 1) * 128], True, False)
                        kb.mm(po[:, h * 128:(h + 1) * 128], qtT[hb_:hb_ + 64, hc, :], Sst[hc][hb_:hb_ + 64, :], False, True)
                        kb.mm(pds[hb_:hb_ + 64, hc * 128:(hc + 1) * 128], kend[:, h * 64:(h + 1) * 64], gv[:, h * 128:(h + 1) * 128], True, True)
                    for hc in range(2):
                        kb.stt(Sst[hc], Sst[hc], eq[:, hc * 128 + 127:hc * 128 + 128], pds[:, hc * 128:(hc + 1) * 128], ALU.mult, ALU.add)
                    if stop == 'A9':
                        kb.barrier()
                        return nc
                    kb.copy("act", o32, po)
                    kb.tt("pool", osq, o32, o32, ALU.mult)
                    kb.red(ssh[:, 0:4], osq.v(lambda a: a.rearrange("p (h e) -> p h e", h=4)))
                    rstd_of(ssh[:, 8:12], ssh[:, 0:4], 128.0, 4)
                    o3 = o32.v(lambda a: a.rearrange("p (h e) -> p h e", h=4))
                    kb.tt("dve", o3, o3, ssh[:, 8:12].v(lambda a: a.unsqueeze(2).to_broadcast([128, 4, 128])), ALU.mult)
                    kb.tt("dve", o3, o3, og_h.v(lambda a: a.unsqueeze(1).to_broadcast([128, 4, 128])), ALU.mult)
                    kb.tt("dve", ob, o32, sr, ALU.mult)
                    p = psrot.next()
                    pb = psb(p)
                    for c in range(4):
                        kb.tr(pb[:, c * 128:(c + 1) * 128], ob[:, c * 128:(c + 1) * 128], identb)
                    kb.copy("act", obTb[:, :, tsl], pb[:, 0:512].v(lambda a: a.rearrange("p (c t) -> p c t", c=4)))

                if stop == 'A10':
                    kb.barrier()
                    return nc
                kb.dma(DR(QT[:, :, bsl].rearrange("h d t -> d h t")), qTb)
                kb.dma(DR(KT[:, :, bsl].rearrange("h d t -> d h t")), kTb)
                kb.dma(DR(Vd.rearrange("h p t d -> p h t d")[:, :, blk * 4:(blk + 1) * 4, :]), vb)
                kb.dma(DR(obT[:, bsl].rearrange("(c p) t -> p c t", p=128)), obTb)
            kb.barrier()

        if stop == 'A':
            return nc
        with ExitStack() as es:
            def A(name, shape, dt=F32, es=es):
                return TV(es.enter_context(nc.sbuf_tensor("S%d_%s" % (l, name), list(shape), dt)).ap())

            srot = Rot(PS[5:8]) if os.environ.get("SROT") != "all" else psrot
            pkt = A("pk", [128, PKW])
            kb.dma(pkt, DR(pk[l]))
            WAs = A("WAs", [128, 8, 512], BF16)
            load_w(WAs, 0, w_in[l], 2224, 2736, 8, pkt[:, PK_G1:PK_G1 + 8])
            Wglu = A("Wglu", [128, 4, 512], BF16)
            load_w(Wglu, 0, w_glu[l], 0, 512, 4, None)
            BTr = A("BTr", [128, 16, 128], BF16)
            BTi = A("BTi", [128, 16, 128], BF16)
            CTr = A("CTr", [128, 16, 128], BF16)
            CTi = A("CTi", [128, 16, 128], BF16)
            for dst, src, sh in ((BTr, btd[l, 0], (16, 128)), (BTi, btd[l, 1], (16, 128)), (CTr, ctd[l, 0], (16, 128)), (CTi, ctd[l, 1], (16, 128))):
                st = stage.next()
                sv = st.v(lambda a: a.rearrange("p k n -> p (k n)"))[:, 0:sh[0] * sh[1]].v(lambda a: a.rearrange("p (k n) -> p k n", k=sh[0]))
                kb.dma(sv, DR(src))
                kb.copy("dve", dst, sv)

            lre = A("lre", [128, 16])
            kb.ts("dve", lre, pkt[:, PK_LRE:PK_LRE + 16], -1e-4, ALU.min)
            lim = pkt[:, PK_LIM:PK_LIM + 16]
            dtt = A("dtt", [128, 16])
            kb.act(dtt, pkt[:, PK_LDT:PK_LDT + 16], AF.Exp)
            th = A("th", [128, 16])
            kb.tt("dve", th, lim, dtt, ALU.mult)
            rr_ = A("rr", [128, 16])
            kb.tt("dve", rr_, lre, dtt, ALU.mult)
            kb.act(rr_, rr_, AF.Exp)
            sth = A("sth", [128, 16])
            cth = A("cth", [128, 16])
            sincos(A, th, 16, sth, cth, "t")
            nre = A("nre", [128, 16])
            nim = A("nim", [128, 16])
            kb.tt("dve", nre, rr_, cth, ALU.mult)
            kb.ts("dve", nre, nre, -1.0, ALU.add)
            kb.tt("dve", nim, rr_, sth, ALU.mult)
            den = A("den", [128, 16])
            tmp16 = A("tmp16", [128, 16])
            kb.tt("dve", den, lre, lre, ALU.mult)
            kb.tt("dve", tmp16, lim, lim, ALU.mult)
            kb.tt("dve", den, den, tmp16, ALU.add)
            kb.recip(den, den)
            cre = A("cre", [128, 16])
            cim = A("cim", [128, 16])
            kb.tt("dve", cre, nre, lre, ALU.mult)
            kb.tt("dve", tmp16, nim, lim, ALU.mult)
            kb.tt("dve", cre, cre, tmp16, ALU.add)
            kb.tt("dve", cre, cre, den, ALU.mult)
            kb.tt("dve", cim, nim, lre, ALU.mult)
            kb.tt("dve", tmp16, nre, lim, ALU.mult)
            kb.tt("dve", cim, cim, tmp16, ALU.subtract)
            kb.tt("dve", cim, cim, den, ALU.mult)
            E2c = A("E2c", [128, 2048])
            E2s = A("E2s", [128, 2048])
            E1r = A("E1r", [128, 2048])
            E1i = A("E1i", [128, 2048])
            Rz = A("Rz", [128, 2048])
            es2 = ExitStack()
            tang = A("tang", [128, 2048], es=es2)
            tau = cst[:, C_TAU:C_TAU + 128]
            for j in range(16):
                kb.ts("dve", tang[:, j * 128:(j + 1) * 128], tau, th[:, j:j + 1], ALU.mult)
            sincos(lambda n_, s_, d_: A(n_, s_, d_, es=es2), tang, 2048, E2s, E2c, "T")
            for j in range(16):
                sl = slice(j * 128, (j + 1) * 128)
                kb.ts("dve", tang[:, sl], E2s[:, sl], cim[:, j:j + 1], ALU.mult)
                kb.stt(E1r[:, sl], E2c[:, sl], cre[:, j:j + 1], tang[:, sl], ALU.mult, ALU.add)
                kb.ts("dve", tang[:, sl], E2s[:, sl], cre[:, j:j + 1], ALU.mult)
                kb.stt(E1i[:, sl], E2c[:, sl], cim[:, j:j + 1], tang[:, sl], ALU.mult, ALU.subtract)
                kb.ts("dve", Rz[:, sl], cst[:, C_TRIU + 127:C_TRIU + 128].v(lambda a: a.to_broadcast([128, 128])), rr_[:, j:j + 1], ALU.mult)
            Rz3 = Rz.v(lambda a: a.rearrange("p (j t) -> p j t", t=128))
            kb.memset("dve", Rz3[:, :, 0:1], 0.0)
            kb.barrier()
            es2.close()
            car_r = A("carr", [128, 16])
            car_i = A("cari", [128, 16])
            kb.memset("dve", car_r, 0.0)
            kb.memset("dve", car_i, 0.0)
            if stop == 'S1':
                kb.barrier()
                return nc
            xpool = Rot([A("x%d" % i, [128, 1024]) for i in range(2)])
            junk = A("junk", [128, 1024], BF16)
            ssr = Rot([A("ssr%d" % i, [128, 2]) for i in range(2)])
            hb = A("hb", [128, 1024], BF16)
            hT = A("hT", [128, 8, 512], BF16)
            uT = A("uT", [128, 4, 512])
            uTb = A("uTb", [128, 4, 512], BF16)
            t1 = A("t1", [128, 1024])
            t2 = A("t2", [128, 1024])
            btr = A("btr", [128, 1024])
            bti = A("bti", [128, 1024])
            xsr = A("xsr", [128, 1024])
            xsi = A("xsi", [128, 1024])
            xbr = A("xbr", [128, 1024], BF16)
            xbi = A("xbi", [128, 1024], BF16)
            rc = A("rc", [128, 16])
            ctmp = A("ctmp", [128, 32])
            y32 = A("y32", [128, 4, 512])
            yb = A("yb", [128, 4, 512], BF16)
            ocTb = A("ocTb", [128, 4, 512], BF16)
            g3 = A("g3", [128, 512])
            d5 = pkt[:, PK_D5:PK_D5 + 4]
            bglu = A("bgluh", [128, 4])
            kb.ts("dve", bglu, pkt[:, PK_BGLU:PK_BGLU + 4], 0.5, ALU.mult)

            for blk in range(NB):
                bsl = slice(blk * 512, (blk + 1) * 512)
                for tt_ in range(4):
                    tix = blk * 4 + tt_
                    xt = xpool.next()
                    kb.dma(xt, DR(xsrc[tix * 128:(tix + 1) * 128, :]))
                    norm_transpose(xt, hT, slice(tt_ * 128, (tt_ + 1) * 128), (junk, ssr.next(), hb), srot)
                if stop == 'S15':
                    kb.barrier()
                    return nc
                for ci in range(4):
                    p = srot.next()
                    for kc in range(8):
                        kb.mm(p, WAs[:, kc, ci * 128:(ci + 1) * 128], hT[:, kc, :], kc == 0, kc == 7)
                    if stop == 'S16':
                        kb.barrier()
                        return nc
                    kb.copy("dve", uT[:, ci, :], p)
                    kb.copy("pool", uTb[:, ci, :], uT[:, ci, :])
                if stop == 'S2':
                    kb.barrier()
                    return nc
                for tt_ in range(4):
                    tsl = slice(tt_ * 128, (tt_ + 1) * 128)
                    py = [PS[0]]
                    for half in range(2):
                        pbr, pbi, pbr2, pbi2 = PS[1], PS[2], PS[3], PS[4]
                        banks_r, banks_i = (pbr, pbr2), (pbi, pbi2)
                        for jj in range(8):
                            j = half * 8 + jj
                            c, q_ = j // 4, j % 4
                            rs = slice(32 * q_, 32 * q_ + 32)
                            kb.mm(banks_r[jj // 4][:, (jj % 4) * 128:(jj % 4 + 1) * 128], BTr[:, j, :], uTb[:, c, tsl], True, True)
                            kb.mm(banks_i[jj // 4][:, (jj % 4) * 128:(jj % 4 + 1) * 128], BTi[:, j, :], uTb[:, c, tsl], True, True)
                        hs = slice(half * 1024, (half + 1) * 1024)
                        for g_ in range(2):
                            gs = slice(g_ * 512, (g_ + 1) * 512)
                            hg = slice(half * 1024 + g_ * 512, half * 1024 + (g_ + 1) * 512)
                            kb.tt("dve", t1[:, gs], banks_r[g_], E1r[:, hg], ALU.mult)
                            kb.tt("dve", t2[:, gs], banks_i[g_], E1i[:, hg], ALU.mult)
                            kb.tt("dve", btr[:, gs], t1[:, gs], t2[:, gs], ALU.subtract)
                            kb.tt("dve", t1[:, gs], banks_i[g_], E1r[:, hg], ALU.mult)
                            kb.tt("dve", t2[:, gs], banks_r[g_], E1i[:, hg], ALU.mult)
                            kb.tt("pool", bti[:, gs], t1[:, gs], t2[:, gs], ALU.add)
                        if stop == 'S3':
                            kb.barrier()
                            return nc
                        js = slice(half * 8, half * 8 + 8)
                        kb.tt("dve", rc[:, 0:8], rr_[:, js], car_r[:, js], ALU.mult)
                        kb.tt("dve", rc[:, 8:16], rr_[:, js], car_i[:, js], ALU.mult)
                        b3r = btr.v(lambda a: a.rearrange("p (j t) -> p j t", t=128))
                        b3i = bti.v(lambda a: a.rearrange("p (j t) -> p j t", t=128))
                        kb.tt("dve", b3r[:, :, 0], b3r[:, :, 0], rc[:, 0:8], ALU.add)
                        kb.tt("dve", b3i[:, :, 0], b3i[:, :, 0], rc[:, 8:16], ALU.add)
                        kb.scan(xsr, Rz[:, hs], btr, 0.0)
                        kb.scan(xsi, Rz[:, hs], bti, 0.0)
                        if stop == 'S4':
                            kb.barrier()
                            return nc
                        x3r = xsr.v(lambda a: a.rearrange("p (j t) -> p j t", t=128))
                        x3i = xsi.v(lambda a: a.rearrange("p (j t) -> p j t", t=128))
                        e3c = E2c[:, hs].v(lambda a: a.rearrange("p (j t) -> p j t", t=128))
                        e3s = E2s[:, hs].v(lambda a: a.rearrange("p (j t) -> p j t", t=128))
                        kb.tt("dve", ctmp[:, 0:8], x3r[:, :, 127], e3c[:, :, 127], ALU.mult)
                        kb.tt("dve", ctmp[:, 8:16], x3i[:, :, 127], e3s[:, :, 127], ALU.mult)
                        kb.tt("dve", ctmp[:, 16:24], x3r[:, :, 127], e3s[:, :, 127], ALU.mult)
                        kb.tt("dve", ctmp[:, 24:32], x3i[:, :, 127], e3c[:, :, 127], ALU.mult)
                        kb.tt("dve", car_r[:, js], ctmp[:, 0:8], ctmp[:, 8:16], ALU.subtract)
                        kb.tt("dve", car_i[:, js], ctmp[:, 16:24], ctmp[:, 24:32], ALU.add)
                        if stop == 'S5':
                            kb.barrier()
                            return nc
                        kb.tt("pool", t1, xsr, E2c[:, hs], ALU.mult)
                        kb.tt("pool", t2, xsi, E2s[:, hs], ALU.mult)
                        kb.tt("dve", xbr, t1, t2, ALU.subtract)
                        kb.tt("dve", t1, xsi, E2c[:, hs], ALU.mult)
                        kb.tt("dve", t2, xsr, E2s[:, hs], ALU.mult)
                        kb.stt(xbi, t1, -1.0, t2, ALU.mult, ALU.subtract)
                        for jj in range(8):
                            j = half * 8 + jj
                            c, q_ = j // 4, j % 4
                            rs = slice(32 * q_, 32 * q_ + 32)
                            kb.mm(py[0][:, c * 128:(c + 1) * 128], CTr[:, j, :], xbr[:, jj * 128:(jj + 1) * 128], q_ == 0, False)
                            kb.mm(py[0][:, c * 128:(c + 1) * 128], CTi[:, j, :], xbi[:, jj * 128:(jj + 1) * 128], False, q_ == 3)
                    if stop == 'S6':
                        kb.barrier()
                        return nc
                    for c in range(4):
                        kb.stt(y32[:, c, tsl], uT[:, c, tsl], d5[:, c:c + 1], py[0][:, c * 128:(c + 1) * 128], ALU.mult, ALU.add)

                if stop == 'S7':
                    kb.barrier()
                    return nc
                for c in range(4):
                    yc = y32[:, c, :]
                    kb.tt("pool", g3, yc, yc, ALU.mult)
                    kb.ts("dve", g3, g3, 0.044715, ALU.mult, 1.0, ALU.add)
                    kb.tt("pool", g3, g3, yc, ALU.mult)
                    kb.act(g3, g3, AF.Tanh, scale=0.7978845608028654)
                    kb.stt(g3, g3, 1.0, yc, ALU.add, ALU.mult)
                    kb.ts("dve", yc, g3, 0.5, ALU.mult)
                    kb.copy("pool", yb[:, c, :], yc)
                if stop == 'S8':
                    kb.barrier()
                    return nc
                for mc in range(4):
                    p = srot.next()
                    for c in range(4):
                        kb.mm(p, Wglu[:, c, mc * 128:(mc + 1) * 128], yb[:, c, :], c == 0, c == 3)
                    kb.act(g3, p, AF.Tanh, bias=bglu[:, mc:mc + 1], scale=0.5)
                    kb.stt(g3, g3, 1.0, y32[:, mc, :], ALU.add, ALU.mult)
                    kb.ts("dve", ocTb[:, mc, :], g3, 0.5, ALU.mult)
                kb.dma(DR(ocT[:, bsl].rearrange("(c p) t -> p c t", p=128)), ocTb)
            kb.barrier()


        if stop == 'S':
            return nc
        with ExitStack() as es:
            def Bt(name, shape, dt=F32):
                return TV(es.enter_context(nc.sbuf_tensor("B%d_%s" % (l, name), list(shape), dt)).ap())

            KTh = Rot([Bt("KT%d" % i, [96, L], BF16) for i in range(2)])
            Vh = Rot([Bt("V%d" % i, [128, NT, 65], BF16) for i in range(2)])
            Vraw = Rot([Bt("Vr%d" % i, [128, NT, 64], BF16) for i in range(2)])
            QTh = Rot([Bt("QT%d" % i, [96, L], BF16) for i in range(2)])
            Pt = Rot([Bt("P%d" % i, [128, 512], BF16) for i in range(3)])
            orec = Bt("orec", [128, 4])
            onb = Rot([Bt("onb%d" % i, [128, 4, 64], BF16) for i in range(2)])
            oTs = Rot([Bt("oTs%d" % i, [64, 512], BF16) for i in range(2)])
            for V_ in Vh.items:
                kb.memset("dve", V_[:, :, 64:65], 1.0)
            SB_ = Rot(PS[0:3])
            OB_ = PS[3:7]
            for h in range(8):
                Kt_, V_, Q_ = KTh.next(), Vh.next(), QTh.next()
                kb.dma(Kt_, DR(KT[h]))
                kb.dma(Q_, DR(QT[h]))
                Vr_ = Vraw.next()
                kb.dma(Vr_, DR(Vd[h]))
                kb.copy("pool", V_[:, :, 0:64], Vr_)
                for qb_ in range(NB):
                    nk = 4 * (qb_ + 1)
                    for kt in range(nk):
                        j = kt - 4 * qb_
                        q0 = max(j, 0) * 128
                        ps_ = SB_.next()
                        kb.mm(ps_[:, q0:512], Kt_[:, kt * 128:(kt + 1) * 128], Q_[:, qb_ * 512 + q0:(qb_ + 1) * 512], True, True)
                        P_ = Pt.next()
                        kb.act(P_[:, q0:512], ps_[:, q0:512], AF.Exp)
                        if j >= 0:
                            kb.tt("dve", P_[:, j * 128:(j + 1) * 128], P_[:, j * 128:(j + 1) * 128], trimb, ALU.mult)
                        for qi in range(max(j, 0), 4):
                            last = (kt == 4 * qb_ + qi)
                            kb.mm(OB_[qi][:, 0:65], P_[:, qi * 128:(qi + 1) * 128], V_[:, kt, :], kt == 0, last)
                    on_ = onb.next()
                    for qi in range(4):
                        kb.recip(orec[:, qi:qi + 1], OB_[qi][:, 64:65])
                        kb.ts("dve", on_[:, qi, :], OB_[qi][:, 0:64], orec[:, qi:qi + 1], ALU.mult)
                    pT = PS[7]
                    pTb = psb(pT)
                    for qi in range(4):
                        kb.tr(pTb[0:64, qi * 128:(qi + 1) * 128], on_[:, qi, :], identb)
                    oT_ = oTs.next()
                    kb.copy("act", oT_, pTb[0:64, 0:512])
                    kb.dma(DR(oaT[h * 64:(h + 1) * 64, qb_ * 512:(qb_ + 1) * 512]), oT_)
            kb.barrier()

        if stop == 'B':
            return nc
        with ExitStack() as es:
            def Ct(name, shape, dt=F32):
                return TV(es.enter_context(nc.sbuf_tensor("C%d_%s" % (l, name), list(shape), dt)).ap())

            pkt = Ct("pk", [128, PKW])
            kb.dma(pkt, DR(pk[l]))
            gbh = Ct("gbh", [128, 24])
            kb.ts("dve", gbh, pkt[:, PK_GB:PK_GB + 24], 0.5, ALU.mult)
            Wg = Ct("Wg", [128, 8, 3072], BF16)
            load_w(Wg, 0, w_in[l], 2736, 5808, 8, pkt[:, PK_G1:PK_G1 + 8])
            Wb = [Ct("Wb%d" % i, [128, 4, 1024], BF16) for i in range(3)]
            for i in range(3):
                load_w(Wb[i], 0, w_br[i][l], 0, 1024, 4, None)
            Wo = Ct("Wo", [128, 8, 1024], BF16)
            load_w(Wo, 0, w_out[l], 0, 1024, 8, None)
            xt4 = [Ct("x%d" % i, [128, 1024]) for i in range(4)]
            junk = Ct("junk", [128, 1024], BF16)
            ssr = Rot([Ct("ssr%d" % i, [128, 2]) for i in range(2)])
            hb = Ct("hb", [128, 1024], BF16)
            hT = Ct("hT", [128, 8, 512], BF16)
            oin = [Ct("oin%d" % i, [128, 4, 512], BF16) for i in range(3)]
            gsb = Rot([Ct("g%d" % i, [128, 512]) for i in range(3)])
            macc = Ct("macc", [128, 512])
            mtmp = Ct("mtmp", [128, 512])
            mT = Ct("mT", [128, 8, 512], BF16)
            xo = Rot([Ct("xo%d" % i, [128, 1024]) for i in range(2)])
            for blk in range(NB):
                bsl = slice(blk * 512, (blk + 1) * 512)
                for i, src in enumerate((oaT, obT, ocT)):
                    kb.dma(oin[i], DR(src[:, bsl].rearrange("(c p) t -> p c t", p=128)))
                for tt_ in range(4):
                    tix = blk * 4 + tt_
                    kb.dma(xt4[tt_], DR(xsrc[tix * 128:(tix + 1) * 128, :]))
                    norm_transpose(xt4[tt_], hT, slice(tt_ * 128, (tt_ + 1) * 128), (junk, ssr.next(), hb))
                for fc in range(8):
                    for b in range(3):
                        pg = psrot.next()
                        gc = b * 8 + fc
                        for kc in range(8):
                            kb.mm(pg, Wg[:, kc, gc * 128:(gc + 1) * 128], hT[:, kc, :], kc == 0, kc == 7)
                        g_ = gsb.next()
                        kb.act(g_, pg, AF.Tanh, bias=gbh[:, gc:gc + 1], scale=0.5)
                        pp = psrot.next()
                        for c in range(4):
                            kb.mm(pp, Wb[b][:, c, fc * 128:(fc + 1) * 128], oin[b][:, c, :], c == 0, c == 3)
                        if b == 0:
                            kb.stt(macc, g_, 1.0, pp, ALU.add, ALU.mult)
                        else:
                            kb.stt(mtmp, g_, 1.0, pp, ALU.add, ALU.mult)
                            kb.tt("pool", macc, macc, mtmp, ALU.add)
                    kb.ts("dve", mT[:, fc, :], macc, 0.5, ALU.mult)
                for tt_ in range(4):
                    tix = blk * 4 + tt_
                    tsl = slice(tt_ * 128, (tt_ + 1) * 128)
                    xo_ = xo.next()
                    for nh in range(2):
                        p = psrot.next()
                        for kc in range(8):
                            kb.mm(p, mT[:, kc, tsl], Wo[:, kc, nh * 512:(nh + 1) * 512], kc == 0, kc == 7)
                        kb.tt("dve", xo_[:, nh * 512:(nh + 1) * 512], p, xt4[tt_][:, nh * 512:(nh + 1) * 512], ALU.add)
                    kb.dma(DR(x1d[tix * 128:(tix + 1) * 128, :]), xo_)
            kb.barrier()

        if stop == 'C':
            return nc
        with ExitStack() as es:
            def Dt(name, shape, dt=F32):
                return TV(es.enter_context(nc.sbuf_tensor("D%d_%s" % (l, name), list(shape), dt)).ap())

            pkt = Dt("pk", [128, PKW])
            kb.dma(pkt, DR(pk[l]))
            W1 = Dt("W1", [128, 8, 4096], BF16)
            load_w(W1, 0, w_ff1[l], 0, 4096, 8, pkt[:, PK_G2:PK_G2 + 8])
            W2 = Dt("W2", [128, 32, 1024], BF16)
            for k0 in range(0, 32, 8):
                load_w(W2[:, k0:k0 + 8, :], 0, w_ff2[l][k0 * 128:(k0 + 8) * 128, :], 0, 1024, 8, None)
            xt4 = [Dt("x%d" % i, [128, 1024]) for i in range(2)]
            junk = Dt("junk", [128, 1024], BF16)
            ssr = Rot([Dt("ssr%d" % i, [128, 2]) for i in range(2)])
            hb = Dt("hb", [128, 1024], BF16)
            hT = Dt("hT", [128, 8, 256], BF16)
            uT = Dt("uT", [128, 32, 256], BF16)
            rl = Rot([Dt("rl%d" % i, [128, 256]) for i in range(2)])
            for blk in range(L // 256):
                for tt_ in range(2):
                    tix = blk * 2 + tt_
                    kb.dma(xt4[tt_], DR(x1d[tix * 128:(tix + 1) * 128, :]))
                    norm_transpose(xt4[tt_], hT, slice(tt_ * 128, (tt_ + 1) * 128), (junk, ssr.next(), hb))
                for fc in range(32):
                    p = psrot.next()
                    for kc in range(8):
                        kb.mm(p[:, 0:256], W1[:, kc, fc * 128:(fc + 1) * 128], hT[:, kc, :], kc == 0, kc == 7)
                    r_ = rl.next()
                    kb.act(r_, p[:, 0:256], AF.Relu)
                    kb.tt("pool" if fc % 2 else "dve", uT[:, fc, :], r_, r_, ALU.mult)
                for tt_ in range(2):
                    tix = blk * 2 + tt_
                    tsl = slice(tt_ * 128, (tt_ + 1) * 128)
                    xo_ = xt4[tt_]
                    for nh in range(2):
                        p = psrot.next()
                        for fc in range(32):
                            kb.mm(p, uT[:, fc, tsl], W2[:, fc, nh * 512:(nh + 1) * 512], fc == 0, fc == 31)
                        kb.tt("dve", xo_[:, nh * 512:(nh + 1) * 512], p, xt4[tt_][:, nh * 512:(nh + 1) * 512], ALU.add)
                    kb.dma(DR(xdst[tix * 128:(tix + 1) * 128, :]), xo_)
            kb.barrier()
    return nc


def _host_params(inp, L):
    f = lambda a: np.ascontiguousarray(np.asarray(a, dtype=np.float32))
    pk = np.zeros((DEPTH, 128, PKW), np.float32)
    rowp = np.zeros((DEPTH, 320), np.float32)
    wgp = np.zeros((DEPTH, 17, 256), np.float32)
    bt = np.zeros((DEPTH, 2, 128, 16, 128), np.float32)
    ct = np.zeros((DEPTH, 2, 128, 16, 128), np.float32)
    for l in range(DEPTH):
        pk[l, :, PK_G1:PK_G1 + 8] = f(inp["norm1_g"])[l].reshape(8, 128).T
        pk[l, :, PK_G2:PK_G2 + 8] = f(inp["norm2_g"])[l].reshape(8, 128).T
        pk[l, :, PK_QNG:PK_QNG + 3] = f(inp["mla_q_norm_g"])[l].reshape(3, 128).T
        pk[l, :, PK_KVNG:PK_KVNG + 2] = f(inp["mla_kv_norm_g"])[l].reshape(2, 128).T

        def st16(a):
            return a.reshape(16, 2, 64).transpose(1, 2, 0).reshape(128, 16)

        pk[l, :, PK_LRE:PK_LRE + 16] = st16(f(inp["s5_lam_re"])[l])
        pk[l, :, PK_LIM:PK_LIM + 16] = st16(f(inp["s5_lam_im"])[l])
        pk[l, :, PK_LDT:PK_LDT + 16] = st16(np.repeat(f(inp["s5_log_dt"])[l][:, None], 64, axis=1))
        pk[l, :, PK_D5:PK_D5 + 4] = f(inp["s5_d"])[l].reshape(4, 128).T
        pk[l, :, PK_BGLU:PK_BGLU + 4] = f(inp["s5_b_glu"])[l].reshape(4, 128).T
        pk[l, :, PK_GB:PK_GB + 24] = f(inp["gate_b"])[l].reshape(24, 128).T
        rowp[l, 0:96] = f(inp["mla_q_head_g"])[l]
        rowp[l, 96:192] = f(inp["mla_k_head_g"])[l]
        rowp[l, 192:320] = f(inp["gla_out_g"])[l]
        wgp[l, 0:16] = f(inp["gla_w_gate"])[l]
        wgp[l, 16] = f(inp["gla_b_gate"])[l]
        for ri, (bk, ck) in enumerate((("s5_b_re", "s5_c_re"), ("s5_b_im", "s5_c_im"))):
            B = f(inp[bk])[l]
            C = f(inp[ck])[l]
            for j in range(16):
                c, q = j // 4, j % 4
                for gl in range(2):
                    g = 2 * j + gl
                    bt[l, ri, 32 * q + 16 * gl:32 * q + 16 * gl + 16, j, 64 * gl:64 * gl + 64] = B[g].T
                    ct[l, ri, 64 * gl:64 * gl + 64, j, 32 * q + 16 * gl:32 * q + 16 * gl + 16] = C[g].T
    NT = L // 128
    consts = np.zeros((128, C_POS + NT), np.float32)
    consts[:, C_ID:C_ID + 128] = np.eye(128, dtype=np.float32)
    consts[:, C_TRIU:C_TRIU + 128] = np.triu(np.ones((128, 128), np.float32))
    consts[:, C_TRIL:C_TRIL + 128] = np.tril(np.ones((128, 128), np.float32), -1)
    consts[:, C_TAU:C_TAU + 128] = np.arange(1, 129, dtype=np.float32)[None, :]
    consts[:, C_INVF:C_INVF + 16] = (10000.0 ** (-np.arange(0, 32, 2, dtype=np.float32) / 32.0)).astype(np.float32)[None, :]
    consts[:, C_POS:C_POS + NT] = (np.arange(NT, dtype=np.float32)[None, :] * 128.0 + np.arange(128, dtype=np.float32)[:, None])
    return dict(pk=pk, rowp=rowp, wg=wgp, bt=bt, ct=ct, consts=consts)


def make_in_maps(inp, L, n_cores):
    f = lambda a: np.ascontiguousarray(np.asarray(a, dtype=np.float32))
    shared = dict(
        w_in=f(inp["w_in"]), w_uq=f(inp["mla_w_uq"]), w_ukv=f(inp["mla_w_ukv"]), w_glu=f(inp["s5_w_glu"]),
        w_br0=f(inp["w_br_mla"]), w_br1=f(inp["w_br_gla"]), w_br2=f(inp["w_br_s5"]), w_out=f(inp["w_out"]),
        w_ff1=f(inp["w_ff1"]), w_ff2=f(inp["w_ff2"]))
    shared.update(_host_params(inp, L))
    x = f(inp["x"])
    nb = x.shape[0]
    maps = []
    for c in range(n_cores):
        m = dict(shared)
        m["x"] = np.ascontiguousarray(x[c % nb, :L])
        maps.append(m)
    return maps


def kernel(**inputs):
    L = SEQ
    nc = build_nc(L)
    maps = make_in_maps(inputs, L, 8)
    res = run_bass_kernel_spmd(nc, maps, core_ids=list(range(8)))
    out = np.stack([np.asarray(res.results[b]["out"], dtype=np.float32) for b in range(BATCH)], axis=0)
    return out


def build_nc(L, dbg=False, nlayers=DEPTH, stop=None):
    return build(L, dbg, nlayers, stop)
```

```python
import math
import os
from contextlib import ExitStack
import numpy as np
import concourse.bass as bass
import concourse.mybir as mybir
from concourse.bass_utils import run_bass_kernel_spmd

F32 = mybir.dt.float32
BF16 = mybir.dt.bfloat16
AF = mybir.ActivationFunctionType
ALU = mybir.AluOpType
AX = mybir.AxisListType

D = 1024
DEPTH = 2
SEQ = 8192
BATCH = 4
EPS = 1e-6
MAGIC = 12582912.0
TWO_PI = 2.0 * math.pi

PK_G1, PK_G2, PK_QNG, PK_KVNG, PK_LRE, PK_LIM, PK_LDT, PK_D5, PK_BGLU, PK_GB = 0, 8, 16, 19, 21, 37, 53, 69, 73, 77
PKW = 101
C_ID, C_TRIU, C_TRIL, C_TAU, C_INVF, C_POS = 0, 128, 256, 384, 512, 528


class Res:
    __slots__ = ("w", "r")

    def __init__(self):
        self.w = None
        self.r = {}


class TV:
    def __init__(self, ap, res=None):
        self.ap = ap
        self.res = res if res is not None else Res()

    def __getitem__(self, k):
        return TV(self.ap[k], self.res)

    def v(self, f):
        return TV(f(self.ap), self.res)

    def sub(self, k):
        return TV(self.ap[k], Res())


class KB:
    def __init__(self, nc):
        self.nc = nc
        self.eng = {"pe": nc.tensor, "act": nc.scalar, "dve": nc.vector, "pool": nc.gpsimd, "sp": nc.sync}
        self.sems = {}
        self.cnt = {}
        for e in ("pe", "act", "dve", "pool"):
            self.sems[e] = nc.alloc_semaphore("c_" + e)
            self.cnt[e] = 0
        self.dq = {}
        for q, n in (("sp", 16), ("act", 4), ("pool", 4)):
            keys = []
            for i in range(n):
                k = "d_%s%d" % (q, i)
                self.sems[k] = nc.alloc_semaphore(k)
                self.cnt[k] = 0
                keys.append(k)
            self.dq[q] = [keys, 0]
        self.seen = {e: {} for e in self.eng}
        self.rr = 0

    def _deps(self, reads, writes):
        need = {}

        def add(k, v):
            if need.get(k, 0) < v:
                need[k] = v

        for R in reads:
            if R.w is not None:
                add(*R.w)
        for R in writes:
            if R.w is not None:
                add(*R.w)
            for k, v in R.r.items():
                add(k, v)
        return need

    def _wait(self, e, need):
        for k, v in need.items():
            if e == "pe" and k == "pe":
                continue
            if self.seen[e].get(k, 0) >= v:
                continue
            self.eng[e].wait_ge(self.sems[k], v)
            self.seen[e][k] = v

    def op(self, e, fn, reads, writes):
        reads = [t.res for t in reads]
        writes = [t.res for t in writes]
        self._wait(e, self._deps(reads, writes))
        ins = fn()
        self.cnt[e] += 1
        ins.then_inc(self.sems[e], 1)
        c = self.cnt[e]
        for R in writes:
            R.w = (e, c)
            R.r = {}
        for R in reads:
            R.r[e] = c
        return ins

    def dma(self, out, in_, q="sp"):
        keys, idx = self.dq[q]
        k = keys[idx]
        self.dq[q][1] = (idx + 1) % len(keys)
        need = self._deps([in_.res], [out.res])
        if self.cnt[k] > 0:
            need[k] = max(need.get(k, 0), self.cnt[k])
        self._wait(q, need)
        ins = self.eng[q].dma_start(out=out.ap, in_=in_.ap)
        self.cnt[k] += 16
        ins.then_inc(self.sems[k], 16)
        c = self.cnt[k]
        out.res.w = (k, c)
        out.res.r = {}
        in_.res.r[k] = c

    def barrier(self):
        for e in self.eng:
            for k, v in self.cnt.items():
                if v > 0 and self.seen[e].get(k, 0) < v:
                    self.eng[e].wait_ge(self.sems[k], v)
                    self.seen[e][k] = v

    def _ve(self, e):
        return self.eng[e]

    def tt(self, e, out, a, b, op):
        return self.op(e, lambda: self._ve(e).tensor_tensor(out=out.ap, in0=a.ap, in1=b.ap, op=op), [a, b], [out])

    def ts(self, e, out, a, s1, op0, s2=None, op1=None):
        rd = [a]
        s1a, s2a = s1, s2
        if isinstance(s1, TV):
            rd.append(s1)
            s1a = s1.ap
        if isinstance(s2, TV):
            rd.append(s2)
            s2a = s2.ap
        if op1 is None:
            return self.op(e, lambda: self._ve(e).tensor_scalar(out=out.ap, in0=a.ap, scalar1=s1a, scalar2=None, op0=op0), rd, [out])
        return self.op(e, lambda: self._ve(e).tensor_scalar(out=out.ap, in0=a.ap, scalar1=s1a, scalar2=s2a, op0=op0, op1=op1), rd, [out])

    def stt(self, out, a, sc, b, op0, op1):
        rd = [a, b]
        sca = sc
        if isinstance(sc, TV):
            rd.append(sc)
            sca = sc.ap
        return self.op("dve", lambda: self.nc.vector.scalar_tensor_tensor(out=out.ap, in0=a.ap, scalar=sca, in1=b.ap, op0=op0, op1=op1), rd, [out])

    def copy(self, e, out, a):
        if e == "act":
            return self.op(e, lambda: self.nc.scalar.copy(out=out.ap, in_=a.ap), [a], [out])
        return self.op(e, lambda: self._ve(e).tensor_copy(out=out.ap, in_=a.ap), [a], [out])

    def memset(self, e, out, val):
        return self.op(e, lambda: self._ve(e).memset(out.ap, val), [], [out])

    def act(self, out, a, func, bias=None, scale=None, accum=None):
        rd = [a]
        wr = [out]
        kw = {}
        if bias is not None:
            if isinstance(bias, TV):
                rd.append(bias)
                kw["bias"] = bias.ap
            else:
                kw["bias"] = bias
        if scale is not None:
            if isinstance(scale, TV):
                rd.append(scale)
                kw["scale"] = scale.ap
            else:
                kw["scale"] = scale
        if accum is not None:
            wr.append(accum)
            kw["accum_out"] = accum.ap
        return self.op("act", lambda: self.nc.scalar.activation(out=out.ap, in_=a.ap, func=func, **kw), rd, wr)

    def mm(self, out, lhsT, rhs, start, stop):
        return self.op("pe", lambda: self.nc.tensor.matmul(out.ap, lhsT=lhsT.ap, rhs=rhs.ap, start=start, stop=stop), [lhsT, rhs], [out])

    def tr(self, out, a, ident):
        return self.op("pe", lambda: self.nc.tensor.transpose(out.ap, a.ap, ident.ap), [a, ident], [out])

    def red(self, out, a, op=ALU.add):
        return self.op("dve", lambda: self.nc.vector.tensor_reduce(out=out.ap, in_=a.ap, axis=AX.X, op=op), [a], [out])

    def scan(self, out, d0, d1, init, op0=ALU.mult, op1=ALU.add):
        rd = [d0, d1]
        ia = init
        if isinstance(init, TV):
            rd.append(init)
            ia = init.ap
        return self.op("dve", lambda: self.nc.vector.tensor_tensor_scan(out=out.ap, data0=d0.ap, data1=d1.ap, initial=ia, op0=op0, op1=op1), rd, [out])

    def recip(self, out, a):
        return self.op("dve", lambda: self.nc.vector.reciprocal(out=out.ap, in_=a.ap), [a], [out])


class Rot:
    def __init__(self, items):
        self.items = items
        self.i = 0

    def next(self):
        t = self.items[self.i]
        self.i = (self.i + 1) % len(self.items)
        return t


def build(L, dbg=False, nlayers=DEPTH, stop=None):
    nc = bass.Bass("TRN2", target_bir_lowering=False)
    kb = KB(nc)
    NT = L // 128
    NB = L // 512

    def din(name, shape, dt=F32):
        return nc.dram_tensor(name, list(shape), dt, kind="ExternalInput").ap()

    def dscr(name, shape, dt):
        if dbg:
            return nc.dram_tensor(name, list(shape), dt, kind="ExternalOutput").ap()
        return nc.dram_tensor(name, list(shape), dt).ap()

    x_in = din("x", [L, D])
    w_in = din("w_in", [DEPTH, 1024, 5808])
    w_uq = din("w_uq", [DEPTH, 384, 768])
    w_ukv = din("w_ukv", [DEPTH, 256, 1024])
    w_glu = din("w_glu", [DEPTH, 512, 512])
    w_br = [din("w_br%d" % i, [DEPTH, 512, 1024]) for i in range(3)]
    w_out = din("w_out", [DEPTH, 1024, 1024])
    w_ff1 = din("w_ff1", [DEPTH, 1024, 4096])
    w_ff2 = din("w_ff2", [DEPTH, 4096, 1024])
    pk = din("pk", [DEPTH, 128, PKW])
    rowp = din("rowp", [DEPTH, 320])
    wg = din("wg", [DEPTH, 17, 256])
    btd = din("bt", [DEPTH, 2, 128, 16, 128])
    ctd = din("ct", [DEPTH, 2, 128, 16, 128])
    CW = C_POS + NT
    consts = din("consts", [128, CW])

    QT = dscr("QT", [8, 96, L], BF16)
    KT = dscr("KT", [8, 96, L], BF16)
    Vd = dscr("Vd", [8, 128, NT, 64], BF16)
    oaT = dscr("oaT", [512, L], BF16)
    obT = dscr("obT", [512, L], BF16)
    ocT = dscr("ocT", [512, L], BF16)
    x1d = dscr("x1", [L, D], F32)
    xmd = dscr("xm", [L, D], F32)
    outd = nc.dram_tensor("out", [L, D], F32, kind="ExternalOutput").ap()

    def DR(ap):
        return TV(ap)

    def sb(name, shape, dt=F32):
        return TV(nc.alloc_sbuf_tensor(name, list(shape), dt).ap())

    PS = [TV(nc.alloc_psum_tensor("ps%d" % i, [128, 512], F32).ap()) for i in range(8)]
    psrot = Rot(PS)

    def psb(p):
        return p.v(lambda a: a.bitcast(BF16))

    cst = sb("cst", [128, CW])
    kb.dma(cst, DR(consts))
    identf = cst[:, C_ID:C_ID + 128]
    identb = sb("identb", [128, 128], BF16)
    kb.copy("dve", identb, identf)
    trimb = sb("trimb", [128, 128], BF16)
    kb.copy("dve", trimb, cst[:, C_TRIU:C_TRIU + 128])
    triU = cst[:, C_TRIU:C_TRIU + 128]
    triUs = sb("triUs", [128, 128])
    triLs = sb("triLs", [128, 128])
    kb.ts("dve", triUs, cst[:, C_TRIU:C_TRIU + 128], -1.0 / 16.0, ALU.mult)
    kb.ts("dve", triLs, cst[:, C_TRIL:C_TRIL + 128], -1.0 / 16.0, ALU.mult)
    onesb = sb("onesb", [128, 128], BF16)
    kb.memset("dve", onesb, 1.0)
    mhalf = sb("mhalf", [128, 16])
    kb.memset("dve", mhalf, -0.5)

    def sincos(alloc_fn, ang, n, sin_out, cos_out, tag):
        t0 = alloc_fn("sc0" + tag, [128, n], F32)
        t1 = alloc_fn("sc1" + tag, [128, n], F32)
        for off, dst in ((0.0, sin_out), (0.25, cos_out)):
            kb.ts("dve", t0, ang, 1.0 / TWO_PI, ALU.mult, off, ALU.add)
            kb.ts("dve", t1, t0, MAGIC, ALU.add)
            kb.ts("dve", t1, t1, MAGIC, ALU.subtract)
            kb.tt("dve", t0, t0, t1, ALU.subtract)
            kb.act(dst, t0, AF.Sin, scale=TWO_PI * (1.0 - 1e-6))

    ropec = sb("ropec", [128, NT * 16])
    ropes = sb("ropes", [128, NT * 16])
    es0 = ExitStack()

    def sb0(name, shape, dt=F32):
        return TV(es0.enter_context(nc.sbuf_tensor(name, list(shape), dt)).ap())

    rang = sb0("rang", [128, NT * 16])
    kb.tt("dve", rang.v(lambda a: a.rearrange("p (t i) -> p t i", i=16)),
          cst[:, C_POS:C_POS + NT].v(lambda a: a.unsqueeze(2).to_broadcast([128, NT, 16])),
          cst[:, C_INVF:C_INVF + 16].v(lambda a: a.unsqueeze(1).to_broadcast([128, NT, 16])), ALU.mult)
    sincos(sb0, rang, NT * 16, ropes, ropec, "r")
    kb.barrier()
    es0.close()
    ropec3 = ropec.v(lambda a: a.rearrange("p (t i) -> p t i", i=16))
    ropes3 = ropes.v(lambda a: a.rearrange("p (t i) -> p t i", i=16))

    stage = Rot([sb("stg%d" % i, [128, 8, 256]) for i in range(2)])
    engcyc = Rot(["dve", "pool", "act"])

    def load_w(dst, dcol0, src2d, c0, c1, KC, scale=None):
        for cc in range(c0, c1, 256):
            ce = min(cc + 256, c1)
            n = ce - cc
            st = stage.next()
            kb.dma(st[:, 0:KC, 0:n], DR(src2d[:, cc:ce].rearrange("(kc p) n -> p kc n", p=128)))
            d0 = dcol0 + (cc - c0)
            if scale is None:
                kb.copy(engcyc.next(), dst[:, :, d0:d0 + n], st[:, 0:KC, 0:n])
            else:
                for kc in range(KC):
                    e = engcyc.next()
                    if e == "act":
                        kb.act(dst[:, kc, d0:d0 + n], st[:, kc, 0:n], AF.Copy, scale=scale[:, kc:kc + 1])
                    else:
                        kb.ts(e, dst[:, kc, d0:d0 + n], st[:, kc, 0:n], scale[:, kc:kc + 1], ALU.mult)

    def rstd_of(out, ss, n, cols):
        kb.ts("pool", out, ss, 1.0 / n, ALU.mult, EPS, ALU.add)
        kb.tt("pool", out, out, mhalf[:, 0:cols], ALU.pow)

    def norm_transpose(x_t, hT_dst, tcols, pool_tiles, rot=None):
        junk, ssr, hb = pool_tiles
        kb.act(junk, x_t, AF.Square, accum=ssr[:, 0:1])
        rstd_of(ssr[:, 1:2], ssr[:, 0:1], 1024.0, 1)
        kb.ts("dve", hb, x_t, ssr[:, 1:2], ALU.mult)
        p = (rot or psrot).next()
        pb = psb(p)
        for kc in range(8):
            kb.tr(pb[:, kc * 128:(kc + 1) * 128], hb[:, kc * 128:(kc + 1) * 128], identb)
        kb.copy("act", hT_dst[:, :, tcols], pb.v(lambda a: a.rearrange("p (k t) -> p k t", k=8)))

    if stop == '0':
        kb.barrier()
        return nc
    for l in range(nlayers):
        xsrc = x_in if l == 0 else xmd
        xdst = xmd if l == 0 else outd

        with ExitStack() as es:
            def A(name, shape, dt=F32, es=es):
                return TV(es.enter_context(nc.sbuf_tensor("A%d_%s" % (l, name), list(shape), dt)).ap())

            pkt = A("pk", [128, PKW])
            kb.dma(pkt, DR(pk[l]))
            rows = A("rows", [128, 320])
            kb.dma(rows, DR(rowp[l].partition_broadcast(128)))
            gq_s = A("gqs", [128, 96])
            kb.ts("dve", gq_s, rows[:, 0:96], 96.0 ** -0.5, ALU.mult)
            gk_r = rows[:, 96:192]
            og_h = A("ogh", [128, 128])
            kb.ts("dve", og_h, rows[:, 192:320], 0.5, ALU.mult)
            wga = A("wga", [32, 256])
            kb.dma(wga[0:17, :], DR(wg[l]))
            WAf = A("WAf", [128, 8, 1168], BF16)
            WAt = A("WAt", [128, 8, 1312], BF16)
            g1p = pkt[:, PK_G1:PK_G1 + 8]
            wl = w_in[l]
            for (c0, c1, d0) in ((0, 384, 0), (384, 640, 384), (672, 928, 640), (928, 1184, 896), (1696, 1712, 1152)):
                load_w(WAf, d0, wl, c0, c1, 8, g1p)
            for (c0, c1, d0) in ((640, 672, 0), (928, 1184, 32), (1184, 1696, 288), (1712, 2224, 800)):
                load_w(WAt, d0, wl, c0, c1, 8, g1p)
            if stop == 'A1':
                kb.barrier()
                return nc
            Wuq = A("Wuq", [128, 3, 768], BF16)
            load_w(Wuq, 0, w_uq[l], 0, 768, 3, pkt[:, PK_QNG:PK_QNG + 3])
            Wukv = A("Wukv", [128, 2, 1024], BF16)
            load_w(Wukv, 0, w_ukv[l], 0, 1024, 2, pkt[:, PK_KVNG:PK_KVNG + 2])
            Sst = [A("S%d" % i, [128, 128]) for i in range(2)]
            for s_ in Sst:
                kb.memset("dve", s_, 0.0)

            if stop == 'A2':
                kb.barrier()
                return nc
            xpool = Rot([A("x%d" % i, [128, 1024]) for i in range(2)])
            junk = A("junk", [128, 1024], BF16)
            ssr = Rot([A("ssr%d" % i, [128, 2]) for i in range(2)])
            hb = A("hb", [128, 1024], BF16)
            hT = A("hT", [128, 8, 512], BF16)
            cqT = A("cqT", [128, 3, 512], BF16)
            ckvT = A("ckvT", [128, 2, 512], BF16)
            sqT = A("sqT", [128, 5, 512], BF16)
            gqT = A("gqT", [128, 2, 512])
            gkT = A("gkT", [128, 2, 512])
            glrT = A("glrT", [32, 512])
            kb.memset("dve", glrT, 1.0)
            kk = A("kk", [128, 288])
            gv = A("gv", [128, 512])
            sr = A("sr", [128, 512])
            st2 = A("st2", [128, 4])
            q32 = A("q32", [128, 768])
            k96 = A("k96", [128, 768])
            kv32 = A("kv32", [128, 1024])
            sq768 = A("sq768", [128, 768])
            ssh = A("ssh", [128, 16])
            nrm = A("nrm", [128, 768])
            rtmp = A("rtmp", [128, 8, 16])
            rtmp2 = A("rtmp2", [128, 8, 16])
            qb = A("qb", [128, 768], BF16)
            qTb = A("qTb", [96, 8, 512], BF16)
            kTb = A("kTb", [96, 8, 512], BF16)
            vb = A("vb", [128, 8, 4, 64], BF16)
            lsp = A("lsp", [128, 256])
            eq = A("eq", [128, 256])
            ek = A("ek", [128, 256])
            eend = A("eend", [128, 256])
            qtT = A("qtT", [128, 2, 128])
            ktT = A("ktT", [128, 2, 128])
            kend = A("kend", [128, 256])
            Am = A("Am", [128, 128])
            o32 = A("o32", [128, 512])
            osq = A("osq", [128, 512])
            ob = A("ob", [128, 512], BF16)
            obTb = A("obTb", [128, 4, 512], BF16)
            def headnorm_rope(src, g_rep, tix, dstT, tcols):
                s3 = src.v(lambda a: a.rearrange("p (h d) -> p h d", h=8))
                kb.tt("pool", sq768, src, src, ALU.mult)
                kb.red(ssh[:, 0:8], sq768.v(lambda a: a.rearrange("p (h d) -> p h d", h=8)))
                rstd_of(ssh[:, 8:16], ssh[:, 0:8], 96.0, 8)
                n3 = nrm.v(lambda a: a.rearrange("p (h d) -> p h d", h=8))
                kb.tt("dve", n3, s3, ssh[:, 8:16].v(lambda a: a.unsqueeze(2).to_broadcast([128, 8, 96])), ALU.mult)
                kb.tt("dve", n3, n3, g_rep.v(lambda a: a.unsqueeze(1).to_broadcast([128, 8, 96])), ALU.mult)
                b3 = qb.v(lambda a: a.rearrange("p (h d) -> p h d", h=8))
                kb.copy("pool", b3[:, :, 0:64], n3[:, :, 0:64])
                cs = ropec3[:, tix, :].v(lambda a: a.unsqueeze(1).to_broadcast([128, 8, 16]))
                sn = ropes3[:, tix, :].v(lambda a: a.unsqueeze(1).to_broadcast([128, 8, 16]))
                x1_, x2_ = n3[:, :, 64:80], n3[:, :, 80:96]
                kb.tt("dve", rtmp, x1_, cs, ALU.mult)
                kb.tt("dve", rtmp2, x2_, sn, ALU.mult)
                kb.tt("dve", b3[:, :, 64:80], rtmp, rtmp2, ALU.subtract)
                kb.tt("dve", rtmp, x1_, sn, ALU.mult)
                kb.tt("dve", rtmp2, x2_, cs, ALU.mult)
                kb.tt("dve", b3[:, :, 80:96], rtmp, rtmp2, ALU.add)
                p = psrot.next()
                pb = psb(p)
                for h in range(8):
                    kb.tr(pb[0:96, h * 128:(h + 1) * 128], qb[:, h * 96:(h + 1) * 96], identb)
                kb.copy("act", dstT[:, :, tcols], pb[0:96, :].v(lambda a: a.rearrange("p (h t) -> p h t", h=8)))

            for blk in range(NB):
                bsl = slice(blk * 512, (blk + 1) * 512)
                for tt_ in range(4):
                    tix = blk * 4 + tt_
                    xt = xpool.next()
                    kb.dma(xt, DR(xsrc[tix * 128:(tix + 1) * 128, :]))
                    norm_transpose(xt, hT, slice(tt_ * 128, (tt_ + 1) * 128), (junk, ssr.next(), hb))
                if stop == 'A3':
                    kb.barrier()
                    return nc
                fm = [(0, 128, ("cq", 0)), (128, 128, ("cq", 1)), (256, 128, ("cq", 2)), (384, 128, ("ckv", 0)), (512, 128, ("ckv", 1)),
                      (640, 128, ("gq", 0)), (768, 128, ("gq", 1)), (896, 128, ("gk", 0)), (1024, 128, ("gk", 1)), (1152, 16, ("glr", 0))]
                import os
                for (c0, M, (kind, ci)) in fm[:int(os.environ.get('FMN', '99'))]:
                    p = psrot.next()
                    for kc in range(8):
                        kb.mm(p[0:M, :], WAf[:, kc, c0:c0 + M], hT[:, kc, :], kc == 0, kc == 7)
                    if kind == "cq":
                        kb.copy("dve", cqT[:, ci, :], p)
                        kb.tt("pool", sqT[:, ci, :], cqT[:, ci, :], cqT[:, ci, :], ALU.mult)
                    elif kind == "ckv":
                        kb.copy("dve", ckvT[:, ci, :], p)
                        kb.tt("pool", sqT[:, 3 + ci, :], ckvT[:, ci, :], ckvT[:, ci, :], ALU.mult)
                    elif kind == "gq":
                        kb.copy("act", gqT[:, ci, :], p)
                    elif kind == "gk":
                        kb.copy("dve", gkT[:, ci, :], p)
                    else:
                        kb.copy("act", glrT[0:16, :], p[0:16, :])

                if stop == 'A4':
                    kb.barrier()
                    return nc
                for tt_ in range(4):
                    tix = blk * 4 + tt_
                    tsl = slice(tt_ * 128, (tt_ + 1) * 128)
                    pkk, pgv, pgr = psrot.next(), psrot.next(), psrot.next()
                    for (c0, n, p) in ((0, 288, pkk), (288, 512, pgv), (800, 512, pgr)):
                        for kc in range(8):
                            kb.mm(p[:, 0:n], hT[:, kc, tsl], WAt[:, kc, c0:c0 + n], kc == 0, kc == 7)
                    kb.copy("act", kk, pkk[:, 0:288])
                    kb.copy("dve", gv, pgv)
                    kb.act(sr, pgr, AF.Tanh, scale=0.5)
                    kb.stt(sr, sr, 1.0, pgr, ALU.add, ALU.mult)
                    if stop == 'A5':
                        kb.barrier()
                        return nc
                    pst = psrot.next()
                    for c in range(3):
                        kb.mm(pst[:, 0:1], sqT[:, c, tsl], onesb[:, 0:1], c == 0, c == 2)
                    for c in range(2):
                        kb.mm(pst[:, 1:2], sqT[:, 3 + c, tsl], onesb[:, 0:1], c == 0, c == 1)
                    kb.copy("dve", st2[:, 0:2], pst[:, 0:2])
                    rstd_of(st2[:, 2:3], st2[:, 0:1], 384.0, 1)
                    rstd_of(st2[:, 3:4], st2[:, 1:2], 256.0, 1)
                    pq0, pq1 = psrot.next(), psrot.next()
                    for (p, n0, n1) in ((pq0, 0, 512), (pq1, 512, 768)):
                        for c in range(3):
                            kb.mm(p[:, 0:n1 - n0], cqT[:, c, tsl], Wuq[:, c, n0:n1], c == 0, c == 2)
                    kb.act(q32[:, 0:512], pq0, AF.Copy, scale=st2[:, 2:3])
                    kb.act(q32[:, 512:768], pq1[:, 0:256], AF.Copy, scale=st2[:, 2:3])
                    headnorm_rope(q32, gq_s, tix, qTb, tsl)
                    if stop == 'A6':
                        kb.barrier()
                        return nc
                    pk0, pk1 = psrot.next(), psrot.next()
                    for (p, n0) in ((pk0, 0), (pk1, 512)):
                        for c in range(2):
                            kb.mm(p, ckvT[:, c, tsl], Wukv[:, c, n0:n0 + 512], c == 0, c == 1)
                    kb.act(kv32[:, 0:512], pk0, AF.Copy, scale=st2[:, 3:4])
                    kb.act(kv32[:, 512:1024], pk1, AF.Copy, scale=st2[:, 3:4])
                    kv3 = kv32.v(lambda a: a.rearrange("p (h d) -> p h d", h=8))
                    k3 = k96.v(lambda a: a.rearrange("p (h d) -> p h d", h=8))
                    kb.copy("pool", k3[:, :, 0:64], kv3[:, :, 0:64])
                    kb.copy("pool", k3[:, :, 64:96], kk[:, 0:32].v(lambda a: a.unsqueeze(1).to_broadcast([128, 8, 32])))
                    kb.copy("pool", vb[:, :, tt_, :], kv3[:, :, 64:128])
                    headnorm_rope(k96, gk_r, tix, kTb, tsl)

                    if stop == 'A7':
                        kb.barrier()
                        return nc
                    pl = psrot.next()
                    kb.mm(pl[:, 0:256], glrT[0:17, tsl], wga[0:17, :], True, True)
                    kb.act(lsp, pl[:, 0:256], AF.Exp, scale=-1.0)
                    kb.act(lsp, lsp, AF.Ln, bias=1.0)
                    pbc = psrot.next()
                    for hc in range(2):
                        kb.mm(pbc[:, hc * 128:(hc + 1) * 128], lsp[:, hc * 128:(hc + 1) * 128], triUs, True, True)
                    kb.mm(pbc[:, 256:512], triLs, lsp, True, True)
                    kb.act(eq, pbc[:, 0:256], AF.Exp)
                    kb.act(ek, pbc[:, 0:256], AF.Exp, scale=-1.0)
                    kb.act(eend, pbc[:, 256:512], AF.Exp)
                    for hc in range(2):
                        kb.stt(qtT[:, hc, :], gqT[:, hc, tsl], 0.125, eq[:, hc * 128:(hc + 1) * 128], ALU.mult, ALU.mult)
                        kb.tt("pool", ktT[:, hc, :], gkT[:, hc, tsl], ek[:, hc * 128:(hc + 1) * 128], ALU.mult)
                    kb.tt("pool", kend, kk[:, 32:288], eend, ALU.mult)
                    if stop == 'A8':
                        kb.barrier()
                        return nc
                    po = psrot.next()
                    pds = psrot.next()
                    for h in range(4):
                        hc, hb_ = h // 2, (h % 2) * 64
                        pa = psrot.next()
                        kb.mm(pa[:, 0:128], ktT[hb_:hb_ + 64, hc, :], qtT[hb_:hb_ + 64, hc, :], True, True)
                        kb.tt("dve", Am, pa[:, 0:128], triU, ALU.mult)
                        kb.mm(po[:, h * 128:(h + 1) * 128], Am, gv[:, h * 128:(h + 1) * 128], True, False)
                        kb.mm(po[:, h * 128:(h + 1) * 128], qtT[hb_:hb_ + 64, hc, :], Sst[hc][hb_:hb_ + 64, :], False, True)
                        kb.mm(pds[hb_:hb_ + 64, hc * 128:(hc + 1) * 128], kend[:, h * 64:(h + 1) * 64], gv[:, h * 128:(h + 1) * 128], True, True)
                    for hc in range(2):
                        kb.stt(Sst[hc], Sst[hc], eq[:, hc * 128 + 127:hc * 128 + 128], pds[:, hc * 128:(hc + 1) * 128], ALU.mult, ALU.add)
                    if stop == 'A9':
                        kb.barrier()
                        return nc
                    kb.copy("act", o32, po)
                    kb.tt("pool", osq, o32, o32, ALU.mult)
                    kb.red(ssh[:, 0:4], osq.v(lambda a: a.rearrange("p (h e) -> p h e", h=4)))
                    rstd_of(ssh[:, 8:12], ssh[:, 0:4], 128.0, 4)
                    o3 = o32.v(lambda a: a.rearrange("p (h e) -> p h e", h=4))
                    kb.tt("dve", o3, o3, ssh[:, 8:12].v(lambda a: a.unsqueeze(2).to_broadcast([128, 4, 128])), ALU.mult)
                    kb.tt("dve", o3, o3, og_h.v(lambda a: a.unsqueeze(1).to_broadcast([128, 4, 128])), ALU.mult)
                    kb.tt("dve", ob, o32, sr, ALU.mult)
                    p = psrot.next()
                    pb = psb(p)
                    for c in range(4):
                        kb.tr(pb[:, c * 128:(c + 1) * 128], ob[:, c * 128:(c + 1) * 128], identb)
                    kb.copy("act", obTb[:, :, tsl], pb[:, 0:512].v(lambda a: a.rearrange("p (c t) -> p c t", c=4)))

                if stop == 'A10':
                    kb.barrier()
                    return nc
                kb.dma(DR(QT[:, :, bsl].rearrange("h d t -> d h t")), qTb)
                kb.dma(DR(KT[:, :, bsl].rearrange("h d t -> d h t")), kTb)
                kb.dma(DR(Vd.rearrange("h p t d -> p h t d")[:, :, blk * 4:(blk + 1) * 4, :]), vb)
                kb.dma(DR(obT[:, bsl].rearrange("(c p) t -> p c t", p=128)), obTb)
            kb.barrier()

        if stop == 'A':
            return nc
        with ExitStack() as es:
            def A(name, shape, dt=F32, es=es):
                return TV(es.enter_context(nc.sbuf_tensor("S%d_%s" % (l, name), list(shape), dt)).ap())

            srot = Rot(PS[5:8]) if os.environ.get("SROT") != "all" else psrot
            pkt = A("pk", [128, PKW])
            kb.dma(pkt, DR(pk[l]))
            WAs = A("WAs", [128, 8, 512], BF16)
            load_w(WAs, 0, w_in[l], 2224, 2736, 8, pkt[:, PK_G1:PK_G1 + 8])
            Wglu = A("Wglu", [128, 4, 512], BF16)
            load_w(Wglu, 0, w_glu[l], 0, 512, 4, None)
            BTr = A("BTr", [128, 16, 128], BF16)
            BTi = A("BTi", [128, 16, 128], BF16)
            CTr = A("CTr", [128, 16, 128], BF16)
            CTi = A("CTi", [128, 16, 128], BF16)
            for dst, src, sh in ((BTr, btd[l, 0], (16, 128)), (BTi, btd[l, 1], (16, 128)), (CTr, ctd[l, 0], (16, 128)), (CTi, ctd[l, 1], (16, 128))):
                st = stage.next()
                sv = st.v(lambda a: a.rearrange("p k n -> p (k n)"))[:, 0:sh[0] * sh[1]].v(lambda a: a.rearrange("p (k n) -> p k n", k=sh[0]))
                kb.dma(sv, DR(src))
                kb.copy("dve", dst, sv)

            lre = A("lre", [128, 16])
            kb.ts("dve", lre, pkt[:, PK_LRE:PK_LRE + 16], -1e-4, ALU.min)
            lim = pkt[:, PK_LIM:PK_LIM + 16]
            dtt = A("dtt", [128, 16])
            kb.act(dtt, pkt[:, PK_LDT:PK_LDT + 16], AF.Exp)
            th = A("th", [128, 16])
            kb.tt("dve", th, lim, dtt, ALU.mult)
            rr_ = A("rr", [128, 16])
            kb.tt("dve", rr_, lre, dtt, ALU.mult)
            kb.act(rr_, rr_, AF.Exp)
            sth = A("sth", [128, 16])
            cth = A("cth", [128, 16])
            sincos(A, th, 16, sth, cth, "t")
            nre = A("nre", [128, 16])
            nim = A("nim", [128, 16])
            kb.tt("dve", nre, rr_, cth, ALU.mult)
            kb.ts("dve", nre, nre, -1.0, ALU.add)
            kb.tt("dve", nim, rr_, sth, ALU.mult)
            den = A("den", [128, 16])
            tmp16 = A("tmp16", [128, 16])
            kb.tt("dve", den, lre, lre, ALU.mult)
            kb.tt("dve", tmp16, lim, lim, ALU.mult)
            kb.tt("dve", den, den, tmp16, ALU.add)
            kb.recip(den, den)
            cre = A("cre", [128, 16])
            cim = A("cim", [128, 16])
            kb.tt("dve", cre, nre, lre, ALU.mult)
            kb.tt("dve", tmp16, nim, lim, ALU.mult)
            kb.tt("dve", cre, cre, tmp16, ALU.add)
            kb.tt("dve", cre, cre, den, ALU.mult)
            kb.tt("dve", cim, nim, lre, ALU.mult)
            kb.tt("dve", tmp16, nre, lim, ALU.mult)
            kb.tt("dve", cim, cim, tmp16, ALU.subtract)
            kb.tt("dve", cim, cim, den, ALU.mult)
            E2c = A("E2c", [128, 2048])
            E2s = A("E2s", [128, 2048])
            E1r = A("E1r", [128, 2048])
            E1i = A("E1i", [128, 2048])
            Rz = A("Rz", [128, 2048])
            es2 = ExitStack()
            tang = A("tang", [128, 2048], es=es2)
            tau = cst[:, C_TAU:C_TAU + 128]
            for j in range(16):
                kb.ts("dve", tang[:, j * 128:(j + 1) * 128], tau, th[:, j:j + 1], ALU.mult)
            sincos(lambda n_, s_, d_: A(n_, s_, d_, es=es2), tang, 2048, E2s, E2c, "T")
            for j in range(16):
                sl = slice(j * 128, (j + 1) * 128)
                kb.ts("dve", tang[:, sl], E2s[:, sl], cim[:, j:j + 1], ALU.mult)
                kb.stt(E1r[:, sl], E2c[:, sl], cre[:, j:j + 1], tang[:, sl], ALU.mult, ALU.add)
                kb.ts("dve", tang[:, sl], E2s[:, sl], cre[:, j:j + 1], ALU.mult)
                kb.stt(E1i[:, sl], E2c[:, sl], cim[:, j:j + 1], tang[:, sl], ALU.mult, ALU.subtract)
                kb.ts("dve", Rz[:, sl], cst[:, C_TRIU + 127:C_TRIU + 128].v(lambda a: a.to_broadcast([128, 128])), rr_[:, j:j + 1], ALU.mult)
            Rz3 = Rz.v(lambda a: a.rearrange("p (j t) -> p j t", t=128))
            kb.memset("dve", Rz3[:, :, 0:1], 0.0)
            kb.barrier()
            es2.close()
            car_r = A("carr", [128, 16])
            car_i = A("cari", [128, 16])
            kb.memset("dve", car_r, 0.0)
            kb.memset("dve", car_i, 0.0)
            if stop == 'S1':
                kb.barrier()
                return nc
            xpool = Rot([A("x%d" % i, [128, 1024]) for i in range(2)])
            junk = A("junk", [128, 1024], BF16)
            ssr = Rot([A("ssr%d" % i, [128, 2]) for i in range(2)])
            hb = A("hb", [128, 1024], BF16)
            hT = A("hT", [128, 8, 512], BF16)
            uT = A("uT", [128, 4, 512])
            uTb = A("uTb", [128, 4, 512], BF16)
            t1 = A("t1", [128, 1024])
            t2 = A("t2", [128, 1024])
            btr = A("btr", [128, 1024])
            bti = A("bti", [128, 1024])
            xsr = A("xsr", [128, 1024])
            xsi = A("xsi", [128, 1024])
            xbr = A("xbr", [128, 1024], BF16)
            xbi = A("xbi", [128, 1024], BF16)
            rc = A("rc", [128, 16])
            ctmp = A("ctmp", [128, 32])
            y32 = A("y32", [128, 4, 512])
            yb = A("yb", [128, 4, 512], BF16)
            ocTb = A("ocTb", [128, 4, 512], BF16)
            g3 = A("g3", [128, 512])
            d5 = pkt[:, PK_D5:PK_D5 + 4]
            bglu = A("bgluh", [128, 4])
            kb.ts("dve", bglu, pkt[:, PK_BGLU:PK_BGLU + 4], 0.5, ALU.mult)

            for blk in range(NB):
                bsl = slice(blk * 512, (blk + 1) * 512)
                for tt_ in range(4):
                    tix = blk * 4 + tt_
                    xt = xpool.next()
                    kb.dma(xt, DR(xsrc[tix * 128:(tix + 1) * 128, :]))
                    norm_transpose(xt, hT, slice(tt_ * 128, (tt_ + 1) * 128), (junk, ssr.next(), hb), srot)
                if stop == 'S15':
                    kb.barrier()
                    return nc
                for ci in range(4):
                    p = srot.next()
                    for kc in range(8):
                        kb.mm(p, WAs[:, kc, ci * 128:(ci + 1) * 128], hT[:, kc, :], kc == 0, kc == 7)
                    if stop == 'S16':
                        kb.barrier()
                        return nc
                    kb.copy("dve", uT[:, ci, :], p)
                    kb.copy("pool", uTb[:, ci, :], uT[:, ci, :])
                if stop == 'S2':
                    kb.barrier()
                    return nc
                for tt_ in range(4):
                    tsl = slice(tt_ * 128, (tt_ + 1) * 128)
                    py = [PS[0]]
                    for half in range(2):
                        pbr, pbi, pbr2, pbi2 = PS[1], PS[2], PS[3], PS[4]
                        banks_r, banks_i = (pbr, pbr2), (pbi, pbi2)
                        for jj in range(8):
                            j = half * 8 + jj
                            c, q_ = j // 4, j % 4
                            rs = slice(32 * q_, 32 * q_ + 32)
                            kb.mm(banks_r[jj // 4][:, (jj % 4) * 128:(jj % 4 + 1) * 128], BTr[:, j, :], uTb[:, c, tsl], True, True)
                            kb.mm(banks_i[jj // 4][:, (jj % 4) * 128:(jj % 4 + 1) * 128], BTi[:, j, :], uTb[:, c, tsl], True, True)
                        hs = slice(half * 1024, (half + 1) * 1024)
                        for g_ in range(2):
                            gs = slice(g_ * 512, (g_ + 1) * 512)
                            hg = slice(half * 1024 + g_ * 512, half * 1024 + (g_ + 1) * 512)
                            kb.tt("dve", t1[:, gs], banks_r[g_], E1r[:, hg], ALU.mult)
                            kb.tt("dve", t2[:, gs], banks_i[g_], E1i[:, hg], ALU.mult)
                            kb.tt("dve", btr[:, gs], t1[:, gs], t2[:, gs], ALU.subtract)
                            kb.tt("dve", t1[:, gs], banks_i[g_], E1r[:, hg], ALU.mult)
                            kb.tt("dve", t2[:, gs], banks_r[g_], E1i[:, hg], ALU.mult)
                            kb.tt("pool", bti[:, gs], t1[:, gs], t2[:, gs], ALU.add)
                        if stop == 'S3':
                            kb.barrier()
                            return nc
                        js = slice(half * 8, half * 8 + 8)
                        kb.tt("dve", rc[:, 0:8], rr_[:, js], car_r[:, js], ALU.mult)
                        kb.tt("dve", rc[:, 8:16], rr_[:, js], car_i[:, js], ALU.mult)
                        b3r = btr.v(lambda a: a.rearrange("p (j t) -> p j t", t=128))
                        b3i = bti.v(lambda a: a.rearrange("p (j t) -> p j t", t=128))
                        kb.tt("dve", b3r[:, :, 0], b3r[:, :, 0], rc[:, 0:8], ALU.add)
                        kb.tt("dve", b3i[:, :, 0], b3i[:, :, 0], rc[:, 8:16], ALU.add)
                        kb.scan(xsr, Rz[:, hs], btr, 0.0)
                        kb.scan(xsi, Rz[:, hs], bti, 0.0)
                        if stop == 'S4':
                            kb.barrier()
                            return nc
                        x3r = xsr.v(lambda a: a.rearrange("p (j t) -> p j t", t=128))
                        x3i = xsi.v(lambda a: a.rearrange("p (j t) -> p j t", t=128))
                        e3c = E2c[:, hs].v(lambda a: a.rearrange("p (j t) -> p j t", t=128))
                        e3s = E2s[:, hs].v(lambda a: a.rearrange("p (j t) -> p j t", t=128))
                        kb.tt("dve", ctmp[:, 0:8], x3r[:, :, 127], e3c[:, :, 127], ALU.mult)
                        kb.tt("dve", ctmp[:, 8:16], x3i[:, :, 127], e3s[:, :, 127], ALU.mult)
                        kb.tt("dve", ctmp[:, 16:24], x3r[:, :, 127], e3s[:, :, 127], ALU.mult)
                        kb.tt("dve", ctmp[:, 24:32], x3i[:, :, 127], e3c[:, :, 127], ALU.mult)
                        kb.tt("dve", car_r[:, js], ctmp[:, 0:8], ctmp[:, 8:16], ALU.subtract)
                        kb.tt("dve", car_i[:, js], ctmp[:, 16:24], ctmp[:, 24:32], ALU.add)
                        if stop == 'S5':
                            kb.barrier()
                            return nc
                        kb.tt("pool", t1, xsr, E2c[:, hs], ALU.mult)
                        kb.tt("pool", t2, xsi, E2s[:, hs], ALU.mult)
                        kb.tt("dve", xbr, t1, t2, ALU.subtract)
                        kb.tt("dve", t1, xsi, E2c[:, hs], ALU.mult)
                        kb.tt("dve", t2, xsr, E2s[:, hs], ALU.mult)
                        kb.stt(xbi, t1, -1.0, t2, ALU.mult, ALU.subtract)
                        for jj in range(8):
                            j = half * 8 + jj
                            c, q_ = j // 4, j % 4
                            rs = slice(32 * q_, 32 * q_ + 32)
                            kb.mm(py[0][:, c * 128:(c + 1) * 128], CTr[:, j, :], xbr[:, jj * 128:(jj + 1) * 128], q_ == 0, False)
                            kb.mm(py[0][:, c * 128:(c + 1) * 128], CTi[:, j, :], xbi[:, jj * 128:(jj + 1) * 128], False, q_ == 3)
                    if stop == 'S6':
                        kb.barrier()
                        return nc
                    for c in range(4):
                        kb.stt(y32[:, c, tsl], uT[:, c, tsl], d5[:, c:c + 1], py[0][:, c * 128:(c + 1) * 128], ALU.mult, ALU.add)

                if stop == 'S7':
                    kb.barrier()
                    return nc
                for c in range(4):
                    yc = y32[:, c, :]
                    kb.tt("pool", g3, yc, yc, ALU.mult)
                    kb.ts("dve", g3, g3, 0.044715, ALU.mult, 1.0, ALU.add)
                    kb.tt("pool", g3, g3, yc, ALU.mult)
                    kb.act(g3, g3, AF.Tanh, scale=0.7978845608028654)
                    kb.stt(g3, g3, 1.0, yc, ALU.add, ALU.mult)
                    kb.ts("dve", yc, g3, 0.5, ALU.mult)
                    kb.copy("pool", yb[:, c, :], yc)
                if stop == 'S8':
                    kb.barrier()
                    return nc
                for mc in range(4):
                    p = srot.next()
                    for c in range(4):
                        kb.mm(p, Wglu[:, c, mc * 128:(mc + 1) * 128], yb[:, c, :], c == 0, c == 3)
                    kb.act(g3, p, AF.Tanh, bias=bglu[:, mc:mc + 1], scale=0.5)
                    kb.stt(g3, g3, 1.0, y32[:, mc, :], ALU.add, ALU.mult)
                    kb.ts("dve", ocTb[:, mc, :], g3, 0.5, ALU.mult)
                kb.dma(DR(ocT[:, bsl].rearrange("(c p) t -> p c t", p=128)), ocTb)
            kb.barrier()


        if stop == 'S':
            return nc
        with ExitStack() as es:
            def Bt(name, shape, dt=F32):
                return TV(es.enter_context(nc.sbuf_tensor("B%d_%s" % (l, name), list(shape), dt)).ap())

            KTh = Rot([Bt("KT%d" % i, [96, L], BF16) for i in range(2)])
            Vh = Rot([Bt("V%d" % i, [128, NT, 65], BF16) for i in range(2)])
            Vraw = Rot([Bt("Vr%d" % i, [128, NT, 64], BF16) for i in range(2)])
            QTh = Rot([Bt("QT%d" % i, [96, L], BF16) for i in range(2)])
            Pt = Rot([Bt("P%d" % i, [128, 512], BF16) for i in range(3)])
            orec = Bt("orec", [128, 4])
            onb = Rot([Bt("onb%d" % i, [128, 4, 64], BF16) for i in range(2)])
            oTs = Rot([Bt("oTs%d" % i, [64, 512], BF16) for i in range(2)])
            for V_ in Vh.items:
                kb.memset("dve", V_[:, :, 64:65], 1.0)
            SB_ = Rot(PS[0:3])
            OB_ = PS[3:7]
            for h in range(8):
                Kt_, V_, Q_ = KTh.next(), Vh.next(), QTh.next()
                kb.dma(Kt_, DR(KT[h]))
                kb.dma(Q_, DR(QT[h]))
                Vr_ = Vraw.next()
                kb.dma(Vr_, DR(Vd[h]))
                kb.copy("pool", V_[:, :, 0:64], Vr_)
                for qb_ in range(NB):
                    nk = 4 * (qb_ + 1)
                    for kt in range(nk):
                        j = kt - 4 * qb_
                        q0 = max(j, 0) * 128
                        ps_ = SB_.next()
                        kb.mm(ps_[:, q0:512], Kt_[:, kt * 128:(kt + 1) * 128], Q_[:, qb_ * 512 + q0:(qb_ + 1) * 512], True, True)
                        P_ = Pt.next()
                        kb.act(P_[:, q0:512], ps_[:, q0:512], AF.Exp)
                        if j >= 0:
                            kb.tt("dve", P_[:, j * 128:(j + 1) * 128], P_[:, j * 128:(j + 1) * 128], trimb, ALU.mult)
                        for qi in range(max(j, 0), 4):
                            last = (kt == 4 * qb_ + qi)
                            kb.mm(OB_[qi][:, 0:65], P_[:, qi * 128:(qi + 1) * 128], V_[:, kt, :], kt == 0, last)
                    on_ = onb.next()
                    for qi in range(4):
                        kb.recip(orec[:, qi:qi + 1], OB_[qi][:, 64:65])
                        kb.ts("dve", on_[:, qi, :], OB_[qi][:, 0:64], orec[:, qi:qi + 1], ALU.mult)
                    pT = PS[7]
                    pTb = psb(pT)
                    for qi in range(4):
                        kb.tr(pTb[0:64, qi * 128:(qi + 1) * 128], on_[:, qi, :], identb)
                    oT_ = oTs.next()
                    kb.copy("act", oT_, pTb[0:64, 0:512])
                    kb.dma(DR(oaT[h * 64:(h + 1) * 64, qb_ * 512:(qb_ + 1) * 512]), oT_)
            kb.barrier()

        if stop == 'B':
            return nc
        with ExitStack() as es:
            def Ct(name, shape, dt=F32):
                return TV(es.enter_context(nc.sbuf_tensor("C%d_%s" % (l, name), list(shape), dt)).ap())

            pkt = Ct("pk", [128, PKW])
            kb.dma(pkt, DR(pk[l]))
            gbh = Ct("gbh", [128, 24])
            kb.ts("dve", gbh, pkt[:, PK_GB:PK_GB + 24], 0.5, ALU.mult)
            Wg = Ct("Wg", [128, 8, 3072], BF16)
            load_w(Wg, 0, w_in[l], 2736, 5808, 8, pkt[:, PK_G1:PK_G1 + 8])
            Wb = [Ct("Wb%d" % i, [128, 4, 1024], BF16) for i in range(3)]
            for i in range(3):
                load_w(Wb[i], 0, w_br[i][l], 0, 1024, 4, None)
            Wo = Ct("Wo", [128, 8, 1024], BF16)
            load_w(Wo, 0, w_out[l], 0, 1024, 8, None)
            xt4 = [Ct("x%d" % i, [128, 1024]) for i in range(4)]
            junk = Ct("junk", [128, 1024], BF16)
            ssr = Rot([Ct("ssr%d" % i, [128, 2]) for i in range(2)])
            hb = Ct("hb", [128, 1024], BF16)
            hT = Ct("hT", [128, 8, 512], BF16)
            oin = [Ct("oin%d" % i, [128, 4, 512], BF16) for i in range(3)]
            gsb = Rot([Ct("g%d" % i, [128, 512]) for i in range(3)])
            macc = Ct("macc", [128, 512])
            mtmp = Ct("mtmp", [128, 512])
            mT = Ct("mT", [128, 8, 512], BF16)
            xo = Rot([Ct("xo%d" % i, [128, 1024]) for i in range(2)])
            for blk in range(NB):
                bsl = slice(blk * 512, (blk + 1) * 512)
                for i, src in enumerate((oaT, obT, ocT)):
                    kb.dma(oin[i], DR(src[:, bsl].rearrange("(c p) t -> p c t", p=128)))
                for tt_ in range(4):
                    tix = blk * 4 + tt_
                    kb.dma(xt4[tt_], DR(xsrc[tix * 128:(tix + 1) * 128, :]))
                    norm_transpose(xt4[tt_], hT, slice(tt_ * 128, (tt_ + 1) * 128), (junk, ssr.next(), hb))
                for fc in range(8):
                    for b in range(3):
                        pg = psrot.next()
                        gc = b * 8 + fc
                        for kc in range(8):
                            kb.mm(pg, Wg[:, kc, gc * 128:(gc + 1) * 128], hT[:, kc, :], kc == 0, kc == 7)
                        g_ = gsb.next()
                        kb.act(g_, pg, AF.Tanh, bias=gbh[:, gc:gc + 1], scale=0.5)
                        pp = psrot.next()
                        for c in range(4):
                            kb.mm(pp, Wb[b][:, c, fc * 128:(fc + 1) * 128], oin[b][:, c, :], c == 0, c == 3)
                        if b == 0:
                            kb.stt(macc, g_, 1.0, pp, ALU.add, ALU.mult)
                        else:
                            kb.stt(mtmp, g_, 1.0, pp, ALU.add, ALU.mult)
                            kb.tt("pool", macc, macc, mtmp, ALU.add)
                    kb.ts("dve", mT[:, fc, :], macc, 0.5, ALU.mult)
                for tt_ in range(4):
                    tix = blk * 4 + tt_
                    tsl = slice(tt_ * 128, (tt_ + 1) * 128)
                    xo_ = xo.next()
                    for nh in range(2):
                        p = psrot.next()
                        for kc in range(8):
                            kb.mm(p, mT[:, kc, tsl], Wo[:, kc, nh * 512:(nh + 1) * 512], kc == 0, kc == 7)
                        kb.tt("dve", xo_[:, nh * 512:(nh + 1) * 512], p, xt4[tt_][:, nh * 512:(nh + 1) * 512], ALU.add)
                    kb.dma(DR(x1d[tix * 128:(tix + 1) * 128, :]), xo_)
            kb.barrier()

        if stop == 'C':
            return nc
        with ExitStack() as es:
            def Dt(name, shape, dt=F32):
                return TV(es.enter_context(nc.sbuf_tensor("D%d_%s" % (l, name), list(shape), dt)).ap())

            pkt = Dt("pk", [128, PKW])
            kb.dma(pkt, DR(pk[l]))
            W1 = Dt("W1", [128, 8, 4096], BF16)
            load_w(W1, 0, w_ff1[l], 0, 4096, 8, pkt[:, PK_G2:PK_G2 + 8])
            W2 = Dt("W2", [128, 32, 1024], BF16)
            for k0 in range(0, 32, 8):
                load_w(W2[:, k0:k0 + 8, :], 0, w_ff2[l][k0 * 128:(k0 + 8) * 128, :], 0, 1024, 8, None)
            xt4 = [Dt("x%d" % i, [128, 1024]) for i in range(2)]
            junk = Dt("junk", [128, 1024], BF16)
            ssr = Rot([Dt("ssr%d" % i, [128, 2]) for i in range(2)])
            hb = Dt("hb", [128, 1024], BF16)
            hT = Dt("hT", [128, 8, 256], BF16)
            uT = Dt("uT", [128, 32, 256], BF16)
            rl = Rot([Dt("rl%d" % i, [128, 256]) for i in range(2)])
            for blk in range(L // 256):
                for tt_ in range(2):
                    tix = blk * 2 + tt_
                    kb.dma(xt4[tt_], DR(x1d[tix * 128:(tix + 1) * 128, :]))
                    norm_transpose(xt4[tt_], hT, slice(tt_ * 128, (tt_ + 1) * 128), (junk, ssr.next(), hb))
                for fc in range(32):
                    p = psrot.next()
                    for kc in range(8):
                        kb.mm(p[:, 0:256], W1[:, kc, fc * 128:(fc + 1) * 128], hT[:, kc, :], kc == 0, kc == 7)
                    r_ = rl.next()
                    kb.act(r_, p[:, 0:256], AF.Relu)
                    kb.tt("pool" if fc % 2 else "dve", uT[:, fc, :], r_, r_, ALU.mult)
                for tt_ in range(2):
                    tix = blk * 2 + tt_
                    tsl = slice(tt_ * 128, (tt_ + 1) * 128)
                    xo_ = xt4[tt_]
                    for nh in range(2):
                        p = psrot.next()
                        for fc in range(32):
                            kb.mm(p, uT[:, fc, tsl], W2[:, fc, nh * 512:(nh + 1) * 512], fc == 0, fc == 31)
                        kb.tt("dve", xo_[:, nh * 512:(nh + 1) * 512], p, xt4[tt_][:, nh * 512:(nh + 1) * 512], ALU.add)
                    kb.dma(DR(xdst[tix * 128:(tix + 1) * 128, :]), xo_)
            kb.barrier()
    return nc


def _host_params(inp, L):
    f = lambda a: np.ascontiguousarray(np.asarray(a, dtype=np.float32))
    pk = np.zeros((DEPTH, 128, PKW), np.float32)
    rowp = np.zeros((DEPTH, 320), np.float32)
    wgp = np.zeros((DEPTH, 17, 256), np.float32)
    bt = np.zeros((DEPTH, 2, 128, 16, 128), np.float32)
    ct = np.zeros((DEPTH, 2, 128, 16, 128), np.float32)
    for l in range(DEPTH):
        pk[l, :, PK_G1:PK_G1 + 8] = f(inp["norm1_g"])[l].reshape(8, 128).T
        pk[l, :, PK_G2:PK_G2 + 8] = f(inp["norm2_g"])[l].reshape(8, 128).T
        pk[l, :, PK_QNG:PK_QNG + 3] = f(inp["mla_q_norm_g"])[l].reshape(3, 128).T
        pk[l, :, PK_KVNG:PK_KVNG + 2] = f(inp["mla_kv_norm_g"])[l].reshape(2, 128).T

        def st16(a):
            return a.reshape(16, 2, 64).transpose(1, 2, 0).reshape(128, 16)

        pk[l, :, PK_LRE:PK_LRE + 16] = st16(f(inp["s5_lam_re"])[l])
        pk[l, :, PK_LIM:PK_LIM + 16] = st16(f(inp["s5_lam_im"])[l])
        pk[l, :, PK_LDT:PK_LDT + 16] = st16(np.repeat(f(inp["s5_log_dt"])[l][:, None], 64, axis=1))
        pk[l, :, PK_D5:PK_D5 + 4] = f(inp["s5_d"])[l].reshape(4, 128).T
        pk[l, :, PK_BGLU:PK_BGLU + 4] = f(inp["s5_b_glu"])[l].reshape(4, 128).T
        pk[l, :, PK_GB:PK_GB + 24] = f(inp["gate_b"])[l].reshape(24, 128).T
        rowp[l, 0:96] = f(inp["mla_q_head_g"])[l]
        rowp[l, 96:192] = f(inp["mla_k_head_g"])[l]
        rowp[l, 192:320] = f(inp["gla_out_g"])[l]
        wgp[l, 0:16] = f(inp["gla_w_gate"])[l]
        wgp[l, 16] = f(inp["gla_b_gate"])[l]
        for ri, (bk, ck) in enumerate((("s5_b_re", "s5_c_re"), ("s5_b_im", "s5_c_im"))):
            B = f(inp[bk])[l]
            C = f(inp[ck])[l]
            for j in range(16):
                c, q = j // 4, j % 4
                for gl in range(2):
                    g = 2 * j + gl
                    bt[l, ri, 32 * q + 16 * gl:32 * q + 16 * gl + 16, j, 64 * gl:64 * gl + 64] = B[g].T
                    ct[l, ri, 64 * gl:64 * gl + 64, j, 32 * q + 16 * gl:32 * q + 16 * gl + 16] = C[g].T
    NT = L // 128
    consts = np.zeros((128, C_POS + NT), np.float32)
    consts[:, C_ID:C_ID + 128] = np.eye(128, dtype=np.float32)
    consts[:, C_TRIU:C_TRIU + 128] = np.triu(np.ones((128, 128), np.float32))
    consts[:, C_TRIL:C_TRIL + 128] = np.tril(np.ones((128, 128), np.float32), -1)
    consts[:, C_TAU:C_TAU + 128] = np.arange(1, 129, dtype=np.float32)[None, :]
    consts[:, C_INVF:C_INVF + 16] = (10000.0 ** (-np.arange(0, 32, 2, dtype=np.float32) / 32.0)).astype(np.float32)[None, :]
    consts[:, C_POS:C_POS + NT] = (np.arange(NT, dtype=np.float32)[None, :] * 128.0 + np.arange(128, dtype=np.float32)[:, None])
    return dict(pk=pk, rowp=rowp, wg=wgp, bt=bt, ct=ct, consts=consts)


def make_in_maps(inp, L, n_cores):
    f = lambda a: np.ascontiguousarray(np.asarray(a, dtype=np.float32))
    shared = dict(
        w_in=f(inp["w_in"]), w_uq=f(inp["mla_w_uq"]), w_ukv=f(inp["mla_w_ukv"]), w_glu=f(inp["s5_w_glu"]),
        w_br0=f(inp["w_br_mla"]), w_br1=f(inp["w_br_gla"]), w_br2=f(inp["w_br_s5"]), w_out=f(inp["w_out"]),
        w_ff1=f(inp["w_ff1"]), w_ff2=f(inp["w_ff2"]))
    shared.update(_host_params(inp, L))
    x = f(inp["x"])
    nb = x.shape[0]
    maps = []
    for c in range(n_cores):
        m = dict(shared)
        m["x"] = np.ascontiguousarray(x[c % nb, :L])
        maps.append(m)
    return maps


def kernel(**inputs):
    L = SEQ
    nc = build_nc(L)
    maps = make_in_maps(inputs, L, BATCH)
    res = run_bass_kernel_spmd(nc, maps, core_ids=list(range(BATCH)))
    out = np.stack([np.asarray(res.results[b]["out"], dtype=np.float32) for b in range(BATCH)], axis=0)
    return out


def build_nc(L, dbg=False, nlayers=DEPTH, stop=None):
    return build(L, dbg, nlayers, stop)
```

```python
import math
import os
from contextlib import ExitStack
import numpy as np
import concourse.bass as bass
import concourse.mybir as mybir
from concourse.bass_utils import run_bass_kernel_spmd

F32 = mybir.dt.float32
BF16 = mybir.dt.bfloat16
AF = mybir.ActivationFunctionType
ALU = mybir.AluOpType
AX = mybir.AxisListType

D = 1024
DEPTH = 2
SEQ = 8192
BATCH = 4
EPS = 1e-6
MAGIC = 12582912.0
TWO_PI = 2.0 * math.pi

PK_G1, PK_G2, PK_QNG, PK_KVNG, PK_LRE, PK_LIM, PK_LDT, PK_D5, PK_BGLU, PK_GB = 0, 8, 16, 19, 21, 37, 53, 69, 73, 77
PKW = 101
C_ID, C_TRIU, C_TRIL, C_TAU, C_INVF, C_POS = 0, 128, 256, 384, 512, 528


class Res:
    __slots__ = ("w", "r")

    def __init__(self):
        self.w = None
        self.r = {}


class TV:
    def __init__(self, ap, res=None):
        self.ap = ap
        self.res = res if res is not None else Res()

    def __getitem__(self, k):
        return TV(self.ap[k], self.res)

    def v(self, f):
        return TV(f(self.ap), self.res)

    def sub(self, k):
        return TV(self.ap[k], Res())


class KB:
    def __init__(self, nc):
        self.nc = nc
        self.eng = {"pe": nc.tensor, "act": nc.scalar, "dve": nc.vector, "pool": nc.gpsimd, "sp": nc.sync}
        self.sems = {}
        self.cnt = {}
        for e in ("pe", "act", "dve", "pool"):
            self.sems[e] = nc.alloc_semaphore("c_" + e)
            self.cnt[e] = 0
        self.dq = {}
        for q, n in (("sp", 16), ("act", 4), ("pool", 4)):
            keys = []
            for i in range(n):
                k = "d_%s%d" % (q, i)
                self.sems[k] = nc.alloc_semaphore(k)
                self.cnt[k] = 0
                keys.append(k)
            self.dq[q] = [keys, 0]
        self.seen = {e: {} for e in self.eng}
        self.rr = 0

    def _deps(self, reads, writes):
        need = {}

        def add(k, v):
            if need.get(k, 0) < v:
                need[k] = v

        for R in reads:
            if R.w is not None:
                add(*R.w)
        for R in writes:
            if R.w is not None:
                add(*R.w)
            for k, v in R.r.items():
                add(k, v)
        return need

    def _wait(self, e, need):
        for k, v in need.items():
            if e == "pe" and k == "pe":
                continue
            if self.seen[e].get(k, 0) >= v:
                continue
            self.eng[e].wait_ge(self.sems[k], v)
            self.seen[e][k] = v

    def op(self, e, fn, reads, writes):
        reads = [t.res for t in reads]
        writes = [t.res for t in writes]
        self._wait(e, self._deps(reads, writes))
        ins = fn()
        self.cnt[e] += 1
        ins.then_inc(self.sems[e], 1)
        c = self.cnt[e]
        for R in writes:
            R.w = (e, c)
            R.r = {}
        for R in reads:
            R.r[e] = c
        return ins

    def dma(self, out, in_, q="sp"):
        keys, idx = self.dq[q]
        k = keys[idx]
        self.dq[q][1] = (idx + 1) % len(keys)
        need = self._deps([in_.res], [out.res])
        if self.cnt[k] > 0:
            need[k] = max(need.get(k, 0), self.cnt[k])
        self._wait(q, need)
        ins = self.eng[q].dma_start(out=out.ap, in_=in_.ap)
        self.cnt[k] += 16
        ins.then_inc(self.sems[k], 16)
        c = self.cnt[k]
        out.res.w = (k, c)
        out.res.r = {}
        in_.res.r[k] = c

    def barrier(self):
        for e in self.eng:
            for k, v in self.cnt.items():
                if v > 0 and self.seen[e].get(k, 0) < v:
                    self.eng[e].wait_ge(self.sems[k], v)
                    self.seen[e][k] = v

    def _ve(self, e):
        return self.eng[e]

    def tt(self, e, out, a, b, op):
        return self.op(e, lambda: self._ve(e).tensor_tensor(out=out.ap, in0=a.ap, in1=b.ap, op=op), [a, b], [out])

    def ts(self, e, out, a, s1, op0, s2=None, op1=None):
        rd = [a]
        s1a, s2a = s1, s2
        if isinstance(s1, TV):
            rd.append(s1)
            s1a = s1.ap
        if isinstance(s2, TV):
            rd.append(s2)
            s2a = s2.ap
        if op1 is None:
            return self.op(e, lambda: self._ve(e).tensor_scalar(out=out.ap, in0=a.ap, scalar1=s1a, scalar2=None, op0=op0), rd, [out])
        return self.op(e, lambda: self._ve(e).tensor_scalar(out=out.ap, in0=a.ap, scalar1=s1a, scalar2=s2a, op0=op0, op1=op1), rd, [out])

    def stt(self, out, a, sc, b, op0, op1):
        rd = [a, b]
        sca = sc
        if isinstance(sc, TV):
            rd.append(sc)
            sca = sc.ap
        return self.op("dve", lambda: self.nc.vector.scalar_tensor_tensor(out=out.ap, in0=a.ap, scalar=sca, in1=b.ap, op0=op0, op1=op1), rd, [out])

    def copy(self, e, out, a):
        if e == "act":
            return self.op(e, lambda: self.nc.scalar.copy(out=out.ap, in_=a.ap), [a], [out])
        return self.op(e, lambda: self._ve(e).tensor_copy(out=out.ap, in_=a.ap), [a], [out])

    def memset(self, e, out, val):
        return self.op(e, lambda: self._ve(e).memset(out.ap, val), [], [out])

    def act(self, out, a, func, bias=None, scale=None, accum=None):
        rd = [a]
        wr = [out]
        kw = {}
        if bias is not None:
            if isinstance(bias, TV):
                rd.append(bias)
                kw["bias"] = bias.ap
            else:
                kw["bias"] = bias
        if scale is not None:
            if isinstance(scale, TV):
                rd.append(scale)
                kw["scale"] = scale.ap
            else:
                kw["scale"] = scale
        if accum is not None:
            wr.append(accum)
            kw["accum_out"] = accum.ap
        return self.op("act", lambda: self.nc.scalar.activation(out=out.ap, in_=a.ap, func=func, **kw), rd, wr)

    def mm(self, out, lhsT, rhs, start, stop):
        return self.op("pe", lambda: self.nc.tensor.matmul(out.ap, lhsT=lhsT.ap, rhs=rhs.ap, start=start, stop=stop), [lhsT, rhs], [out])

    def tr(self, out, a, ident):
        return self.op("pe", lambda: self.nc.tensor.transpose(out.ap, a.ap, ident.ap), [a, ident], [out])

    def red(self, out, a, op=ALU.add):
        return self.op("dve", lambda: self.nc.vector.tensor_reduce(out=out.ap, in_=a.ap, axis=AX.X, op=op), [a], [out])

    def scan(self, out, d0, d1, init, op0=ALU.mult, op1=ALU.add):
        rd = [d0, d1]
        ia = init
        if isinstance(init, TV):
            rd.append(init)
            ia = init.ap
        return self.op("dve", lambda: self.nc.vector.tensor_tensor_scan(out=out.ap, data0=d0.ap, data1=d1.ap, initial=ia, op0=op0, op1=op1), rd, [out])

    def recip(self, out, a):
        return self.op("dve", lambda: self.nc.vector.reciprocal(out=out.ap, in_=a.ap), [a], [out])


class Rot:
    def __init__(self, items):
        self.items = items
        self.i = 0

    def next(self):
        t = self.items[self.i]
        self.i = (self.i + 1) % len(self.items)
        return t


def build(L, dbg=False, nlayers=DEPTH, stop=None):
    nc = bass.Bass("TRN2", target_bir_lowering=False)
    kb = KB(nc)
    NT = L // 128
    NB = L // 512

    def din(name, shape, dt=F32):
        return nc.dram_tensor(name, list(shape), dt, kind="ExternalInput").ap()

    def dscr(name, shape, dt):
        if dbg:
            return nc.dram_tensor(name, list(shape), dt, kind="ExternalOutput").ap()
        return nc.dram_tensor(name, list(shape), dt).ap()

    x_in = din("x", [L, D])
    w_in = din("w_in", [DEPTH, 1024, 5808])
    w_uq = din("w_uq", [DEPTH, 384, 768])
    w_ukv = din("w_ukv", [DEPTH, 256, 1024])
    w_glu = din("w_glu", [DEPTH, 512, 512])
    w_br = [din("w_br%d" % i, [DEPTH, 512, 1024]) for i in range(3)]
    w_out = din("w_out", [DEPTH, 1024, 1024])
    w_ff1 = din("w_ff1", [DEPTH, 1024, 4096])
    w_ff2 = din("w_ff2", [DEPTH, 4096, 1024])
    pk = din("pk", [DEPTH, 128, PKW])
    rowp = din("rowp", [DEPTH, 320])
    wg = din("wg", [DEPTH, 17, 256])
    btd = din("bt", [DEPTH, 2, 128, 16, 128])
    ctd = din("ct", [DEPTH, 2, 128, 16, 128])
    CW = C_POS + NT
    consts = din("consts", [128, CW])

    QT = dscr("QT", [8, 96, L], BF16)
    KT = dscr("KT", [8, 96, L], BF16)
    Vd = dscr("Vd", [8, 128, NT, 64], BF16)
    oaT = dscr("oaT", [512, L], BF16)
    obT = dscr("obT", [512, L], BF16)
    ocT = dscr("ocT", [512, L], BF16)
    x1d = dscr("x1", [L, D], F32)
    xmd = dscr("xm", [L, D], F32)
    outd = nc.dram_tensor("out", [L, D], F32, kind="ExternalOutput").ap()

    def DR(ap):
        return TV(ap)

    def sb(name, shape, dt=F32):
        return TV(nc.alloc_sbuf_tensor(name, list(shape), dt).ap())

    PS = [TV(nc.alloc_psum_tensor("ps%d" % i, [128, 512], F32).ap()) for i in range(8)]
    psrot = Rot(PS)

    def psb(p):
        return p.v(lambda a: a.bitcast(BF16))

    cst = sb("cst", [128, CW])
    kb.dma(cst, DR(consts))
    identf = cst[:, C_ID:C_ID + 128]
    identb = sb("identb", [128, 128], BF16)
    kb.copy("dve", identb, identf)
    trimb = sb("trimb", [128, 128], BF16)
    kb.copy("dve", trimb, cst[:, C_TRIU:C_TRIU + 128])
    triU = cst[:, C_TRIU:C_TRIU + 128]
    triUs = sb("triUs", [128, 128])
    triLs = sb("triLs", [128, 128])
    kb.ts("dve", triUs, cst[:, C_TRIU:C_TRIU + 128], -1.0 / 16.0, ALU.mult)
    kb.ts("dve", triLs, cst[:, C_TRIL:C_TRIL + 128], -1.0 / 16.0, ALU.mult)
    onesb = sb("onesb", [128, 128], BF16)
    kb.memset("dve", onesb, 1.0)
    mhalf = sb("mhalf", [128, 16])
    kb.memset("dve", mhalf, -0.5)

    def sincos(alloc_fn, ang, n, sin_out, cos_out, tag):
        t0 = alloc_fn("sc0" + tag, [128, n], F32)
        t1 = alloc_fn("sc1" + tag, [128, n], F32)
        for off, dst in ((0.0, sin_out), (0.25, cos_out)):
            kb.ts("dve", t0, ang, 1.0 / TWO_PI, ALU.mult, off, ALU.add)
            kb.ts("dve", t1, t0, MAGIC, ALU.add)
            kb.ts("dve", t1, t1, MAGIC, ALU.subtract)
            kb.tt("dve", t0, t0, t1, ALU.subtract)
            kb.act(dst, t0, AF.Sin, scale=TWO_PI * (1.0 - 1e-6))

    ropec = sb("ropec", [128, NT * 16])
    ropes = sb("ropes", [128, NT * 16])
    es0 = ExitStack()

    def sb0(name, shape, dt=F32):
        return TV(es0.enter_context(nc.sbuf_tensor(name, list(shape), dt)).ap())

    rang = sb0("rang", [128, NT * 16])
    kb.tt("dve", rang.v(lambda a: a.rearrange("p (t i) -> p t i", i=16)),
          cst[:, C_POS:C_POS + NT].v(lambda a: a.unsqueeze(2).to_broadcast([128, NT, 16])),
          cst[:, C_INVF:C_INVF + 16].v(lambda a: a.unsqueeze(1).to_broadcast([128, NT, 16])), ALU.mult)
    sincos(sb0, rang, NT * 16, ropes, ropec, "r")
    kb.barrier()
    es0.close()
    ropec3 = ropec.v(lambda a: a.rearrange("p (t i) -> p t i", i=16))
    ropes3 = ropes.v(lambda a: a.rearrange("p (t i) -> p t i", i=16))

    stage = Rot([sb("stg%d" % i, [128, 8, 256]) for i in range(2)])
    engcyc = Rot(["dve", "pool", "act"])

    def load_w(dst, dcol0, src2d, c0, c1, KC, scale=None):
        for cc in range(c0, c1, 256):
            ce = min(cc + 256, c1)
            n = ce - cc
            st = stage.next()
            kb.dma(st[:, 0:KC, 0:n], DR(src2d[:, cc:ce].rearrange("(kc p) n -> p kc n", p=128)))
            d0 = dcol0 + (cc - c0)
            if scale is None:
                kb.copy(engcyc.next(), dst[:, :, d0:d0 + n], st[:, 0:KC, 0:n])
            else:
                for kc in range(KC):
                    e = engcyc.next()
                    if e == "act":
                        kb.act(dst[:, kc, d0:d0 + n], st[:, kc, 0:n], AF.Copy, scale=scale[:, kc:kc + 1])
                    else:
                        kb.ts(e, dst[:, kc, d0:d0 + n], st[:, kc, 0:n], scale[:, kc:kc + 1], ALU.mult)

    def rstd_of(out, ss, n, cols):
        kb.ts("pool", out, ss, 1.0 / n, ALU.mult, EPS, ALU.add)
        kb.tt("pool", out, out, mhalf[:, 0:cols], ALU.pow)

    def norm_transpose(x_t, hT_dst, tcols, pool_tiles, rot=None):
        junk, ssr, hb = pool_tiles
        kb.act(junk, x_t, AF.Square, accum=ssr[:, 0:1])
        rstd_of(ssr[:, 1:2], ssr[:, 0:1], 1024.0, 1)
        kb.ts("dve", hb, x_t, ssr[:, 1:2], ALU.mult)
        p = (rot or psrot).next()
        pb = psb(p)
        for kc in range(8):
            kb.tr(pb[:, kc * 128:(kc + 1) * 128], hb[:, kc * 128:(kc + 1) * 128], identb)
        kb.copy("act", hT_dst[:, :, tcols], pb.v(lambda a: a.rearrange("p (k t) -> p k t", k=8)))

    if stop == '0':
        kb.barrier()
        return nc
    for l in range(nlayers):
        xsrc = x_in if l == 0 else xmd
        xdst = xmd if l == 0 else outd

        with ExitStack() as es:
            def A(name, shape, dt=F32, es=es):
                return TV(es.enter_context(nc.sbuf_tensor("A%d_%s" % (l, name), list(shape), dt)).ap())

            pkt = A("pk", [128, PKW])
            kb.dma(pkt, DR(pk[l]))
            rows = A("rows", [128, 320])
            kb.dma(rows, DR(rowp[l].partition_broadcast(128)))
            gq_s = A("gqs", [128, 96])
            kb.ts("dve", gq_s, rows[:, 0:96], 96.0 ** -0.5, ALU.mult)
            gk_r = rows[:, 96:192]
            og_h = A("ogh", [128, 128])
            kb.ts("dve", og_h, rows[:, 192:320], 0.5, ALU.mult)
            wga = A("wga", [32, 256])
            kb.dma(wga[0:17, :], DR(wg[l]))
            WAf = A("WAf", [128, 8, 1168], BF16)
            WAt = A("WAt", [128, 8, 1312], BF16)
            g1p = pkt[:, PK_G1:PK_G1 + 8]
            wl = w_in[l]
            for (c0, c1, d0) in ((0, 384, 0), (384, 640, 384), (672, 928, 640), (928, 1184, 896), (1696, 1712, 1152)):
                load_w(WAf, d0, wl, c0, c1, 8, g1p)
            for (c0, c1, d0) in ((640, 672, 0), (928, 1184, 32), (1184, 1696, 288), (1712, 2224, 800)):
                load_w(WAt, d0, wl, c0, c1, 8, g1p)
            if stop == 'A1':
                kb.barrier()
                return nc
            Wuq = A("Wuq", [128, 3, 768], BF16)
            load_w(Wuq, 0, w_uq[l], 0, 768, 3, pkt[:, PK_QNG:PK_QNG + 3])
            Wukv = A("Wukv", [128, 2, 1024], BF16)
            load_w(Wukv, 0, w_ukv[l], 0, 1024, 2, pkt[:, PK_KVNG:PK_KVNG + 2])
            Sst = [A("S%d" % i, [128, 128]) for i in range(2)]
            for s_ in Sst:
                kb.memset("dve", s_, 0.0)

            if stop == 'A2':
                kb.barrier()
                return nc
            xpool = Rot([A("x%d" % i, [128, 1024]) for i in range(2)])
            junk = A("junk", [128, 1024], BF16)
            ssr = Rot([A("ssr%d" % i, [128, 2]) for i in range(2)])
            hb = A("hb", [128, 1024], BF16)
            hT = A("hT", [128, 8, 512], BF16)
            cqT = A("cqT", [128, 3, 512], BF16)
            ckvT = A("ckvT", [128, 2, 512], BF16)
            sqT = A("sqT", [128, 5, 512], BF16)
            gqT = A("gqT", [128, 2, 512])
            gkT = A("gkT", [128, 2, 512])
            glrT = A("glrT", [32, 512])
            kb.memset("dve", glrT, 1.0)
            kk_R = Rot([A("kk%d" % i_, [128, 288]) for i_ in range(2)])
            gv_R = Rot([A("gv%d" % i_, [128, 512]) for i_ in range(2)])
            sr_R = Rot([A("sr%d" % i_, [128, 512]) for i_ in range(2)])
            st2_R = Rot([A("st2%d" % i_, [128, 4]) for i_ in range(2)])
            q32_R = Rot([A("q32%d" % i_, [128, 768]) for i_ in range(1)])
            k96_R = Rot([A("k96%d" % i_, [128, 768]) for i_ in range(1)])
            kv32_R = Rot([A("kv32%d" % i_, [128, 1024]) for i_ in range(1)])
            sq768_R = Rot([A("sq768%d" % i_, [128, 768]) for i_ in range(2)])
            ssg_R = Rot([A("ssg%d" % i_, [128, 16]) for i_ in range(2)])
            ssh_R = Rot([A("ssh%d" % i_, [128, 16]) for i_ in range(2)])
            nrm_R = Rot([A("nrm%d" % i_, [128, 768]) for i_ in range(2)])
            rtmp_R = Rot([A("rtmp%d" % i_, [128, 8, 16]) for i_ in range(2)])
            rtmp2_R = Rot([A("rtmp2%d" % i_, [128, 8, 16]) for i_ in range(2)])
            qb_R = Rot([A("qb%d" % i_, [128, 768], BF16) for i_ in range(2)])
            qTb = A("qTb", [96, 8, 512], BF16)
            kTb = A("kTb", [96, 8, 512], BF16)
            vb = A("vb", [128, 8, 4, 64], BF16)
            lsp_R = Rot([A("lsp%d" % i_, [128, 256]) for i_ in range(2)])
            eq_R = Rot([A("eq%d" % i_, [128, 256]) for i_ in range(2)])
            ek_R = Rot([A("ek%d" % i_, [128, 256]) for i_ in range(2)])
            eend_R = Rot([A("eend%d" % i_, [128, 256]) for i_ in range(2)])
            qtT_R = Rot([A("qtT%d" % i_, [128, 2, 128]) for i_ in range(2)])
            ktT_R = Rot([A("ktT%d" % i_, [128, 2, 128]) for i_ in range(2)])
            kend_R = Rot([A("kend%d" % i_, [128, 256]) for i_ in range(2)])
            Am_R = Rot([A("Am%d" % i_, [128, 128]) for i_ in range(2)])
            o32_R = Rot([A("o32%d" % i_, [128, 512]) for i_ in range(1)])
            osq_R = Rot([A("osq%d" % i_, [128, 512]) for i_ in range(1)])
            ob_R = Rot([A("ob%d" % i_, [128, 512], BF16) for i_ in range(2)])
            obTb = A("obTb", [128, 4, 512], BF16)
            def headnorm_rope(src, g_rep, tix, dstT, tcols):
                sq768 = sq768_R.next()
                ssh = ssh_R.next()
                nrm = nrm_R.next()
                rtmp = rtmp_R.next()
                rtmp2 = rtmp2_R.next()
                qb = qb_R.next()
                s3 = src.v(lambda a: a.rearrange("p (h d) -> p h d", h=8))
                kb.tt("pool", sq768, src, src, ALU.mult)
                kb.red(ssh[:, 0:8], sq768.v(lambda a: a.rearrange("p (h d) -> p h d", h=8)))
                rstd_of(ssh[:, 8:16], ssh[:, 0:8], 96.0, 8)
                n3 = nrm.v(lambda a: a.rearrange("p (h d) -> p h d", h=8))
                kb.tt("dve", n3, s3, ssh[:, 8:16].v(lambda a: a.unsqueeze(2).to_broadcast([128, 8, 96])), ALU.mult)
                kb.tt("dve", n3, n3, g_rep.v(lambda a: a.unsqueeze(1).to_broadcast([128, 8, 96])), ALU.mult)
                b3 = qb.v(lambda a: a.rearrange("p (h d) -> p h d", h=8))
                kb.copy("pool", b3[:, :, 0:64], n3[:, :, 0:64])
                cs = ropec3[:, tix, :].v(lambda a: a.unsqueeze(1).to_broadcast([128, 8, 16]))
                sn = ropes3[:, tix, :].v(lambda a: a.unsqueeze(1).to_broadcast([128, 8, 16]))
                x1_, x2_ = n3[:, :, 64:80], n3[:, :, 80:96]
                kb.tt("dve", rtmp, x1_, cs, ALU.mult)
                kb.tt("dve", rtmp2, x2_, sn, ALU.mult)
                kb.tt("dve", b3[:, :, 64:80], rtmp, rtmp2, ALU.subtract)
                kb.tt("dve", rtmp, x1_, sn, ALU.mult)
                kb.tt("dve", rtmp2, x2_, cs, ALU.mult)
                kb.tt("dve", b3[:, :, 80:96], rtmp, rtmp2, ALU.add)
                p = psrot.next()
                pb = psb(p)
                for h in range(8):
                    kb.tr(pb[0:96, h * 128:(h + 1) * 128], qb[:, h * 96:(h + 1) * 96], identb)
                kb.copy("act", dstT[:, :, tcols], pb[0:96, :].v(lambda a: a.rearrange("p (h t) -> p h t", h=8)))

            for blk in range(NB):
                bsl = slice(blk * 512, (blk + 1) * 512)
                for tt_ in range(4):
                    tix = blk * 4 + tt_
                    xt = xpool.next()
                    kb.dma(xt, DR(xsrc[tix * 128:(tix + 1) * 128, :]))
                    norm_transpose(xt, hT, slice(tt_ * 128, (tt_ + 1) * 128), (junk, ssr.next(), hb))
                if stop == 'A3':
                    kb.barrier()
                    return nc
                fm = [(0, 128, ("cq", 0)), (128, 128, ("cq", 1)), (256, 128, ("cq", 2)), (384, 128, ("ckv", 0)), (512, 128, ("ckv", 1)),
                      (640, 128, ("gq", 0)), (768, 128, ("gq", 1)), (896, 128, ("gk", 0)), (1024, 128, ("gk", 1)), (1152, 16, ("glr", 0))]
                import os
                for (c0, M, (kind, ci)) in fm[:int(os.environ.get('FMN', '99'))]:
                    p = psrot.next()
                    for kc in range(8):
                        kb.mm(p[0:M, :], WAf[:, kc, c0:c0 + M], hT[:, kc, :], kc == 0, kc == 7)
                    if kind == "cq":
                        kb.copy("dve", cqT[:, ci, :], p)
                        kb.tt("pool", sqT[:, ci, :], cqT[:, ci, :], cqT[:, ci, :], ALU.mult)
                    elif kind == "ckv":
                        kb.copy("dve", ckvT[:, ci, :], p)
                        kb.tt("pool", sqT[:, 3 + ci, :], ckvT[:, ci, :], ckvT[:, ci, :], ALU.mult)
                    elif kind == "gq":
                        kb.copy("act", gqT[:, ci, :], p)
                    elif kind == "gk":
                        kb.copy("dve", gkT[:, ci, :], p)
                    else:
                        kb.copy("act", glrT[0:16, :], p[0:16, :])

                if stop == 'A4':
                    kb.barrier()
                    return nc
                for tt_ in range(4):
                    tix = blk * 4 + tt_
                    tsl = slice(tt_ * 128, (tt_ + 1) * 128)
                    kk = kk_R.next()
                    gv = gv_R.next()
                    sr = sr_R.next()
                    st2 = st2_R.next()
                    q32 = q32_R.next()
                    k96 = k96_R.next()
                    kv32 = kv32_R.next()
                    lsp = lsp_R.next()
                    eq = eq_R.next()
                    ek = ek_R.next()
                    eend = eend_R.next()
                    qtT = qtT_R.next()
                    ktT = ktT_R.next()
                    kend = kend_R.next()
                    o32 = o32_R.next()
                    osq = osq_R.next()
                    ob = ob_R.next()
                    ssg = ssg_R.next()
                    pkk, pgv, pgr = psrot.next(), psrot.next(), psrot.next()
                    for (c0, n, p) in ((0, 288, pkk), (288, 512, pgv), (800, 512, pgr)):
                        for kc in range(8):
                            kb.mm(p[:, 0:n], hT[:, kc, tsl], WAt[:, kc, c0:c0 + n], kc == 0, kc == 7)
                    kb.copy("act", kk, pkk[:, 0:288])
                    kb.copy("dve", gv, pgv)
                    kb.act(sr, pgr, AF.Tanh, scale=0.5)
                    kb.stt(sr, sr, 1.0, pgr, ALU.add, ALU.mult)
                    if stop == 'A5':
                        kb.barrier()
                        return nc
                    pst = psrot.next()
                    for c in range(3):
                        kb.mm(pst[:, 0:1], sqT[:, c, tsl], onesb[:, 0:1], c == 0, c == 2)
                    for c in range(2):
                        kb.mm(pst[:, 1:2], sqT[:, 3 + c, tsl], onesb[:, 0:1], c == 0, c == 1)
                    kb.copy("dve", st2[:, 0:2], pst[:, 0:2])
                    rstd_of(st2[:, 2:3], st2[:, 0:1], 384.0, 1)
                    rstd_of(st2[:, 3:4], st2[:, 1:2], 256.0, 1)
                    pq0, pq1 = psrot.next(), psrot.next()
                    for (p, n0, n1) in ((pq0, 0, 512), (pq1, 512, 768)):
                        for c in range(3):
                            kb.mm(p[:, 0:n1 - n0], cqT[:, c, tsl], Wuq[:, c, n0:n1], c == 0, c == 2)
                    kb.act(q32[:, 0:512], pq0, AF.Copy, scale=st2[:, 2:3])
                    kb.act(q32[:, 512:768], pq1[:, 0:256], AF.Copy, scale=st2[:, 2:3])
                    headnorm_rope(q32, gq_s, tix, qTb, tsl)
                    if stop == 'A6':
                        kb.barrier()
                        return nc
                    pk0, pk1 = psrot.next(), psrot.next()
                    for (p, n0) in ((pk0, 0), (pk1, 512)):
                        for c in range(2):
                            kb.mm(p, ckvT[:, c, tsl], Wukv[:, c, n0:n0 + 512], c == 0, c == 1)
                    kb.act(kv32[:, 0:512], pk0, AF.Copy, scale=st2[:, 3:4])
                    kb.act(kv32[:, 512:1024], pk1, AF.Copy, scale=st2[:, 3:4])
                    kv3 = kv32.v(lambda a: a.rearrange("p (h d) -> p h d", h=8))
                    k3 = k96.v(lambda a: a.rearrange("p (h d) -> p h d", h=8))
                    kb.copy("pool", k3[:, :, 0:64], kv3[:, :, 0:64])
                    kb.copy("pool", k3[:, :, 64:96], kk[:, 0:32].v(lambda a: a.unsqueeze(1).to_broadcast([128, 8, 32])))
                    kb.copy("pool", vb[:, :, tt_, :], kv3[:, :, 64:128])
                    headnorm_rope(k96, gk_r, tix, kTb, tsl)

                    if stop == 'A7':
                        kb.barrier()
                        return nc
                    pl = psrot.next()
                    kb.mm(pl[:, 0:256], glrT[0:17, tsl], wga[0:17, :], True, True)
                    kb.act(lsp, pl[:, 0:256], AF.Exp, scale=-1.0)
                    kb.act(lsp, lsp, AF.Ln, bias=1.0)
                    pbc = psrot.next()
                    for hc in range(2):
                        kb.mm(pbc[:, hc * 128:(hc + 1) * 128], lsp[:, hc * 128:(hc + 1) * 128], triUs, True, True)
                    kb.mm(pbc[:, 256:512], triLs, lsp, True, True)
                    kb.act(eq, pbc[:, 0:256], AF.Exp)
                    kb.act(ek, pbc[:, 0:256], AF.Exp, scale=-1.0)
                    kb.act(eend, pbc[:, 256:512], AF.Exp)
                    for hc in range(2):
                        kb.stt(qtT[:, hc, :], gqT[:, hc, tsl], 0.125, eq[:, hc * 128:(hc + 1) * 128], ALU.mult, ALU.mult)
                        kb.tt("pool", ktT[:, hc, :], gkT[:, hc, tsl], ek[:, hc * 128:(hc + 1) * 128], ALU.mult)
                    kb.tt("pool", kend, kk[:, 32:288], eend, ALU.mult)
                    if stop == 'A8':
                        kb.barrier()
                        return nc
                    po = psrot.next()
                    pds = psrot.next()
                    for h in range(4):
                        hc, hb_ = h // 2, (h % 2) * 64
                        pa = psrot.next()
                        Am = Am_R.next()
                        kb.mm(pa[:, 0:128], ktT[hb_:hb_ + 64, hc, :], qtT[hb_:hb_ + 64, hc, :], True, True)
                        kb.tt("dve", Am, pa[:, 0:128], triU, ALU.mult)
                        kb.mm(po[:, h * 128:(h + 1) * 128], Am, gv[:, h * 128:(h + 1) * 128], True, False)
                        kb.mm(po[:, h * 128:(h + 1) * 128], qtT[hb_:hb_ + 64, hc, :], Sst[hc][hb_:hb_ + 64, :], False, True)
                        kb.mm(pds[hb_:hb_ + 64, hc * 128:(hc + 1) * 128], kend[:, h * 64:(h + 1) * 64], gv[:, h * 128:(h + 1) * 128], True, True)
                    for hc in range(2):
                        kb.stt(Sst[hc], Sst[hc], eq[:, hc * 128 + 127:hc * 128 + 128], pds[:, hc * 128:(hc + 1) * 128], ALU.mult, ALU.add)
                    if stop == 'A9':
                        kb.barrier()
                        return nc
                    kb.copy("act", o32, po)
                    kb.tt("pool", osq, o32, o32, ALU.mult)
                    kb.red(ssg[:, 0:4], osq.v(lambda a: a.rearrange("p (h e) -> p h e", h=4)))
                    rstd_of(ssg[:, 8:12], ssg[:, 0:4], 128.0, 4)
                    o3 = o32.v(lambda a: a.rearrange("p (h e) -> p h e", h=4))
                    kb.tt("dve", o3, o3, ssg[:, 8:12].v(lambda a: a.unsqueeze(2).to_broadcast([128, 4, 128])), ALU.mult)
                    kb.tt("dve", o3, o3, og_h.v(lambda a: a.unsqueeze(1).to_broadcast([128, 4, 128])), ALU.mult)
                    kb.tt("dve", ob, o32, sr, ALU.mult)
                    p = psrot.next()
                    pb = psb(p)
                    for c in range(4):
                        kb.tr(pb[:, c * 128:(c + 1) * 128], ob[:, c * 128:(c + 1) * 128], identb)
                    kb.copy("act", obTb[:, :, tsl], pb[:, 0:512].v(lambda a: a.rearrange("p (c t) -> p c t", c=4)))

                if stop == 'A10':
                    kb.barrier()
                    return nc
                kb.dma(DR(QT[:, :, bsl].rearrange("h d t -> d h t")), qTb)
                kb.dma(DR(KT[:, :, bsl].rearrange("h d t -> d h t")), kTb)
                kb.dma(DR(Vd.rearrange("h p t d -> p h t d")[:, :, blk * 4:(blk + 1) * 4, :]), vb)
                kb.dma(DR(obT[:, bsl].rearrange("(c p) t -> p c t", p=128)), obTb)
            kb.barrier()

        if stop == 'A':
            return nc
        with ExitStack() as es:
            def A(name, shape, dt=F32, es=es):
                return TV(es.enter_context(nc.sbuf_tensor("S%d_%s" % (l, name), list(shape), dt)).ap())

            srot = Rot(PS[5:8]) if os.environ.get("SROT") != "all" else psrot
            pkt = A("pk", [128, PKW])
            kb.dma(pkt, DR(pk[l]))
            WAs = A("WAs", [128, 8, 512], BF16)
            load_w(WAs, 0, w_in[l], 2224, 2736, 8, pkt[:, PK_G1:PK_G1 + 8])
            Wglu = A("Wglu", [128, 4, 512], BF16)
            load_w(Wglu, 0, w_glu[l], 0, 512, 4, None)
            BTr = A("BTr", [128, 16, 128], BF16)
            BTi = A("BTi", [128, 16, 128], BF16)
            CTr = A("CTr", [128, 16, 128], BF16)
            CTi = A("CTi", [128, 16, 128], BF16)
            for dst, src, sh in ((BTr, btd[l, 0], (16, 128)), (BTi, btd[l, 1], (16, 128)), (CTr, ctd[l, 0], (16, 128)), (CTi, ctd[l, 1], (16, 128))):
                st = stage.next()
                sv = st.v(lambda a: a.rearrange("p k n -> p (k n)"))[:, 0:sh[0] * sh[1]].v(lambda a: a.rearrange("p (k n) -> p k n", k=sh[0]))
                kb.dma(sv, DR(src))
                kb.copy("dve", dst, sv)

            lre = A("lre", [128, 16])
            kb.ts("dve", lre, pkt[:, PK_LRE:PK_LRE + 16], -1e-4, ALU.min)
            lim = pkt[:, PK_LIM:PK_LIM + 16]
            dtt = A("dtt", [128, 16])
            kb.act(dtt, pkt[:, PK_LDT:PK_LDT + 16], AF.Exp)
            th = A("th", [128, 16])
            kb.tt("dve", th, lim, dtt, ALU.mult)
            rr_ = A("rr", [128, 16])
            kb.tt("dve", rr_, lre, dtt, ALU.mult)
            kb.act(rr_, rr_, AF.Exp)
            sth = A("sth", [128, 16])
            cth = A("cth", [128, 16])
            sincos(A, th, 16, sth, cth, "t")
            nre = A("nre", [128, 16])
            nim = A("nim", [128, 16])
            kb.tt("dve", nre, rr_, cth, ALU.mult)
            kb.ts("dve", nre, nre, -1.0, ALU.add)
            kb.tt("dve", nim, rr_, sth, ALU.mult)
            den = A("den", [128, 16])
            tmp16 = A("tmp16", [128, 16])
            kb.tt("dve", den, lre, lre, ALU.mult)
            kb.tt("dve", tmp16, lim, lim, ALU.mult)
            kb.tt("dve", den, den, tmp16, ALU.add)
            kb.recip(den, den)
            cre = A("cre", [128, 16])
            cim = A("cim", [128, 16])
            kb.tt("dve", cre, nre, lre, ALU.mult)
            kb.tt("dve", tmp16, nim, lim, ALU.mult)
            kb.tt("dve", cre, cre, tmp16, ALU.add)
            kb.tt("dve", cre, cre, den, ALU.mult)
            kb.tt("dve", cim, nim, lre, ALU.mult)
            kb.tt("dve", tmp16, nre, lim, ALU.mult)
            kb.tt("dve", cim, cim, tmp16, ALU.subtract)
            kb.tt("dve", cim, cim, den, ALU.mult)
            E2c = A("E2c", [128, 2048])
            E2s = A("E2s", [128, 2048])
            E1r = A("E1r", [128, 2048])
            E1i = A("E1i", [128, 2048])
            Rz = A("Rz", [128, 2048])
            es2 = ExitStack()
            tang = A("tang", [128, 2048], es=es2)
            tau = cst[:, C_TAU:C_TAU + 128]
            for j in range(16):
                kb.ts("dve", tang[:, j * 128:(j + 1) * 128], tau, th[:, j:j + 1], ALU.mult)
            sincos(lambda n_, s_, d_: A(n_, s_, d_, es=es2), tang, 2048, E2s, E2c, "T")
            for j in range(16):
                sl = slice(j * 128, (j + 1) * 128)
                kb.ts("dve", tang[:, sl], E2s[:, sl], cim[:, j:j + 1], ALU.mult)
                kb.stt(E1r[:, sl], E2c[:, sl], cre[:, j:j + 1], tang[:, sl], ALU.mult, ALU.add)
                kb.ts("dve", tang[:, sl], E2s[:, sl], cre[:, j:j + 1], ALU.mult)
                kb.stt(E1i[:, sl], E2c[:, sl], cim[:, j:j + 1], tang[:, sl], ALU.mult, ALU.subtract)
                kb.ts("dve", Rz[:, sl], cst[:, C_TRIU + 127:C_TRIU + 128].v(lambda a: a.to_broadcast([128, 128])), rr_[:, j:j + 1], ALU.mult)
            Rz3 = Rz.v(lambda a: a.rearrange("p (j t) -> p j t", t=128))
            kb.memset("dve", Rz3[:, :, 0:1], 0.0)
            kb.barrier()
            es2.close()
            car_r = A("carr", [128, 16])
            car_i = A("cari", [128, 16])
            kb.memset("dve", car_r, 0.0)
            kb.memset("dve", car_i, 0.0)
            if stop == 'S1':
                kb.barrier()
                return nc
            xpool = Rot([A("x%d" % i, [128, 1024]) for i in range(2)])
            junk = A("junk", [128, 1024], BF16)
            ssr = Rot([A("ssr%d" % i, [128, 2]) for i in range(2)])
            hb = A("hb", [128, 1024], BF16)
            hT = A("hT", [128, 8, 512], BF16)
            uT = A("uT", [128, 4, 512])
            uTb = A("uTb", [128, 4, 512], BF16)
            t1_R = Rot([A("t1%d" % i_, [128, 1024]) for i_ in range(2)])
            t2_R = Rot([A("t2%d" % i_, [128, 1024]) for i_ in range(2)])
            btr_R = Rot([A("btr%d" % i_, [128, 1024]) for i_ in range(2)])
            bti_R = Rot([A("bti%d" % i_, [128, 1024]) for i_ in range(2)])
            xsr_R = Rot([A("xsr%d" % i_, [128, 1024]) for i_ in range(2)])
            xsi_R = Rot([A("xsi%d" % i_, [128, 1024]) for i_ in range(2)])
            xbr_R = Rot([A("xbr%d" % i_, [128, 1024], BF16) for i_ in range(2)])
            xbi_R = Rot([A("xbi%d" % i_, [128, 1024], BF16) for i_ in range(2)])
            rc_R = Rot([A("rc%d" % i_, [128, 16]) for i_ in range(2)])
            ctmp_R = Rot([A("ctmp%d" % i_, [128, 32]) for i_ in range(2)])
            y32 = A("y32", [128, 4, 512])
            yb = A("yb", [128, 4, 512], BF16)
            ocTb = A("ocTb", [128, 4, 512], BF16)
            g3 = A("g3", [128, 512])
            d5 = pkt[:, PK_D5:PK_D5 + 4]
            bglu = A("bgluh", [128, 4])
            kb.ts("dve", bglu, pkt[:, PK_BGLU:PK_BGLU + 4], 0.5, ALU.mult)

            for blk in range(NB):
                bsl = slice(blk * 512, (blk + 1) * 512)
                for tt_ in range(4):
                    tix = blk * 4 + tt_
                    xt = xpool.next()
                    kb.dma(xt, DR(xsrc[tix * 128:(tix + 1) * 128, :]))
                    norm_transpose(xt, hT, slice(tt_ * 128, (tt_ + 1) * 128), (junk, ssr.next(), hb), srot)
                if stop == 'S15':
                    kb.barrier()
                    return nc
                for ci in range(4):
                    p = srot.next()
                    for kc in range(8):
                        kb.mm(p, WAs[:, kc, ci * 128:(ci + 1) * 128], hT[:, kc, :], kc == 0, kc == 7)
                    if stop == 'S16':
                        kb.barrier()
                        return nc
                    kb.copy("dve", uT[:, ci, :], p)
                    kb.copy("pool", uTb[:, ci, :], uT[:, ci, :])
                if stop == 'S2':
                    kb.barrier()
                    return nc
                for tt_ in range(4):
                    tsl = slice(tt_ * 128, (tt_ + 1) * 128)
                    py = [PS[0]]
                    for half in range(2):
                        t1 = t1_R.next()
                        t2 = t2_R.next()
                        btr = btr_R.next()
                        bti = bti_R.next()
                        xsr = xsr_R.next()
                        xsi = xsi_R.next()
                        xbr = xbr_R.next()
                        xbi = xbi_R.next()
                        rc = rc_R.next()
                        ctmp = ctmp_R.next()
                        pbr, pbi, pbr2, pbi2 = PS[1], PS[2], PS[3], PS[4]
                        banks_r, banks_i = (pbr, pbr2), (pbi, pbi2)
                        for jj in range(8):
                            j = half * 8 + jj
                            c, q_ = j // 4, j % 4
                            rs = slice(32 * q_, 32 * q_ + 32)
                            kb.mm(banks_r[jj // 4][:, (jj % 4) * 128:(jj % 4 + 1) * 128], BTr[:, j, :], uTb[:, c, tsl], True, True)
                            kb.mm(banks_i[jj // 4][:, (jj % 4) * 128:(jj % 4 + 1) * 128], BTi[:, j, :], uTb[:, c, tsl], True, True)
                        hs = slice(half * 1024, (half + 1) * 1024)
                        for g_ in range(2):
                            gs = slice(g_ * 512, (g_ + 1) * 512)
                            hg = slice(half * 1024 + g_ * 512, half * 1024 + (g_ + 1) * 512)
                            kb.tt("dve", t1[:, gs], banks_r[g_], E1r[:, hg], ALU.mult)
                            kb.tt("dve", t2[:, gs], banks_i[g_], E1i[:, hg], ALU.mult)
                            kb.tt("dve", btr[:, gs], t1[:, gs], t2[:, gs], ALU.subtract)
                            kb.tt("dve", t1[:, gs], banks_i[g_], E1r[:, hg], ALU.mult)
                            kb.tt("dve", t2[:, gs], banks_r[g_], E1i[:, hg], ALU.mult)
                            kb.tt("pool", bti[:, gs], t1[:, gs], t2[:, gs], ALU.add)
                        if stop == 'S3':
                            kb.barrier()
                            return nc
                        js = slice(half * 8, half * 8 + 8)
                        kb.tt("dve", rc[:, 0:8], rr_[:, js], car_r[:, js], ALU.mult)
                        kb.tt("dve", rc[:, 8:16], rr_[:, js], car_i[:, js], ALU.mult)
                        b3r = btr.v(lambda a: a.rearrange("p (j t) -> p j t", t=128))
                        b3i = bti.v(lambda a: a.rearrange("p (j t) -> p j t", t=128))
                        kb.tt("dve", b3r[:, :, 0], b3r[:, :, 0], rc[:, 0:8], ALU.add)
                        kb.tt("dve", b3i[:, :, 0], b3i[:, :, 0], rc[:, 8:16], ALU.add)
                        kb.scan(xsr, Rz[:, hs], btr, 0.0)
                        kb.scan(xsi, Rz[:, hs], bti, 0.0)
                        if stop == 'S4':
                            kb.barrier()
                            return nc
                        x3r = xsr.v(lambda a: a.rearrange("p (j t) -> p j t", t=128))
                        x3i = xsi.v(lambda a: a.rearrange("p (j t) -> p j t", t=128))
                        e3c = E2c[:, hs].v(lambda a: a.rearrange("p (j t) -> p j t", t=128))
                        e3s = E2s[:, hs].v(lambda a: a.rearrange("p (j t) -> p j t", t=128))
                        kb.tt("dve", ctmp[:, 0:8], x3r[:, :, 127], e3c[:, :, 127], ALU.mult)
                        kb.tt("dve", ctmp[:, 8:16], x3i[:, :, 127], e3s[:, :, 127], ALU.mult)
                        kb.tt("dve", ctmp[:, 16:24], x3r[:, :, 127], e3s[:, :, 127], ALU.mult)
                        kb.tt("dve", ctmp[:, 24:32], x3i[:, :, 127], e3c[:, :, 127], ALU.mult)
                        kb.tt("dve", car_r[:, js], ctmp[:, 0:8], ctmp[:, 8:16], ALU.subtract)
                        kb.tt("dve", car_i[:, js], ctmp[:, 16:24], ctmp[:, 24:32], ALU.add)
                        if stop == 'S5':
                            kb.barrier()
                            return nc
                        kb.tt("pool", t1, xsr, E2c[:, hs], ALU.mult)
                        kb.tt("pool", t2, xsi, E2s[:, hs], ALU.mult)
                        kb.tt("dve", xbr, t1, t2, ALU.subtract)
                        kb.tt("dve", t1, xsi, E2c[:, hs], ALU.mult)
                        kb.tt("dve", t2, xsr, E2s[:, hs], ALU.mult)
                        kb.stt(xbi, t1, -1.0, t2, ALU.mult, ALU.subtract)
                        for jj in range(8):
                            j = half * 8 + jj
                            c, q_ = j // 4, j % 4
                            rs = slice(32 * q_, 32 * q_ + 32)
                            kb.mm(py[0][:, c * 128:(c + 1) * 128], CTr[:, j, :], xbr[:, jj * 128:(jj + 1) * 128], q_ == 0, False)
                            kb.mm(py[0][:, c * 128:(c + 1) * 128], CTi[:, j, :], xbi[:, jj * 128:(jj + 1) * 128], False, q_ == 3)
                    if stop == 'S6':
                        kb.barrier()
                        return nc
                    for c in range(4):
                        kb.stt(y32[:, c, tsl], uT[:, c, tsl], d5[:, c:c + 1], py[0][:, c * 128:(c + 1) * 128], ALU.mult, ALU.add)

                if stop == 'S7':
                    kb.barrier()
                    return nc
                for c in range(4):
                    yc = y32[:, c, :]
                    kb.tt("pool", g3, yc, yc, ALU.mult)
                    kb.ts("dve", g3, g3, 0.044715, ALU.mult, 1.0, ALU.add)
                    kb.tt("pool", g3, g3, yc, ALU.mult)
                    kb.act(g3, g3, AF.Tanh, scale=0.7978845608028654)
                    kb.stt(g3, g3, 1.0, yc, ALU.add, ALU.mult)
                    kb.ts("dve", yc, g3, 0.5, ALU.mult)
                    kb.copy("pool", yb[:, c, :], yc)
                if stop == 'S8':
                    kb.barrier()
                    return nc
                for mc in range(4):
                    p = srot.next()
                    for c in range(4):
                        kb.mm(p, Wglu[:, c, mc * 128:(mc + 1) * 128], yb[:, c, :], c == 0, c == 3)
                    kb.act(g3, p, AF.Tanh, bias=bglu[:, mc:mc + 1], scale=0.5)
                    kb.stt(g3, g3, 1.0, y32[:, mc, :], ALU.add, ALU.mult)
                    kb.ts("dve", ocTb[:, mc, :], g3, 0.5, ALU.mult)
                kb.dma(DR(ocT[:, bsl].rearrange("(c p) t -> p c t", p=128)), ocTb)
            kb.barrier()


        if stop == 'S':
            return nc
        with ExitStack() as es:
            def Bt(name, shape, dt=F32):
                return TV(es.enter_context(nc.sbuf_tensor("B%d_%s" % (l, name), list(shape), dt)).ap())

            KTh = Rot([Bt("KT%d" % i, [96, L], BF16) for i in range(2)])
            Vh = Rot([Bt("V%d" % i, [128, NT, 65], BF16) for i in range(2)])
            Vraw = Rot([Bt("Vr%d" % i, [128, NT, 64], BF16) for i in range(2)])
            QTh = Rot([Bt("QT%d" % i, [96, L], BF16) for i in range(2)])
            Pt = Rot([Bt("P%d" % i, [128, 512], BF16) for i in range(3)])
            orec = Bt("orec", [128, 4])
            onb = Rot([Bt("onb%d" % i, [128, 4, 64], BF16) for i in range(2)])
            oTs = Rot([Bt("oTs%d" % i, [64, 512], BF16) for i in range(2)])
            for V_ in Vh.items:
                kb.memset("dve", V_[:, :, 64:65], 1.0)
            SB_ = Rot(PS[0:3])
            OB_ = PS[3:7]
            for h in range(8):
                Kt_, V_, Q_ = KTh.next(), Vh.next(), QTh.next()
                kb.dma(Kt_, DR(KT[h]))
                kb.dma(Q_, DR(QT[h]))
                Vr_ = Vraw.next()
                kb.dma(Vr_, DR(Vd[h]))
                kb.copy("pool", V_[:, :, 0:64], Vr_)
                for qb_ in range(NB):
                    nk = 4 * (qb_ + 1)
                    for kt in range(nk):
                        j = kt - 4 * qb_
                        q0 = max(j, 0) * 128
                        ps_ = SB_.next()
                        kb.mm(ps_[:, q0:512], Kt_[:, kt * 128:(kt + 1) * 128], Q_[:, qb_ * 512 + q0:(qb_ + 1) * 512], True, True)
                        P_ = Pt.next()
                        kb.act(P_[:, q0:512], ps_[:, q0:512], AF.Exp)
                        if j >= 0:
                            kb.tt("dve", P_[:, j * 128:(j + 1) * 128], P_[:, j * 128:(j + 1) * 128], trimb, ALU.mult)
                        for qi in range(max(j, 0), 4):
                            last = (kt == 4 * qb_ + qi)
                            kb.mm(OB_[qi][:, 0:65], P_[:, qi * 128:(qi + 1) * 128], V_[:, kt, :], kt == 0, last)
                    on_ = onb.next()
                    for qi in range(4):
                        kb.recip(orec[:, qi:qi + 1], OB_[qi][:, 64:65])
                        kb.ts("dve", on_[:, qi, :], OB_[qi][:, 0:64], orec[:, qi:qi + 1], ALU.mult)
                    pT = PS[7]
                    pTb = psb(pT)
                    for qi in range(4):
                        kb.tr(pTb[0:64, qi * 128:(qi + 1) * 128], on_[:, qi, :], identb)
                    oT_ = oTs.next()
                    kb.copy("act", oT_, pTb[0:64, 0:512])
                    kb.dma(DR(oaT[h * 64:(h + 1) * 64, qb_ * 512:(qb_ + 1) * 512]), oT_)
            kb.barrier()

        if stop == 'B':
            return nc
        with ExitStack() as es:
            def Ct(name, shape, dt=F32):
                return TV(es.enter_context(nc.sbuf_tensor("C%d_%s" % (l, name), list(shape), dt)).ap())

            pkt = Ct("pk", [128, PKW])
            kb.dma(pkt, DR(pk[l]))
            gbh = Ct("gbh", [128, 24])
            kb.ts("dve", gbh, pkt[:, PK_GB:PK_GB + 24], 0.5, ALU.mult)
            Wg = Ct("Wg", [128, 8, 3072], BF16)
            load_w(Wg, 0, w_in[l], 2736, 5808, 8, pkt[:, PK_G1:PK_G1 + 8])
            Wb = [Ct("Wb%d" % i, [128, 4, 1024], BF16) for i in range(3)]
            for i in range(3):
                load_w(Wb[i], 0, w_br[i][l], 0, 1024, 4, None)
            Wo = Ct("Wo", [128, 8, 1024], BF16)
            load_w(Wo, 0, w_out[l], 0, 1024, 8, None)
            xt4 = [Ct("x%d" % i, [128, 1024]) for i in range(4)]
            junk = Ct("junk", [128, 1024], BF16)
            ssr = Rot([Ct("ssr%d" % i, [128, 2]) for i in range(2)])
            hb = Ct("hb", [128, 1024], BF16)
            hT_R = Rot([Ct("hT%d" % i_, [128, 8, 512], BF16) for i_ in range(2)])
            oin = [Ct("oin%d" % i, [128, 4, 512], BF16) for i in range(3)]
            gsb = Rot([Ct("g%d" % i, [128, 512]) for i in range(3)])
            macc = Ct("macc", [128, 512])
            mtmp = Ct("mtmp", [128, 512])
            mT = Ct("mT", [128, 8, 512], BF16)
            xo = Rot([Ct("xo%d" % i, [128, 1024]) for i in range(2)])
            for blk in range(NB):
                hT = hT_R.next()
                bsl = slice(blk * 512, (blk + 1) * 512)
                for i, src in enumerate((oaT, obT, ocT)):
                    kb.dma(oin[i], DR(src[:, bsl].rearrange("(c p) t -> p c t", p=128)))
                for tt_ in range(4):
                    tix = blk * 4 + tt_
                    kb.dma(xt4[tt_], DR(xsrc[tix * 128:(tix + 1) * 128, :]))
                    norm_transpose(xt4[tt_], hT, slice(tt_ * 128, (tt_ + 1) * 128), (junk, ssr.next(), hb))
                for fc in range(8):
                    for b in range(3):
                        pg = psrot.next()
                        gc = b * 8 + fc
                        for kc in range(8):
                            kb.mm(pg, Wg[:, kc, gc * 128:(gc + 1) * 128], hT[:, kc, :], kc == 0, kc == 7)
                        g_ = gsb.next()
                        kb.act(g_, pg, AF.Tanh, bias=gbh[:, gc:gc + 1], scale=0.5)
                        pp = psrot.next()
                        for c in range(4):
                            kb.mm(pp, Wb[b][:, c, fc * 128:(fc + 1) * 128], oin[b][:, c, :], c == 0, c == 3)
                        if b == 0:
                            kb.stt(macc, g_, 1.0, pp, ALU.add, ALU.mult)
                        else:
                            kb.stt(mtmp, g_, 1.0, pp, ALU.add, ALU.mult)
                            kb.tt("pool", macc, macc, mtmp, ALU.add)
                    kb.ts("dve", mT[:, fc, :], macc, 0.5, ALU.mult)
                for tt_ in range(4):
                    tix = blk * 4 + tt_
                    tsl = slice(tt_ * 128, (tt_ + 1) * 128)
                    xo_ = xo.next()
                    for nh in range(2):
                        p = psrot.next()
                        for kc in range(8):
                            kb.mm(p, mT[:, kc, tsl], Wo[:, kc, nh * 512:(nh + 1) * 512], kc == 0, kc == 7)
                        kb.tt("dve", xo_[:, nh * 512:(nh + 1) * 512], p, xt4[tt_][:, nh * 512:(nh + 1) * 512], ALU.add)
                    kb.dma(DR(x1d[tix * 128:(tix + 1) * 128, :]), xo_)
            kb.barrier()

        if stop == 'C':
            return nc
        with ExitStack() as es:
            def Dt(name, shape, dt=F32):
                return TV(es.enter_context(nc.sbuf_tensor("D%d_%s" % (l, name), list(shape), dt)).ap())

            pkt = Dt("pk", [128, PKW])
            kb.dma(pkt, DR(pk[l]))
            W1 = Dt("W1", [128, 8, 4096], BF16)
            load_w(W1, 0, w_ff1[l], 0, 4096, 8, pkt[:, PK_G2:PK_G2 + 8])
            W2 = Dt("W2", [128, 32, 1024], BF16)
            for k0 in range(0, 32, 8):
                load_w(W2[:, k0:k0 + 8, :], 0, w_ff2[l][k0 * 128:(k0 + 8) * 128, :], 0, 1024, 8, None)
            xt8 = [Dt("x%d" % i, [128, 1024]) for i in range(4)]
            junk = Dt("junk", [128, 1024], BF16)
            ssr = Rot([Dt("ssr%d" % i, [128, 2]) for i in range(2)])
            hb = Dt("hb", [128, 1024], BF16)
            hT_R = Rot([Dt("hT%d" % i_, [128, 8, 256], BF16) for i_ in range(2)])
            uT = Dt("uT", [128, 32, 256], BF16)
            rl = Rot([Dt("rl%d" % i, [128, 256]) for i in range(2)])
            for blk in range(L // 256):
                hT = hT_R.next()
                xt4 = xt8[(blk % 2) * 2:(blk % 2) * 2 + 2]
                for tt_ in range(2):
                    tix = blk * 2 + tt_
                    kb.dma(xt4[tt_], DR(x1d[tix * 128:(tix + 1) * 128, :]))
                    norm_transpose(xt4[tt_], hT, slice(tt_ * 128, (tt_ + 1) * 128), (junk, ssr.next(), hb))
                for fc in range(32):
                    p = psrot.next()
                    for kc in range(8):
                        kb.mm(p[:, 0:256], W1[:, kc, fc * 128:(fc + 1) * 128], hT[:, kc, :], kc == 0, kc == 7)
                    r_ = rl.next()
                    kb.act(r_, p[:, 0:256], AF.Relu)
                    kb.tt("pool" if fc % 2 else "dve", uT[:, fc, :], r_, r_, ALU.mult)
                for tt_ in range(2):
                    tix = blk * 2 + tt_
                    tsl = slice(tt_ * 128, (tt_ + 1) * 128)
                    xo_ = xt4[tt_]
                    for nh in range(2):
                        p = psrot.next()
                        for fc in range(32):
                            kb.mm(p, uT[:, fc, tsl], W2[:, fc, nh * 512:(nh + 1) * 512], fc == 0, fc == 31)
                        kb.tt("dve", xo_[:, nh * 512:(nh + 1) * 512], p, xt4[tt_][:, nh * 512:(nh + 1) * 512], ALU.add)
                    kb.dma(DR(xdst[tix * 128:(tix + 1) * 128, :]), xo_)
            kb.barrier()
    return nc


def _host_params(inp, L):
    f = lambda a: np.ascontiguousarray(np.asarray(a, dtype=np.float32))
    pk = np.zeros((DEPTH, 128, PKW), np.float32)
    rowp = np.zeros((DEPTH, 320), np.float32)
    wgp = np.zeros((DEPTH, 17, 256), np.float32)
    bt = np.zeros((DEPTH, 2, 128, 16, 128), np.float32)
    ct = np.zeros((DEPTH, 2, 128, 16, 128), np.float32)
    for l in range(DEPTH):
        pk[l, :, PK_G1:PK_G1 + 8] = f(inp["norm1_g"])[l].reshape(8, 128).T
        pk[l, :, PK_G2:PK_G2 + 8] = f(inp["norm2_g"])[l].reshape(8, 128).T
        pk[l, :, PK_QNG:PK_QNG + 3] = f(inp["mla_q_norm_g"])[l].reshape(3, 128).T
        pk[l, :, PK_KVNG:PK_KVNG + 2] = f(inp["mla_kv_norm_g"])[l].reshape(2, 128).T

        def st16(a):
            return a.reshape(16, 2, 64).transpose(1, 2, 0).reshape(128, 16)

        pk[l, :, PK_LRE:PK_LRE + 16] = st16(f(inp["s5_lam_re"])[l])
        pk[l, :, PK_LIM:PK_LIM + 16] = st16(f(inp["s5_lam_im"])[l])
        pk[l, :, PK_LDT:PK_LDT + 16] = st16(np.repeat(f(inp["s5_log_dt"])[l][:, None], 64, axis=1))
        pk[l, :, PK_D5:PK_D5 + 4] = f(inp["s5_d"])[l].reshape(4, 128).T
        pk[l, :, PK_BGLU:PK_BGLU + 4] = f(inp["s5_b_glu"])[l].reshape(4, 128).T
        pk[l, :, PK_GB:PK_GB + 24] = f(inp["gate_b"])[l].reshape(24, 128).T
        rowp[l, 0:96] = f(inp["mla_q_head_g"])[l]
        rowp[l, 96:192] = f(inp["mla_k_head_g"])[l]
        rowp[l, 192:320] = f(inp["gla_out_g"])[l]
        wgp[l, 0:16] = f(inp["gla_w_gate"])[l]
        wgp[l, 16] = f(inp["gla_b_gate"])[l]
        for ri, (bk, ck) in enumerate((("s5_b_re", "s5_c_re"), ("s5_b_im", "s5_c_im"))):
            B = f(inp[bk])[l]
            C = f(inp[ck])[l]
            for j in range(16):
                c, q = j // 4, j % 4
                for gl in range(2):
                    g = 2 * j + gl
                    bt[l, ri, 32 * q + 16 * gl:32 * q + 16 * gl + 16, j, 64 * gl:64 * gl + 64] = B[g].T
                    ct[l, ri, 64 * gl:64 * gl + 64, j, 32 * q + 16 * gl:32 * q + 16 * gl + 16] = C[g].T
    NT = L // 128
    consts = np.zeros((128, C_POS + NT), np.float32)
    consts[:, C_ID:C_ID + 128] = np.eye(128, dtype=np.float32)
    consts[:, C_TRIU:C_TRIU + 128] = np.triu(np.ones((128, 128), np.float32))
    consts[:, C_TRIL:C_TRIL + 128] = np.tril(np.ones((128, 128), np.float32), -1)
    consts[:, C_TAU:C_TAU + 128] = np.arange(1, 129, dtype=np.float32)[None, :]
    consts[:, C_INVF:C_INVF + 16] = (10000.0 ** (-np.arange(0, 32, 2, dtype=np.float32) / 32.0)).astype(np.float32)[None, :]
    consts[:, C_POS:C_POS + NT] = (np.arange(NT, dtype=np.float32)[None, :] * 128.0 + np.arange(128, dtype=np.float32)[:, None])
    return dict(pk=pk, rowp=rowp, wg=wgp, bt=bt, ct=ct, consts=consts)


def make_in_maps(inp, L, n_cores):
    f = lambda a: np.ascontiguousarray(np.asarray(a, dtype=np.float32))
    shared = dict(
        w_in=f(inp["w_in"]), w_uq=f(inp["mla_w_uq"]), w_ukv=f(inp["mla_w_ukv"]), w_glu=f(inp["s5_w_glu"]),
        w_br0=f(inp["w_br_mla"]), w_br1=f(inp["w_br_gla"]), w_br2=f(inp["w_br_s5"]), w_out=f(inp["w_out"]),
        w_ff1=f(inp["w_ff1"]), w_ff2=f(inp["w_ff2"]))
    shared.update(_host_params(inp, L))
    x = f(inp["x"])
    nb = x.shape[0]
    maps = []
    for c in range(n_cores):
        m = dict(shared)
        m["x"] = np.ascontiguousarray(x[c % nb, :L])
        maps.append(m)
    return maps


def kernel(**inputs):
    L = SEQ
    nc = build_nc(L)
    maps = make_in_maps(inputs, L, BATCH)
    res = run_bass_kernel_spmd(nc, maps, core_ids=list(range(BATCH)))
    out = np.stack([np.asarray(res.results[b]["out"], dtype=np.float32) for b in range(BATCH)], axis=0)
    return out


def build_nc(L, dbg=False, nlayers=DEPTH, stop=None):
    return build(L, dbg, nlayers, stop)
```

```python
import math
import os
from contextlib import ExitStack
import numpy as np
import concourse.bass as bass
import concourse.mybir as mybir
from concourse.bass_utils import run_bass_kernel_spmd

F32 = mybir.dt.float32
BF16 = mybir.dt.bfloat16
AF = mybir.ActivationFunctionType
ALU = mybir.AluOpType
AX = mybir.AxisListType

D = 1024
DEPTH = 2
SEQ = 8192
BATCH = 4
EPS = 1e-6
MAGIC = 12582912.0
TWO_PI = 2.0 * math.pi

PK_G1, PK_G2, PK_QNG, PK_KVNG, PK_LRE, PK_LIM, PK_LDT, PK_D5, PK_BGLU, PK_GB = 0, 8, 16, 19, 21, 37, 53, 69, 73, 77
PKW = 101
C_ID, C_TRIU, C_TRIL, C_TAU, C_INVF, C_POS = 0, 128, 256, 384, 512, 528


class Res:
    __slots__ = ("w", "r")

    def __init__(self):
        self.w = None
        self.r = {}


class TV:
    def __init__(self, ap, res=None):
        self.ap = ap
        self.res = res if res is not None else Res()

    def __getitem__(self, k):
        return TV(self.ap[k], self.res)

    def v(self, f):
        return TV(f(self.ap), self.res)

    def sub(self, k):
        return TV(self.ap[k], Res())


class KB:
    def __init__(self, nc):
        self.nc = nc
        self.eng = {"pe": nc.tensor, "act": nc.scalar, "dve": nc.vector, "pool": nc.gpsimd, "sp": nc.sync}
        self.sems = {}
        self.cnt = {}
        for e in ("pe", "act", "dve", "pool"):
            self.sems[e] = nc.alloc_semaphore("c_" + e)
            self.cnt[e] = 0
        self.dq = {}
        for q, n in (("sp", 16), ("act", 4), ("pool", 4)):
            keys = []
            for i in range(n):
                k = "d_%s%d" % (q, i)
                self.sems[k] = nc.alloc_semaphore(k)
                self.cnt[k] = 0
                keys.append(k)
            self.dq[q] = [keys, 0]
        self.seen = {e: {} for e in self.eng}
        self.rr = 0

    def _deps(self, reads, writes):
        need = {}

        def add(k, v):
            if need.get(k, 0) < v:
                need[k] = v

        for R in reads:
            if R.w is not None:
                add(*R.w)
        for R in writes:
            if R.w is not None:
                add(*R.w)
            for k, v in R.r.items():
                add(k, v)
        return need

    def _wait(self, e, need):
        for k, v in need.items():
            if e == "pe" and k == "pe":
                continue
            if self.seen[e].get(k, 0) >= v:
                continue
            self.eng[e].wait_ge(self.sems[k], v)
            self.seen[e][k] = v

    def op(self, e, fn, reads, writes):
        reads = [t.res for t in reads]
        writes = [t.res for t in writes]
        self._wait(e, self._deps(reads, writes))
        ins = fn()
        self.cnt[e] += 1
        ins.then_inc(self.sems[e], 1)
        c = self.cnt[e]
        for R in writes:
            R.w = (e, c)
            R.r = {}
        for R in reads:
            R.r[e] = c
        return ins

    def dma(self, out, in_, q="sp"):
        keys, idx = self.dq[q]
        k = keys[idx]
        self.dq[q][1] = (idx + 1) % len(keys)
        need = self._deps([in_.res], [out.res])
        if self.cnt[k] > 0:
            need[k] = max(need.get(k, 0), self.cnt[k])
        self._wait(q, need)
        ins = self.eng[q].dma_start(out=out.ap, in_=in_.ap)
        self.cnt[k] += 16
        ins.then_inc(self.sems[k], 16)
        c = self.cnt[k]
        out.res.w = (k, c)
        out.res.r = {}
        in_.res.r[k] = c

    def barrier(self):
        for e in self.eng:
            for k, v in self.cnt.items():
                if v > 0 and self.seen[e].get(k, 0) < v:
                    self.eng[e].wait_ge(self.sems[k], v)
                    self.seen[e][k] = v

    def _ve(self, e):
        return self.eng[e]

    def tt(self, e, out, a, b, op):
        return self.op(e, lambda: self._ve(e).tensor_tensor(out=out.ap, in0=a.ap, in1=b.ap, op=op), [a, b], [out])

    def ts(self, e, out, a, s1, op0, s2=None, op1=None):
        rd = [a]
        s1a, s2a = s1, s2
        if isinstance(s1, TV):
            rd.append(s1)
            s1a = s1.ap
        if isinstance(s2, TV):
            rd.append(s2)
            s2a = s2.ap
        if op1 is None:
            return self.op(e, lambda: self._ve(e).tensor_scalar(out=out.ap, in0=a.ap, scalar1=s1a, scalar2=None, op0=op0), rd, [out])
        return self.op(e, lambda: self._ve(e).tensor_scalar(out=out.ap, in0=a.ap, scalar1=s1a, scalar2=s2a, op0=op0, op1=op1), rd, [out])

    def stt(self, out, a, sc, b, op0, op1):
        rd = [a, b]
        sca = sc
        if isinstance(sc, TV):
            rd.append(sc)
            sca = sc.ap
        return self.op("dve", lambda: self.nc.vector.scalar_tensor_tensor(out=out.ap, in0=a.ap, scalar=sca, in1=b.ap, op0=op0, op1=op1), rd, [out])

    def copy(self, e, out, a):
        if e == "act":
            return self.op(e, lambda: self.nc.scalar.copy(out=out.ap, in_=a.ap), [a], [out])
        return self.op(e, lambda: self._ve(e).tensor_copy(out=out.ap, in_=a.ap), [a], [out])

    def memset(self, e, out, val):
        return self.op(e, lambda: self._ve(e).memset(out.ap, val), [], [out])

    def act(self, out, a, func, bias=None, scale=None, accum=None):
        rd = [a]
        wr = [out]
        kw = {}
        if bias is not None:
            if isinstance(bias, TV):
                rd.append(bias)
                kw["bias"] = bias.ap
            else:
                kw["bias"] = bias
        if scale is not None:
            if isinstance(scale, TV):
                rd.append(scale)
                kw["scale"] = scale.ap
            else:
                kw["scale"] = scale
        if accum is not None:
            wr.append(accum)
            kw["accum_out"] = accum.ap
        return self.op("act", lambda: self.nc.scalar.activation(out=out.ap, in_=a.ap, func=func, **kw), rd, wr)

    def mm(self, out, lhsT, rhs, start, stop):
        return self.op("pe", lambda: self.nc.tensor.matmul(out.ap, lhsT=lhsT.ap, rhs=rhs.ap, start=start, stop=stop), [lhsT, rhs], [out])

    def tr(self, out, a, ident):
        return self.op("pe", lambda: self.nc.tensor.transpose(out.ap, a.ap, ident.ap), [a, ident], [out])

    def red(self, out, a, op=ALU.add):
        return self.op("dve", lambda: self.nc.vector.tensor_reduce(out=out.ap, in_=a.ap, axis=AX.X, op=op), [a], [out])

    def scan(self, out, d0, d1, init, op0=ALU.mult, op1=ALU.add):
        rd = [d0, d1]
        ia = init
        if isinstance(init, TV):
            rd.append(init)
            ia = init.ap
        return self.op("dve", lambda: self.nc.vector.tensor_tensor_scan(out=out.ap, data0=d0.ap, data1=d1.ap, initial=ia, op0=op0, op1=op1), rd, [out])

    def recip(self, out, a):
        return self.op("dve", lambda: self.nc.vector.reciprocal(out=out.ap, in_=a.ap), [a], [out])


class Rot:
    def __init__(self, items):
        self.items = items
        self.i = 0

    def next(self):
        t = self.items[self.i]
        self.i = (self.i + 1) % len(self.items)
        return t


def build(L, dbg=False, nlayers=DEPTH, stop=None):
    nc = bass.Bass("TRN2", target_bir_lowering=False)
    kb = KB(nc)
    NT = L // 128
    NB = L // 512

    def din(name, shape, dt=F32):
        return nc.dram_tensor(name, list(shape), dt, kind="ExternalInput").ap()

    def dscr(name, shape, dt):
        if dbg:
            return nc.dram_tensor(name, list(shape), dt, kind="ExternalOutput").ap()
        return nc.dram_tensor(name, list(shape), dt).ap()

    x_in = din("x", [L, D])
    w_in = din("w_in", [DEPTH, 1024, 5808])
    w_uq = din("w_uq", [DEPTH, 384, 768])
    w_ukv = din("w_ukv", [DEPTH, 256, 1024])
    w_glu = din("w_glu", [DEPTH, 512, 512])
    w_br = [din("w_br%d" % i, [DEPTH, 512, 1024]) for i in range(3)]
    w_out = din("w_out", [DEPTH, 1024, 1024])
    w_ff1 = din("w_ff1", [DEPTH, 1024, 4096])
    w_ff2 = din("w_ff2", [DEPTH, 4096, 1024])
    pk = din("pk", [DEPTH, 128, PKW])
    rowp = din("rowp", [DEPTH, 320])
    wg = din("wg", [DEPTH, 17, 256])
    btd = din("bt", [DEPTH, 2, 128, 16, 128])
    ctd = din("ct", [DEPTH, 2, 128, 16, 128])
    CW = C_POS + NT
    consts = din("consts", [128, CW])

    QT = dscr("QT", [8, 96, L], BF16)
    KT = dscr("KT", [8, 96, L], BF16)
    Vd = dscr("Vd", [8, 128, NT, 64], BF16)
    oaT = dscr("oaT", [512, L], BF16)
    obT = dscr("obT", [512, L], BF16)
    ocT = dscr("ocT", [512, L], BF16)
    x1d = dscr("x1", [L, D], F32)
    xmd = dscr("xm", [L, D], F32)
    outd = nc.dram_tensor("out", [L, D], F32, kind="ExternalOutput").ap()

    def DR(ap):
        return TV(ap)

    def sb(name, shape, dt=F32):
        return TV(nc.alloc_sbuf_tensor(name, list(shape), dt).ap())

    PS = [TV(nc.alloc_psum_tensor("ps%d" % i, [128, 512], F32).ap()) for i in range(8)]
    psrot = Rot(PS)

    def psb(p):
        return p.v(lambda a: a.bitcast(BF16))

    cst = sb("cst", [128, CW])
    kb.dma(cst, DR(consts))
    identf = cst[:, C_ID:C_ID + 128]
    identb = sb("identb", [128, 128], BF16)
    kb.copy("dve", identb, identf)
    trimb = sb("trimb", [128, 128], BF16)
    kb.copy("dve", trimb, cst[:, C_TRIU:C_TRIU + 128])
    triU = cst[:, C_TRIU:C_TRIU + 128]
    triUs = sb("triUs", [128, 128])
    triLs = sb("triLs", [128, 128])
    kb.ts("dve", triUs, cst[:, C_TRIU:C_TRIU + 128], -1.0 / 16.0, ALU.mult)
    kb.ts("dve", triLs, cst[:, C_TRIL:C_TRIL + 128], -1.0 / 16.0, ALU.mult)
    onesb = sb("onesb", [128, 128], BF16)
    kb.memset("dve", onesb, 1.0)
    mhalf = sb("mhalf", [128, 16])
    kb.memset("dve", mhalf, -0.5)

    def sincos(alloc_fn, ang, n, sin_out, cos_out, tag):
        t0 = alloc_fn("sc0" + tag, [128, n], F32)
        t1 = alloc_fn("sc1" + tag, [128, n], F32)
        for off, dst in ((0.0, sin_out), (0.25, cos_out)):
            kb.ts("dve", t0, ang, 1.0 / TWO_PI, ALU.mult, off, ALU.add)
            kb.ts("dve", t1, t0, MAGIC, ALU.add)
            kb.ts("dve", t1, t1, MAGIC, ALU.subtract)
            kb.tt("dve", t0, t0, t1, ALU.subtract)
            kb.act(dst, t0, AF.Sin, scale=TWO_PI * (1.0 - 1e-6))

    ropec = sb("ropec", [128, NT * 16])
    ropes = sb("ropes", [128, NT * 16])
    es0 = ExitStack()

    def sb0(name, shape, dt=F32):
        return TV(es0.enter_context(nc.sbuf_tensor(name, list(shape), dt)).ap())

    rang = sb0("rang", [128, NT * 16])
    kb.tt("dve", rang.v(lambda a: a.rearrange("p (t i) -> p t i", i=16)),
          cst[:, C_POS:C_POS + NT].v(lambda a: a.unsqueeze(2).to_broadcast([128, NT, 16])),
          cst[:, C_INVF:C_INVF + 16].v(lambda a: a.unsqueeze(1).to_broadcast([128, NT, 16])), ALU.mult)
    sincos(sb0, rang, NT * 16, ropes, ropec, "r")
    kb.barrier()
    es0.close()
    ropec3 = ropec.v(lambda a: a.rearrange("p (t i) -> p t i", i=16))
    ropes3 = ropes.v(lambda a: a.rearrange("p (t i) -> p t i", i=16))

    stage = Rot([sb("stg%d" % i, [128, 8, 256]) for i in range(2)])
    engcyc = Rot(["dve", "pool", "act"])

    def load_w(dst, dcol0, src2d, c0, c1, KC, scale=None):
        for cc in range(c0, c1, 256):
            ce = min(cc + 256, c1)
            n = ce - cc
            st = stage.next()
            kb.dma(st[:, 0:KC, 0:n], DR(src2d[:, cc:ce].rearrange("(kc p) n -> p kc n", p=128)))
            d0 = dcol0 + (cc - c0)
            if scale is None:
                kb.copy(engcyc.next(), dst[:, :, d0:d0 + n], st[:, 0:KC, 0:n])
            else:
                for kc in range(KC):
                    e = engcyc.next()
                    if e == "act":
                        kb.act(dst[:, kc, d0:d0 + n], st[:, kc, 0:n], AF.Copy, scale=scale[:, kc:kc + 1])
                    else:
                        kb.ts(e, dst[:, kc, d0:d0 + n], st[:, kc, 0:n], scale[:, kc:kc + 1], ALU.mult)

    def rstd_of(out, ss, n, cols):
        kb.ts("pool", out, ss, 1.0 / n, ALU.mult, EPS, ALU.add)
        kb.tt("pool", out, out, mhalf[:, 0:cols], ALU.pow)

    def norm_transpose(x_t, hT_dst, tcols, pool_tiles, rot=None):
        junk, ssr, hb = pool_tiles
        kb.act(junk, x_t, AF.Square, accum=ssr[:, 0:1])
        rstd_of(ssr[:, 1:2], ssr[:, 0:1], 1024.0, 1)
        kb.ts("dve", hb, x_t, ssr[:, 1:2], ALU.mult)
        p = (rot or psrot).next()
        pb = psb(p)
        for kc in range(8):
            kb.tr(pb[:, kc * 128:(kc + 1) * 128], hb[:, kc * 128:(kc + 1) * 128], identb)
        kb.copy("act", hT_dst[:, :, tcols], pb.v(lambda a: a.rearrange("p (k t) -> p k t", k=8)))

    if stop == '0':
        kb.barrier()
        return nc
    for l in range(nlayers):
        xsrc = x_in if l == 0 else xmd
        xdst = xmd if l == 0 else outd

        with ExitStack() as es:
            def A(name, shape, dt=F32, es=es):
                return TV(es.enter_context(nc.sbuf_tensor("A%d_%s" % (l, name), list(shape), dt)).ap())

            pkt = A("pk", [128, PKW])
            kb.dma(pkt, DR(pk[l]))
            rows = A("rows", [128, 320])
            kb.dma(rows, DR(rowp[l].partition_broadcast(128)))
            gq_s = A("gqs", [128, 96])
            kb.ts("dve", gq_s, rows[:, 0:96], 96.0 ** -0.5, ALU.mult)
            gk_r = rows[:, 96:192]
            og_h = A("ogh", [128, 128])
            kb.ts("dve", og_h, rows[:, 192:320], 0.5, ALU.mult)
            wga = A("wga", [32, 256])
            kb.dma(wga[0:17, :], DR(wg[l]))
            WAf = A("WAf", [128, 8, 1168], BF16)
            WAt = A("WAt", [128, 8, 1312], BF16)
            g1p = pkt[:, PK_G1:PK_G1 + 8]
            wl = w_in[l]
            for (c0, c1, d0) in ((0, 384, 0), (384, 640, 384), (672, 928, 640), (928, 1184, 896), (1696, 1712, 1152)):
                load_w(WAf, d0, wl, c0, c1, 8, g1p)
            for (c0, c1, d0) in ((640, 672, 0), (928, 1184, 32), (1184, 1696, 288), (1712, 2224, 800)):
                load_w(WAt, d0, wl, c0, c1, 8, g1p)
            if stop == 'A1':
                kb.barrier()
                return nc
            Wuq = A("Wuq", [128, 3, 768], BF16)
            load_w(Wuq, 0, w_uq[l], 0, 768, 3, pkt[:, PK_QNG:PK_QNG + 3])
            Wukv = A("Wukv", [128, 2, 1024], BF16)
            load_w(Wukv, 0, w_ukv[l], 0, 1024, 2, pkt[:, PK_KVNG:PK_KVNG + 2])
            Sst = [A("S%d" % i, [128, 128]) for i in range(2)]
            for s_ in Sst:
                kb.memset("dve", s_, 0.0)

            if stop == 'A2':
                kb.barrier()
                return nc
            xpool = Rot([A("x%d" % i, [128, 1024]) for i in range(2)])
            junk = A("junk", [128, 1024], BF16)
            ssr = Rot([A("ssr%d" % i, [128, 2]) for i in range(2)])
            hb = A("hb", [128, 1024], BF16)
            hT = A("hT", [128, 8, 512], BF16)
            cqT = A("cqT", [128, 3, 512], BF16)
            ckvT = A("ckvT", [128, 2, 512], BF16)
            sqT = A("sqT", [128, 5, 512], BF16)
            gqT = A("gqT", [128, 2, 512])
            gkT = A("gkT", [128, 2, 512])
            glrT = A("glrT", [32, 512])
            kb.memset("dve", glrT, 1.0)
            kk_R = Rot([A("kk%d" % i_, [128, 288]) for i_ in range(2)])
            gv_R = Rot([A("gv%d" % i_, [128, 512]) for i_ in range(2)])
            sr_R = Rot([A("sr%d" % i_, [128, 512]) for i_ in range(2)])
            st2_R = Rot([A("st2%d" % i_, [128, 4]) for i_ in range(2)])
            q32_R = Rot([A("q32%d" % i_, [128, 768]) for i_ in range(1)])
            k96_R = Rot([A("k96%d" % i_, [128, 768]) for i_ in range(1)])
            kv32_R = Rot([A("kv32%d" % i_, [128, 1024]) for i_ in range(1)])
            sq768_R = Rot([A("sq768%d" % i_, [128, 768]) for i_ in range(2)])
            ssg_R = Rot([A("ssg%d" % i_, [128, 16]) for i_ in range(2)])
            ssh_R = Rot([A("ssh%d" % i_, [128, 16]) for i_ in range(2)])
            nrm_R = Rot([A("nrm%d" % i_, [128, 768]) for i_ in range(2)])
            rtmp_R = Rot([A("rtmp%d" % i_, [128, 8, 16]) for i_ in range(2)])
            rtmp2_R = Rot([A("rtmp2%d" % i_, [128, 8, 16]) for i_ in range(2)])
            qb_R = Rot([A("qb%d" % i_, [128, 768], BF16) for i_ in range(2)])
            qTb = A("qTb", [96, 8, 512], BF16)
            kTb = A("kTb", [96, 8, 512], BF16)
            vb = A("vb", [128, 8, 4, 64], BF16)
            lsp_R = Rot([A("lsp%d" % i_, [128, 256]) for i_ in range(2)])
            eq_R = Rot([A("eq%d" % i_, [128, 256]) for i_ in range(2)])
            ek_R = Rot([A("ek%d" % i_, [128, 256]) for i_ in range(2)])
            eend_R = Rot([A("eend%d" % i_, [128, 256]) for i_ in range(2)])
            qtT_R = Rot([A("qtT%d" % i_, [128, 2, 128]) for i_ in range(2)])
            ktT_R = Rot([A("ktT%d" % i_, [128, 2, 128]) for i_ in range(2)])
            kend_R = Rot([A("kend%d" % i_, [128, 256]) for i_ in range(2)])
            Am_R = Rot([A("Am%d" % i_, [128, 128]) for i_ in range(2)])
            o32_R = Rot([A("o32%d" % i_, [128, 512]) for i_ in range(1)])
            osq_R = Rot([A("osq%d" % i_, [128, 512]) for i_ in range(1)])
            ob_R = Rot([A("ob%d" % i_, [128, 512], BF16) for i_ in range(2)])
            obTb = A("obTb", [128, 4, 512], BF16)
            def headnorm_rope(src, g_rep, tix, dstT, tcols):
                sq768 = sq768_R.next()
                ssh = ssh_R.next()
                nrm = nrm_R.next()
                rtmp = rtmp_R.next()
                rtmp2 = rtmp2_R.next()
                qb = qb_R.next()
                s3 = src.v(lambda a: a.rearrange("p (h d) -> p h d", h=8))
                kb.tt("pool", sq768, src, src, ALU.mult)
                kb.red(ssh[:, 0:8], sq768.v(lambda a: a.rearrange("p (h d) -> p h d", h=8)))
                rstd_of(ssh[:, 8:16], ssh[:, 0:8], 96.0, 8)
                n3 = nrm.v(lambda a: a.rearrange("p (h d) -> p h d", h=8))
                kb.tt("dve", n3, s3, ssh[:, 8:16].v(lambda a: a.unsqueeze(2).to_broadcast([128, 8, 96])), ALU.mult)
                kb.tt("dve", n3, n3, g_rep.v(lambda a: a.unsqueeze(1).to_broadcast([128, 8, 96])), ALU.mult)
                b3 = qb.v(lambda a: a.rearrange("p (h d) -> p h d", h=8))
                kb.copy("pool", b3[:, :, 0:64], n3[:, :, 0:64])
                cs = ropec3[:, tix, :].v(lambda a: a.unsqueeze(1).to_broadcast([128, 8, 16]))
                sn = ropes3[:, tix, :].v(lambda a: a.unsqueeze(1).to_broadcast([128, 8, 16]))
                x1_, x2_ = n3[:, :, 64:80], n3[:, :, 80:96]
                kb.tt("dve", rtmp, x1_, cs, ALU.mult)
                kb.tt("dve", rtmp2, x2_, sn, ALU.mult)
                kb.tt("dve", b3[:, :, 64:80], rtmp, rtmp2, ALU.subtract)
                kb.tt("dve", rtmp, x1_, sn, ALU.mult)
                kb.tt("dve", rtmp2, x2_, cs, ALU.mult)
                kb.tt("dve", b3[:, :, 80:96], rtmp, rtmp2, ALU.add)
                p = psrot.next()
                pb = psb(p)
                for h in range(8):
                    kb.tr(pb[0:96, h * 128:(h + 1) * 128], qb[:, h * 96:(h + 1) * 96], identb)
                kb.copy("act", dstT[:, :, tcols], pb[0:96, :].v(lambda a: a.rearrange("p (h t) -> p h t", h=8)))

            for blk in range(NB):
                bsl = slice(blk * 512, (blk + 1) * 512)
                for tt_ in range(4):
                    tix = blk * 4 + tt_
                    xt = xpool.next()
                    kb.dma(xt, DR(xsrc[tix * 128:(tix + 1) * 128, :]))
                    norm_transpose(xt, hT, slice(tt_ * 128, (tt_ + 1) * 128), (junk, ssr.next(), hb))
                if stop == 'A3':
                    kb.barrier()
                    return nc
                fm = [(0, 128, ("cq", 0)), (128, 128, ("cq", 1)), (256, 128, ("cq", 2)), (384, 128, ("ckv", 0)), (512, 128, ("ckv", 1)),
                      (640, 128, ("gq", 0)), (768, 128, ("gq", 1)), (896, 128, ("gk", 0)), (1024, 128, ("gk", 1)), (1152, 16, ("glr", 0))]
                import os
                for (c0, M, (kind, ci)) in fm[:int(os.environ.get('FMN', '99'))]:
                    p = psrot.next()
                    for kc in range(8):
                        kb.mm(p[0:M, :], WAf[:, kc, c0:c0 + M], hT[:, kc, :], kc == 0, kc == 7)
                    if kind == "cq":
                        kb.copy("dve", cqT[:, ci, :], p)
                        kb.tt("pool", sqT[:, ci, :], cqT[:, ci, :], cqT[:, ci, :], ALU.mult)
                    elif kind == "ckv":
                        kb.copy("dve", ckvT[:, ci, :], p)
                        kb.tt("pool", sqT[:, 3 + ci, :], ckvT[:, ci, :], ckvT[:, ci, :], ALU.mult)
                    elif kind == "gq":
                        kb.copy("act", gqT[:, ci, :], p)
                    elif kind == "gk":
                        kb.copy("dve", gkT[:, ci, :], p)
                    else:
                        kb.copy("act", glrT[0:16, :], p[0:16, :])

                if stop == 'A4':
                    kb.barrier()
                    return nc
                for tt_ in range(4):
                    tix = blk * 4 + tt_
                    tsl = slice(tt_ * 128, (tt_ + 1) * 128)
                    kk = kk_R.next()
                    gv = gv_R.next()
                    sr = sr_R.next()
                    st2 = st2_R.next()
                    q32 = q32_R.next()
                    k96 = k96_R.next()
                    kv32 = kv32_R.next()
                    lsp = lsp_R.next()
                    eq = eq_R.next()
                    ek = ek_R.next()
                    eend = eend_R.next()
                    qtT = qtT_R.next()
                    ktT = ktT_R.next()
                    kend = kend_R.next()
                    o32 = o32_R.next()
                    osq = osq_R.next()
                    ob = ob_R.next()
                    ssg = ssg_R.next()
                    pkk, pgv, pgr = psrot.next(), psrot.next(), psrot.next()
                    for (c0, n, p) in ((0, 288, pkk), (288, 512, pgv), (800, 512, pgr)):
                        for kc in range(8):
                            kb.mm(p[:, 0:n], hT[:, kc, tsl], WAt[:, kc, c0:c0 + n], kc == 0, kc == 7)
                    kb.copy("act", kk, pkk[:, 0:288])
                    kb.copy("dve", gv, pgv)
                    kb.act(sr, pgr, AF.Tanh, scale=0.5)
                    kb.stt(sr, sr, 1.0, pgr, ALU.add, ALU.mult)
                    if stop == 'A5':
                        kb.barrier()
                        return nc
                    pst = psrot.next()
                    for c in range(3):
                        kb.mm(pst[:, 0:1], sqT[:, c, tsl], onesb[:, 0:1], c == 0, c == 2)
                    for c in range(2):
                        kb.mm(pst[:, 1:2], sqT[:, 3 + c, tsl], onesb[:, 0:1], c == 0, c == 1)
                    kb.copy("dve", st2[:, 0:2], pst[:, 0:2])
                    rstd_of(st2[:, 2:3], st2[:, 0:1], 384.0, 1)
                    rstd_of(st2[:, 3:4], st2[:, 1:2], 256.0, 1)
                    pq0, pq1 = psrot.next(), psrot.next()
                    for (p, n0, n1) in ((pq0, 0, 512), (pq1, 512, 768)):
                        for c in range(3):
                            kb.mm(p[:, 0:n1 - n0], cqT[:, c, tsl], Wuq[:, c, n0:n1], c == 0, c == 2)
                    kb.act(q32[:, 0:512], pq0, AF.Copy, scale=st2[:, 2:3])
                    kb.act(q32[:, 512:768], pq1[:, 0:256], AF.Copy, scale=st2[:, 2:3])
                    headnorm_rope(q32, gq_s, tix, qTb, tsl)
                    if stop == 'A6':
                        kb.barrier()
                        return nc
                    pk0, pk1 = psrot.next(), psrot.next()
                    for (p, n0) in ((pk0, 0), (pk1, 512)):
                        for c in range(2):
                            kb.mm(p, ckvT[:, c, tsl], Wukv[:, c, n0:n0 + 512], c == 0, c == 1)
                    kb.act(kv32[:, 0:512], pk0, AF.Copy, scale=st2[:, 3:4])
                    kb.act(kv32[:, 512:1024], pk1, AF.Copy, scale=st2[:, 3:4])
                    kv3 = kv32.v(lambda a: a.rearrange("p (h d) -> p h d", h=8))
                    k3 = k96.v(lambda a: a.rearrange("p (h d) -> p h d", h=8))
                    kb.copy("pool", k3[:, :, 0:64], kv3[:, :, 0:64])
                    kb.copy("pool", k3[:, :, 64:96], kk[:, 0:32].v(lambda a: a.unsqueeze(1).to_broadcast([128, 8, 32])))
                    kb.copy("pool", vb[:, :, tt_, :], kv3[:, :, 64:128])
                    headnorm_rope(k96, gk_r, tix, kTb, tsl)

                    if stop == 'A7':
                        kb.barrier()
                        return nc
                    pl = psrot.next()
                    kb.mm(pl[:, 0:256], glrT[0:17, tsl], wga[0:17, :], True, True)
                    kb.act(lsp, pl[:, 0:256], AF.Exp, scale=-1.0)
                    kb.act(lsp, lsp, AF.Ln, bias=1.0)
                    pbc = psrot.next()
                    for hc in range(2):
                        kb.mm(pbc[:, hc * 128:(hc + 1) * 128], lsp[:, hc * 128:(hc + 1) * 128], triUs, True, True)
                    kb.mm(pbc[:, 256:512], triLs, lsp, True, True)
                    kb.act(eq, pbc[:, 0:256], AF.Exp)
                    kb.act(ek, pbc[:, 0:256], AF.Exp, scale=-1.0)
                    kb.act(eend, pbc[:, 256:512], AF.Exp)
                    for hc in range(2):
                        kb.stt(qtT[:, hc, :], gqT[:, hc, tsl], 0.125, eq[:, hc * 128:(hc + 1) * 128], ALU.mult, ALU.mult)
                        kb.tt("pool", ktT[:, hc, :], gkT[:, hc, tsl], ek[:, hc * 128:(hc + 1) * 128], ALU.mult)
                    kb.tt("pool", kend, kk[:, 32:288], eend, ALU.mult)
                    if stop == 'A8':
                        kb.barrier()
                        return nc
                    po = psrot.next()
                    pds = psrot.next()
                    for h in range(4):
                        hc, hb_ = h // 2, (h % 2) * 64
                        pa = psrot.next()
                        Am = Am_R.next()
                        kb.mm(pa[:, 0:128], ktT[hb_:hb_ + 64, hc, :], qtT[hb_:hb_ + 64, hc, :], True, True)
                        kb.tt("dve", Am, pa[:, 0:128], triU, ALU.mult)
                        kb.mm(po[:, h * 128:(h + 1) * 128], Am, gv[:, h * 128:(h + 1) * 128], True, False)
                        kb.mm(po[:, h * 128:(h + 1) * 128], qtT[hb_:hb_ + 64, hc, :], Sst[hc][hb_:hb_ + 64, :], False, True)
                        kb.mm(pds[hb_:hb_ + 64, hc * 128:(hc + 1) * 128], kend[:, h * 64:(h + 1) * 64], gv[:, h * 128:(h + 1) * 128], True, True)
                    for hc in range(2):
                        kb.stt(Sst[hc], Sst[hc], eq[:, hc * 128 + 127:hc * 128 + 128], pds[:, hc * 128:(hc + 1) * 128], ALU.mult, ALU.add)
                    if stop == 'A9':
                        kb.barrier()
                        return nc
                    kb.copy("act", o32, po)
                    kb.tt("pool", osq, o32, o32, ALU.mult)
                    kb.red(ssg[:, 0:4], osq.v(lambda a: a.rearrange("p (h e) -> p h e", h=4)))
                    rstd_of(ssg[:, 8:12], ssg[:, 0:4], 128.0, 4)
                    o3 = o32.v(lambda a: a.rearrange("p (h e) -> p h e", h=4))
                    kb.tt("dve", o3, o3, ssg[:, 8:12].v(lambda a: a.unsqueeze(2).to_broadcast([128, 4, 128])), ALU.mult)
                    kb.tt("dve", o3, o3, og_h.v(lambda a: a.unsqueeze(1).to_broadcast([128, 4, 128])), ALU.mult)
                    kb.tt("dve", ob, o32, sr, ALU.mult)
                    p = psrot.next()
                    pb = psb(p)
                    for c in range(4):
                        kb.tr(pb[:, c * 128:(c + 1) * 128], ob[:, c * 128:(c + 1) * 128], identb)
                    kb.copy("act", obTb[:, :, tsl], pb[:, 0:512].v(lambda a: a.rearrange("p (c t) -> p c t", c=4)))

                if stop == 'A10':
                    kb.barrier()
                    return nc
                kb.dma(DR(QT[:, :, bsl].rearrange("h d t -> d h t")), qTb)
                kb.dma(DR(KT[:, :, bsl].rearrange("h d t -> d h t")), kTb)
                kb.dma(DR(Vd.rearrange("h p t d -> p h t d")[:, :, blk * 4:(blk + 1) * 4, :]), vb)
                kb.dma(DR(obT[:, bsl].rearrange("(c p) t -> p c t", p=128)), obTb)
            kb.barrier()

        if stop == 'A':
            return nc
        with ExitStack() as es:
            def A(name, shape, dt=F32, es=es):
                return TV(es.enter_context(nc.sbuf_tensor("S%d_%s" % (l, name), list(shape), dt)).ap())

            srot = Rot(PS[5:8]) if os.environ.get("SROT") != "all" else psrot
            pkt = A("pk", [128, PKW])
            kb.dma(pkt, DR(pk[l]))
            WAs = A("WAs", [128, 8, 512], BF16)
            load_w(WAs, 0, w_in[l], 2224, 2736, 8, pkt[:, PK_G1:PK_G1 + 8])
            Wglu = A("Wglu", [128, 4, 512], BF16)
            load_w(Wglu, 0, w_glu[l], 0, 512, 4, None)
            BTr = A("BTr", [128, 16, 128], BF16)
            BTi = A("BTi", [128, 16, 128], BF16)
            CTr = A("CTr", [128, 16, 128], BF16)
            CTi = A("CTi", [128, 16, 128], BF16)
            for dst, src, sh in ((BTr, btd[l, 0], (16, 128)), (BTi, btd[l, 1], (16, 128)), (CTr, ctd[l, 0], (16, 128)), (CTi, ctd[l, 1], (16, 128))):
                st = stage.next()
                sv = st.v(lambda a: a.rearrange("p k n -> p (k n)"))[:, 0:sh[0] * sh[1]].v(lambda a: a.rearrange("p (k n) -> p k n", k=sh[0]))
                kb.dma(sv, DR(src))
                kb.copy("dve", dst, sv)

            lre = A("lre", [128, 16])
            kb.ts("dve", lre, pkt[:, PK_LRE:PK_LRE + 16], -1e-4, ALU.min)
            lim = pkt[:, PK_LIM:PK_LIM + 16]
            dtt = A("dtt", [128, 16])
            kb.act(dtt, pkt[:, PK_LDT:PK_LDT + 16], AF.Exp)
            th = A("th", [128, 16])
            kb.tt("dve", th, lim, dtt, ALU.mult)
            rr_ = A("rr", [128, 16])
            kb.tt("dve", rr_, lre, dtt, ALU.mult)
            kb.act(rr_, rr_, AF.Exp)
            sth = A("sth", [128, 16])
            cth = A("cth", [128, 16])
            sincos(A, th, 16, sth, cth, "t")
            nre = A("nre", [128, 16])
            nim = A("nim", [128, 16])
            kb.tt("dve", nre, rr_, cth, ALU.mult)
            kb.ts("dve", nre, nre, -1.0, ALU.add)
            kb.tt("dve", nim, rr_, sth, ALU.mult)
            den = A("den", [128, 16])
            tmp16 = A("tmp16", [128, 16])
            kb.tt("dve", den, lre, lre, ALU.mult)
            kb.tt("dve", tmp16, lim, lim, ALU.mult)
            kb.tt("dve", den, den, tmp16, ALU.add)
            kb.recip(den, den)
            cre = A("cre", [128, 16])
            cim = A("cim", [128, 16])
            kb.tt("dve", cre, nre, lre, ALU.mult)
            kb.tt("dve", tmp16, nim, lim, ALU.mult)
            kb.tt("dve", cre, cre, tmp16, ALU.add)
            kb.tt("dve", cre, cre, den, ALU.mult)
            kb.tt("dve", cim, nim, lre, ALU.mult)
            kb.tt("dve", tmp16, nre, lim, ALU.mult)
            kb.tt("dve", cim, cim, tmp16, ALU.subtract)
            kb.tt("dve", cim, cim, den, ALU.mult)
            E2c = A("E2c", [128, 2048])
            E2s = A("E2s", [128, 2048])
            E1r = A("E1r", [128, 2048])
            E1i = A("E1i", [128, 2048])
            Rz = A("Rz", [128, 2048])
            es2 = ExitStack()
            tang = A("tang", [128, 2048], es=es2)
            tau = cst[:, C_TAU:C_TAU + 128]
            for j in range(16):
                kb.ts("dve", tang[:, j * 128:(j + 1) * 128], tau, th[:, j:j + 1], ALU.mult)
            sincos(lambda n_, s_, d_: A(n_, s_, d_, es=es2), tang, 2048, E2s, E2c, "T")
            for j in range(16):
                sl = slice(j * 128, (j + 1) * 128)
                kb.ts("dve", tang[:, sl], E2s[:, sl], cim[:, j:j + 1], ALU.mult)
                kb.stt(E1r[:, sl], E2c[:, sl], cre[:, j:j + 1], tang[:, sl], ALU.mult, ALU.add)
                kb.ts("dve", tang[:, sl], E2s[:, sl], cre[:, j:j + 1], ALU.mult)
                kb.stt(E1i[:, sl], E2c[:, sl], cim[:, j:j + 1], tang[:, sl], ALU.mult, ALU.subtract)
                kb.ts("dve", Rz[:, sl], cst[:, C_TRIU + 127:C_TRIU + 128].v(lambda a: a.to_broadcast([128, 128])), rr_[:, j:j + 1], ALU.mult)
            Rz3 = Rz.v(lambda a: a.rearrange("p (j t) -> p j t", t=128))
            kb.memset("dve", Rz3[:, :, 0:1], 0.0)
            kb.barrier()
            es2.close()
            car_r = A("carr", [128, 16])
            car_i = A("cari", [128, 16])
            kb.memset("dve", car_r, 0.0)
            kb.memset("dve", car_i, 0.0)
            if stop == 'S1':
                kb.barrier()
                return nc
            xpool = Rot([A("x%d" % i, [128, 1024]) for i in range(2)])
            junk = A("junk", [128, 1024], BF16)
            ssr = Rot([A("ssr%d" % i, [128, 2]) for i in range(2)])
            hb = A("hb", [128, 1024], BF16)
            hT = A("hT", [128, 8, 512], BF16)
            uT = A("uT", [128, 4, 512])
            uTb = A("uTb", [128, 4, 512], BF16)
            t1_R = Rot([A("t1%d" % i_, [128, 1024]) for i_ in range(2)])
            t2_R = Rot([A("t2%d" % i_, [128, 1024]) for i_ in range(2)])
            btr_R = Rot([A("btr%d" % i_, [128, 1024]) for i_ in range(2)])
            bti_R = Rot([A("bti%d" % i_, [128, 1024]) for i_ in range(2)])
            xsr_R = Rot([A("xsr%d" % i_, [128, 1024]) for i_ in range(2)])
            xsi_R = Rot([A("xsi%d" % i_, [128, 1024]) for i_ in range(2)])
            xbr_R = Rot([A("xbr%d" % i_, [128, 1024], BF16) for i_ in range(2)])
            xbi_R = Rot([A("xbi%d" % i_, [128, 1024], BF16) for i_ in range(2)])
            rc_R = Rot([A("rc%d" % i_, [128, 16]) for i_ in range(2)])
            ctmp_R = Rot([A("ctmp%d" % i_, [128, 32]) for i_ in range(2)])
            y32 = A("y32", [128, 4, 512])
            yb = A("yb", [128, 4, 512], BF16)
            ocTb = A("ocTb", [128, 4, 512], BF16)
            g3 = A("g3", [128, 512])
            d5 = pkt[:, PK_D5:PK_D5 + 4]
            bglu = A("bgluh", [128, 4])
            kb.ts("dve", bglu, pkt[:, PK_BGLU:PK_BGLU + 4], 0.5, ALU.mult)

            for blk in range(NB):
                bsl = slice(blk * 512, (blk + 1) * 512)
                for tt_ in range(4):
                    tix = blk * 4 + tt_
                    xt = xpool.next()
                    kb.dma(xt, DR(xsrc[tix * 128:(tix + 1) * 128, :]))
                    norm_transpose(xt, hT, slice(tt_ * 128, (tt_ + 1) * 128), (junk, ssr.next(), hb), srot)
                if stop == 'S15':
                    kb.barrier()
                    return nc
                for ci in range(4):
                    p = srot.next()
                    for kc in range(8):
                        kb.mm(p, WAs[:, kc, ci * 128:(ci + 1) * 128], hT[:, kc, :], kc == 0, kc == 7)
                    if stop == 'S16':
                        kb.barrier()
                        return nc
                    kb.copy("dve", uT[:, ci, :], p)
                    kb.copy("pool", uTb[:, ci, :], uT[:, ci, :])
                if stop == 'S2':
                    kb.barrier()
                    return nc
                for tt_ in range(4):
                    tsl = slice(tt_ * 128, (tt_ + 1) * 128)
                    py = [PS[0]]
                    for half in range(2):
                        t1 = t1_R.next()
                        t2 = t2_R.next()
                        btr = btr_R.next()
                        bti = bti_R.next()
                        xsr = xsr_R.next()
                        xsi = xsi_R.next()
                        xbr = xbr_R.next()
                        xbi = xbi_R.next()
                        rc = rc_R.next()
                        ctmp = ctmp_R.next()
                        pbr, pbi, pbr2, pbi2 = PS[1], PS[2], PS[3], PS[4]
                        banks_r, banks_i = (pbr, pbr2), (pbi, pbi2)
                        for jj in range(8):
                            j = half * 8 + jj
                            c, q_ = j // 4, j % 4
                            rs = slice(32 * q_, 32 * q_ + 32)
                            kb.mm(banks_r[jj // 4][:, (jj % 4) * 128:(jj % 4 + 1) * 128], BTr[:, j, :], uTb[:, c, tsl], True, True)
                            kb.mm(banks_i[jj // 4][:, (jj % 4) * 128:(jj % 4 + 1) * 128], BTi[:, j, :], uTb[:, c, tsl], True, True)
                        hs = slice(half * 1024, (half + 1) * 1024)
                        for g_ in range(2):
                            gs = slice(g_ * 512, (g_ + 1) * 512)
                            hg = slice(half * 1024 + g_ * 512, half * 1024 + (g_ + 1) * 512)
                            kb.tt("dve", t1[:, gs], banks_r[g_], E1r[:, hg], ALU.mult)
                            kb.tt("dve", t2[:, gs], banks_i[g_], E1i[:, hg], ALU.mult)
                            kb.tt("dve", btr[:, gs], t1[:, gs], t2[:, gs], ALU.subtract)
                            kb.tt("dve", t1[:, gs], banks_i[g_], E1r[:, hg], ALU.mult)
                            kb.tt("dve", t2[:, gs], banks_r[g_], E1i[:, hg], ALU.mult)
                            kb.tt("pool", bti[:, gs], t1[:, gs], t2[:, gs], ALU.add)
                        if stop == 'S3':
                            kb.barrier()
                            return nc
                        js = slice(half * 8, half * 8 + 8)
                        kb.tt("dve", rc[:, 0:8], rr_[:, js], car_r[:, js], ALU.mult)
                        kb.tt("dve", rc[:, 8:16], rr_[:, js], car_i[:, js], ALU.mult)
                        b3r = btr.v(lambda a: a.rearrange("p (j t) -> p j t", t=128))
                        b3i = bti.v(lambda a: a.rearrange("p (j t) -> p j t", t=128))
                        kb.tt("dve", b3r[:, :, 0], b3r[:, :, 0], rc[:, 0:8], ALU.add)
                        kb.tt("dve", b3i[:, :, 0], b3i[:, :, 0], rc[:, 8:16], ALU.add)
                        kb.scan(xsr, Rz[:, hs], btr, 0.0)
                        kb.scan(xsi, Rz[:, hs], bti, 0.0)
                        if stop == 'S4':
                            kb.barrier()
                            return nc
                        x3r = xsr.v(lambda a: a.rearrange("p (j t) -> p j t", t=128))
                        x3i = xsi.v(lambda a: a.rearrange("p (j t) -> p j t", t=128))
                        e3c = E2c[:, hs].v(lambda a: a.rearrange("p (j t) -> p j t", t=128))
                        e3s = E2s[:, hs].v(lambda a: a.rearrange("p (j t) -> p j t", t=128))
                        kb.tt("dve", ctmp[:, 0:8], x3r[:, :, 127], e3c[:, :, 127], ALU.mult)
                        kb.tt("dve", ctmp[:, 8:16], x3i[:, :, 127], e3s[:, :, 127], ALU.mult)
                        kb.tt("dve", ctmp[:, 16:24], x3r[:, :, 127], e3s[:, :, 127], ALU.mult)
                        kb.tt("dve", ctmp[:, 24:32], x3i[:, :, 127], e3c[:, :, 127], ALU.mult)
                        kb.tt("dve", car_r[:, js], ctmp[:, 0:8], ctmp[:, 8:16], ALU.subtract)
                        kb.tt("dve", car_i[:, js], ctmp[:, 16:24], ctmp[:, 24:32], ALU.add)
                        if stop == 'S5':
                            kb.barrier()
                            return nc
                        kb.tt("pool", t1, xsr, E2c[:, hs], ALU.mult)
                        kb.tt("pool", t2, xsi, E2s[:, hs], ALU.mult)
                        kb.tt("dve", xbr, t1, t2, ALU.subtract)
                        kb.tt("dve", t1, xsi, E2c[:, hs], ALU.mult)
                        kb.tt("dve", t2, xsr, E2s[:, hs], ALU.mult)
                        kb.stt(xbi, t1, -1.0, t2, ALU.mult, ALU.subtract)
                        for jj in range(8):
                            j = half * 8 + jj
                            c, q_ = j // 4, j % 4
                            rs = slice(32 * q_, 32 * q_ + 32)
                            kb.mm(py[0][:, c * 128:(c + 1) * 128], CTr[:, j, :], xbr[:, jj * 128:(jj + 1) * 128], q_ == 0, False)
                            kb.mm(py[0][:, c * 128:(c + 1) * 128], CTi[:, j, :], xbi[:, jj * 128:(jj + 1) * 128], False, q_ == 3)
                    if stop == 'S6':
                        kb.barrier()
                        return nc
                    for c in range(4):
                        kb.stt(y32[:, c, tsl], uT[:, c, tsl], d5[:, c:c + 1], py[0][:, c * 128:(c + 1) * 128], ALU.mult, ALU.add)

                if stop == 'S7':
                    kb.barrier()
                    return nc
                for c in range(4):
                    yc = y32[:, c, :]
                    kb.tt("pool", g3, yc, yc, ALU.mult)
                    kb.ts("dve", g3, g3, 0.044715, ALU.mult, 1.0, ALU.add)
                    kb.tt("pool", g3, g3, yc, ALU.mult)
                    kb.act(g3, g3, AF.Tanh, scale=0.7978845608028654)
                    kb.stt(g3, g3, 1.0, yc, ALU.add, ALU.mult)
                    kb.ts("dve", yc, g3, 0.5, ALU.mult)
                    kb.copy("pool", yb[:, c, :], yc)
                if stop == 'S8':
                    kb.barrier()
                    return nc
                for mc in range(4):
                    p = srot.next()
                    for c in range(4):
                        kb.mm(p, Wglu[:, c, mc * 128:(mc + 1) * 128], yb[:, c, :], c == 0, c == 3)
                    kb.act(g3, p, AF.Tanh, bias=bglu[:, mc:mc + 1], scale=0.5)
                    kb.stt(g3, g3, 1.0, y32[:, mc, :], ALU.add, ALU.mult)
                    kb.ts("dve", ocTb[:, mc, :], g3, 0.5, ALU.mult)
                kb.dma(DR(ocT[:, bsl].rearrange("(c p) t -> p c t", p=128)), ocTb)
            kb.barrier()


        if stop == 'S':
            return nc
        with ExitStack() as es:
            def Bt(name, shape, dt=F32):
                return TV(es.enter_context(nc.sbuf_tensor("B%d_%s" % (l, name), list(shape), dt)).ap())

            KTh = Rot([Bt("KT%d" % i, [96, L], BF16) for i in range(2)])
            Vh = Rot([Bt("V%d" % i, [128, NT, 65], BF16) for i in range(2)])
            Vraw = Rot([Bt("Vr%d" % i, [128, NT, 64], BF16) for i in range(2)])
            QTh = Rot([Bt("QT%d" % i, [96, L], BF16) for i in range(2)])
            Pt = Rot([Bt("P%d" % i, [128, 512], BF16) for i in range(4)])
            orec = Bt("orec", [128, 4])
            onb = Rot([Bt("onb%d" % i, [128, 4, 64], BF16) for i in range(2)])
            oTs = Rot([Bt("oTs%d" % i, [64, 512], BF16) for i in range(2)])
            for V_ in Vh.items:
                kb.memset("dve", V_[:, :, 64:65], 1.0)
            SB_ = Rot(PS[0:3])
            OB_ = PS[3:7]
            for h in range(8):
                Kt_, V_, Q_ = KTh.next(), Vh.next(), QTh.next()
                kb.dma(Kt_, DR(KT[h]))
                kb.dma(Q_, DR(QT[h]))
                Vr_ = Vraw.next()
                kb.dma(Vr_, DR(Vd[h]))
                kb.copy("pool", V_[:, :, 0:64], Vr_)
                for qb_ in range(NB):
                    nk = 4 * (qb_ + 1)
                    pend = []

                    def emit_pv(kt, j, P_):
                        for qi in range(max(j, 0), 4):
                            last = (kt == 4 * qb_ + qi)
                            kb.mm(OB_[qi][:, 0:65], P_[:, qi * 128:(qi + 1) * 128], V_[:, kt, :], kt == 0, last)

                    for kt in range(nk):
                        j = kt - 4 * qb_
                        q0 = max(j, 0) * 128
                        ps_ = SB_.next()
                        kb.mm(ps_[:, q0:512], Kt_[:, kt * 128:(kt + 1) * 128], Q_[:, qb_ * 512 + q0:(qb_ + 1) * 512], True, True)
                        P_ = Pt.next()
                        kb.act(P_[:, q0:512], ps_[:, q0:512], AF.Exp)
                        if j >= 0:
                            kb.tt("dve", P_[:, j * 128:(j + 1) * 128], P_[:, j * 128:(j + 1) * 128], trimb, ALU.mult)
                        pend.append((kt, j, P_))
                        if len(pend) > 2:
                            emit_pv(*pend.pop(0))
                    while pend:
                        emit_pv(*pend.pop(0))
                    on_ = onb.next()
                    for qi in range(4):
                        kb.recip(orec[:, qi:qi + 1], OB_[qi][:, 64:65])
                        kb.ts("dve", on_[:, qi, :], OB_[qi][:, 0:64], orec[:, qi:qi + 1], ALU.mult)
                    pT = PS[7]
                    pTb = psb(pT)
                    for qi in range(4):
                        kb.tr(pTb[0:64, qi * 128:(qi + 1) * 128], on_[:, qi, :], identb)
                    oT_ = oTs.next()
                    kb.copy("act", oT_, pTb[0:64, 0:512])
                    kb.dma(DR(oaT[h * 64:(h + 1) * 64, qb_ * 512:(qb_ + 1) * 512]), oT_)
            kb.barrier()

        if stop == 'B':
            return nc
        with ExitStack() as es:
            def Ct(name, shape, dt=F32):
                return TV(es.enter_context(nc.sbuf_tensor("C%d_%s" % (l, name), list(shape), dt)).ap())

            pkt = Ct("pk", [128, PKW])
            kb.dma(pkt, DR(pk[l]))
            gbh = Ct("gbh", [128, 24])
            kb.ts("dve", gbh, pkt[:, PK_GB:PK_GB + 24], 0.5, ALU.mult)
            Wg = Ct("Wg", [128, 8, 3072], BF16)
            load_w(Wg, 0, w_in[l], 2736, 5808, 8, pkt[:, PK_G1:PK_G1 + 8])
            Wb = [Ct("Wb%d" % i, [128, 4, 1024], BF16) for i in range(3)]
            for i in range(3):
                load_w(Wb[i], 0, w_br[i][l], 0, 1024, 4, None)
            Wo = Ct("Wo", [128, 8, 1024], BF16)
            load_w(Wo, 0, w_out[l], 0, 1024, 8, None)
            xt4 = [Ct("x%d" % i, [128, 1024]) for i in range(4)]
            junk = Ct("junk", [128, 1024], BF16)
            ssr = Rot([Ct("ssr%d" % i, [128, 2]) for i in range(2)])
            hb = Ct("hb", [128, 1024], BF16)
            hT_R = Rot([Ct("hT%d" % i_, [128, 8, 512], BF16) for i_ in range(2)])
            oin = [Ct("oin%d" % i, [128, 4, 512], BF16) for i in range(3)]
            gsb = Rot([Ct("g%d" % i, [128, 512]) for i in range(3)])
            macc = Ct("macc", [128, 512])
            mtmp = Ct("mtmp", [128, 512])
            mT = Ct("mT", [128, 8, 512], BF16)
            xo = Rot([Ct("xo%d" % i, [128, 1024]) for i in range(2)])
            for blk in range(NB):
                hT = hT_R.next()
                bsl = slice(blk * 512, (blk + 1) * 512)
                for i, src in enumerate((oaT, obT, ocT)):
                    kb.dma(oin[i], DR(src[:, bsl].rearrange("(c p) t -> p c t", p=128)))
                for tt_ in range(4):
                    tix = blk * 4 + tt_
                    kb.dma(xt4[tt_], DR(xsrc[tix * 128:(tix + 1) * 128, :]))
                    norm_transpose(xt4[tt_], hT, slice(tt_ * 128, (tt_ + 1) * 128), (junk, ssr.next(), hb))
                for fc in range(8):
                    for b in range(3):
                        pg = psrot.next()
                        gc = b * 8 + fc
                        for kc in range(8):
                            kb.mm(pg, Wg[:, kc, gc * 128:(gc + 1) * 128], hT[:, kc, :], kc == 0, kc == 7)
                        g_ = gsb.next()
                        kb.act(g_, pg, AF.Tanh, bias=gbh[:, gc:gc + 1], scale=0.5)
                        pp = psrot.next()
                        for c in range(4):
                            kb.mm(pp, Wb[b][:, c, fc * 128:(fc + 1) * 128], oin[b][:, c, :], c == 0, c == 3)
                        if b == 0:
                            kb.stt(macc, g_, 1.0, pp, ALU.add, ALU.mult)
                        else:
                            kb.stt(mtmp, g_, 1.0, pp, ALU.add, ALU.mult)
                            kb.tt("pool", macc, macc, mtmp, ALU.add)
                    kb.ts("dve", mT[:, fc, :], macc, 0.5, ALU.mult)
                for tt_ in range(4):
                    tix = blk * 4 + tt_
                    tsl = slice(tt_ * 128, (tt_ + 1) * 128)
                    xo_ = xo.next()
                    for nh in range(2):
                        p = psrot.next()
                        for kc in range(8):
                            kb.mm(p, mT[:, kc, tsl], Wo[:, kc, nh * 512:(nh + 1) * 512], kc == 0, kc == 7)
                        kb.tt("dve", xo_[:, nh * 512:(nh + 1) * 512], p, xt4[tt_][:, nh * 512:(nh + 1) * 512], ALU.add)
                    kb.dma(DR(x1d[tix * 128:(tix + 1) * 128, :]), xo_)
            kb.barrier()

        if stop == 'C':
            return nc
        with ExitStack() as es:
            def Dt(name, shape, dt=F32):
                return TV(es.enter_context(nc.sbuf_tensor("D%d_%s" % (l, name), list(shape), dt)).ap())

            pkt = Dt("pk", [128, PKW])
            kb.dma(pkt, DR(pk[l]))
            W1 = Dt("W1", [128, 8, 4096], BF16)
            load_w(W1, 0, w_ff1[l], 0, 4096, 8, pkt[:, PK_G2:PK_G2 + 8])
            W2 = Dt("W2", [128, 32, 1024], BF16)
            for k0 in range(0, 32, 8):
                load_w(W2[:, k0:k0 + 8, :], 0, w_ff2[l][k0 * 128:(k0 + 8) * 128, :], 0, 1024, 8, None)
            xt8 = [Dt("x%d" % i, [128, 1024]) for i in range(4)]
            junk = Dt("junk", [128, 1024], BF16)
            ssr = Rot([Dt("ssr%d" % i, [128, 2]) for i in range(2)])
            hb = Dt("hb", [128, 1024], BF16)
            hT_R = Rot([Dt("hT%d" % i_, [128, 8, 256], BF16) for i_ in range(2)])
            uT = Dt("uT", [128, 32, 256], BF16)
            rl = Rot([Dt("rl%d" % i, [128, 256]) for i in range(2)])
            for blk in range(L // 256):
                hT = hT_R.next()
                xt4 = xt8[(blk % 2) * 2:(blk % 2) * 2 + 2]
                for tt_ in range(2):
                    tix = blk * 2 + tt_
                    kb.dma(xt4[tt_], DR(x1d[tix * 128:(tix + 1) * 128, :]))
                    norm_transpose(xt4[tt_], hT, slice(tt_ * 128, (tt_ + 1) * 128), (junk, ssr.next(), hb))
                for fc in range(32):
                    p = psrot.next()
                    for kc in range(8):
                        kb.mm(p[:, 0:256], W1[:, kc, fc * 128:(fc + 1) * 128], hT[:, kc, :], kc == 0, kc == 7)
                    r_ = rl.next()
                    kb.act(r_, p[:, 0:256], AF.Relu)
                    kb.tt("pool" if fc % 2 else "dve", uT[:, fc, :], r_, r_, ALU.mult)
                for tt_ in range(2):
                    tix = blk * 2 + tt_
                    tsl = slice(tt_ * 128, (tt_ + 1) * 128)
                    xo_ = xt4[tt_]
                    for nh in range(2):
                        p = psrot.next()
                        for fc in range(32):
                            kb.mm(p, uT[:, fc, tsl], W2[:, fc, nh * 512:(nh + 1) * 512], fc == 0, fc == 31)
                        kb.tt("dve", xo_[:, nh * 512:(nh + 1) * 512], p, xt4[tt_][:, nh * 512:(nh + 1) * 512], ALU.add)
                    kb.dma(DR(xdst[tix * 128:(tix + 1) * 128, :]), xo_)
            kb.barrier()
    return nc


def _host_params(inp, L):
    f = lambda a: np.ascontiguousarray(np.asarray(a, dtype=np.float32))
    pk = np.zeros((DEPTH, 128, PKW), np.float32)
    rowp = np.zeros((DEPTH, 320), np.float32)
    wgp = np.zeros((DEPTH, 17, 256), np.float32)
    bt = np.zeros((DEPTH, 2, 128, 16, 128), np.float32)
    ct = np.zeros((DEPTH, 2, 128, 16, 128), np.float32)
    for l in range(DEPTH):
        pk[l, :, PK_G1:PK_G1 + 8] = f(inp["norm1_g"])[l].reshape(8, 128).T
        pk[l, :, PK_G2:PK_G2 + 8] = f(inp["norm2_g"])[l].reshape(8, 128).T
        pk[l, :, PK_QNG:PK_QNG + 3] = f(inp["mla_q_norm_g"])[l].reshape(3, 128).T
        pk[l, :, PK_KVNG:PK_KVNG + 2] = f(inp["mla_kv_norm_g"])[l].reshape(2, 128).T

        def st16(a):
            return a.reshape(16, 2, 64).transpose(1, 2, 0).reshape(128, 16)

        pk[l, :, PK_LRE:PK_LRE + 16] = st16(f(inp["s5_lam_re"])[l])
        pk[l, :, PK_LIM:PK_LIM + 16] = st16(f(inp["s5_lam_im"])[l])
        pk[l, :, PK_LDT:PK_LDT + 16] = st16(np.repeat(f(inp["s5_log_dt"])[l][:, None], 64, axis=1))
        pk[l, :, PK_D5:PK_D5 + 4] = f(inp["s5_d"])[l].reshape(4, 128).T
        pk[l, :, PK_BGLU:PK_BGLU + 4] = f(inp["s5_b_glu"])[l].reshape(4, 128).T
        pk[l, :, PK_GB:PK_GB + 24] = f(inp["gate_b"])[l].reshape(24, 128).T
        rowp[l, 0:96] = f(inp["mla_q_head_g"])[l]
        rowp[l, 96:192] = f(inp["mla_k_head_g"])[l]
        rowp[l, 192:320] = f(inp["gla_out_g"])[l]
        wgp[l, 0:16] = f(inp["gla_w_gate"])[l]
        wgp[l, 16] = f(inp["gla_b_gate"])[l]
        for ri, (bk, ck) in enumerate((("s5_b_re", "s5_c_re"), ("s5_b_im", "s5_c_im"))):
            B = f(inp[bk])[l]
            C = f(inp[ck])[l]
            for j in range(16):
                c, q = j // 4, j % 4
                for gl in range(2):
                    g = 2 * j + gl
                    bt[l, ri, 32 * q + 16 * gl:32 * q + 16 * gl + 16, j, 64 * gl:64 * gl + 64] = B[g].T
                    ct[l, ri, 64 * gl:64 * gl + 64, j, 32 * q + 16 * gl:32 * q + 16 * gl + 16] = C[g].T
    NT = L // 128
    consts = np.zeros((128, C_POS + NT), np.float32)
    consts[:, C_ID:C_ID + 128] = np.eye(128, dtype=np.float32)
    consts[:, C_TRIU:C_TRIU + 128] = np.triu(np.ones((128, 128), np.float32))
    consts[:, C_TRIL:C_TRIL + 128] = np.tril(np.ones((128, 128), np.float32), -1)
    consts[:, C_TAU:C_TAU + 128] = np.arange(1, 129, dtype=np.float32)[None, :]
    consts[:, C_INVF:C_INVF + 16] = (10000.0 ** (-np.arange(0, 32, 2, dtype=np.float32) / 32.0)).astype(np.float32)[None, :]
    consts[:, C_POS:C_POS + NT] = (np.arange(NT, dtype=np.float32)[None, :] * 128.0 + np.arange(128, dtype=np.float32)[:, None])
    return dict(pk=pk, rowp=rowp, wg=wgp, bt=bt, ct=ct, consts=consts)


def make_in_maps(inp, L, n_cores):
    f = lambda a: np.ascontiguousarray(np.asarray(a, dtype=np.float32))
    shared = dict(
        w_in=f(inp["w_in"]), w_uq=f(inp["mla_w_uq"]), w_ukv=f(inp["mla_w_ukv"]), w_glu=f(inp["s5_w_glu"]),
        w_br0=f(inp["w_br_mla"]), w_br1=f(inp["w_br_gla"]), w_br2=f(inp["w_br_s5"]), w_out=f(inp["w_out"]),
        w_ff1=f(inp["w_ff1"]), w_ff2=f(inp["w_ff2"]))
    shared.update(_host_params(inp, L))
    x = f(inp["x"])
    nb = x.shape[0]
    maps = []
    for c in range(n_cores):
        m = dict(shared)
        m["x"] = np.ascontiguousarray(x[c % nb, :L])
        maps.append(m)
    return maps


def kernel(**inputs):
    L = SEQ
    nc = build_nc(L)
    maps = make_in_maps(inputs, L, BATCH)
    res = run_bass_kernel_spmd(nc, maps, core_ids=list(range(BATCH)))
    out = np.stack([np.asarray(res.results[b]["out"], dtype=np.float32) for b in range(BATCH)], axis=0)
    return out


def build_nc(L, dbg=False, nlayers=DEPTH, stop=None):
    return build(L, dbg, nlayers, stop)
```
